# Optimizing a Trainium2 kernel written in Bass

```python
import jax, jax.numpy as jnp
from jax import lax
import numpy as np

D_MODEL = 1024
BATCH = 4
SEQ = 8192
DEPTH = 2

GRID_W = 64
CTX_LEN = 256
GLA_HEADS = 4
GLA_DK = 48
GLA_DV = 96
GLA_RANK = 16
GLA_TAU = 16.0
GLA_CHUNK = 64
SWA_HEADS = 6
SWA_KV_HEADS = 2
HEAD_DIM = 64
WINDOW = 128
Q_BLOCK = 128
ROPE_BASE = 10000.0
POOL_WINDOWS = (2, 4, 8, 16)
POOL_GROUP = 64
D_FF = 2816
CONV_W = 3
EPS = 1e-6
NEG_INF = -1e30

GLA_QK = GLA_HEADS * GLA_DK
GLA_V = GLA_HEADS * GLA_DV
SWA_Q = SWA_HEADS * HEAD_DIM
SWA_KV = SWA_KV_HEADS * HEAD_DIM
POOL_W = len(POOL_WINDOWS) * POOL_GROUP
MIX_W = GLA_V + SWA_Q + POOL_W
IN_SPLITS = (GLA_QK, GLA_QK, GLA_V, GLA_V, GLA_RANK, GLA_RANK, SWA_Q, SWA_KV, SWA_KV, POOL_W)
IN_W = GLA_QK * 2 + GLA_V * 2 + GLA_RANK * 2 + SWA_Q + SWA_KV * 2 + POOL_W

kernel_name = 'hybrid_gla_swa_pool_prefix_dit'

F32 = jnp.float32


def rms_norm(x, g):
    xf = x.astype(F32)
    y = xf * lax.rsqrt(jnp.mean(xf * xf, axis=-1, keepdims=True) + EPS)
    return (y * g.astype(F32)).astype(x.dtype)


def heads(t, n):
    return t.reshape(t.shape[:-1] + (n, t.shape[-1] // n))


def flip(t):
    return jnp.flip(t, axis=1)


def in_proj_split(p):
    offs, acc = [], 0
    for w in IN_SPLITS[:-1]:
        acc += w
        offs.append(acc)
    return jnp.split(p, offs, axis=-1)


def rope_2d_tables(n_tokens):
    rows_n = n_tokens // GRID_W
    rows = jnp.repeat(jnp.arange(rows_n), GRID_W).astype(F32)
    cols = jnp.tile(jnp.arange(GRID_W), rows_n).astype(F32)
    nf = HEAD_DIM // 4
    inv = ROPE_BASE ** (-jnp.arange(nf, dtype=F32) / nf)
    ang = jnp.concatenate([rows[:, None] * inv, cols[:, None] * inv], axis=-1)
    return jnp.cos(ang), jnp.sin(ang)


def apply_rope_2d(x, cos, sin):
    xf = x.astype(F32)
    nf = HEAD_DIM // 4
    cos = cos[:, None, :]
    sin = sin[:, None, :]
    outs = []
    for a in range(2):
        xa = xf[..., a * 2 * nf:(a + 1) * 2 * nf]
        x1, x2 = xa[..., :nf], xa[..., nf:]
        ca, sa = cos[..., a * nf:(a + 1) * nf], sin[..., a * nf:(a + 1) * nf]
        outs += [x1 * ca - x2 * sa, x2 * ca + x1 * sa]
    return jnp.concatenate(outs, axis=-1).astype(x.dtype)


def gla_log_decay(z, w_dec, b_dec):
    la = jax.nn.log_sigmoid(z.astype(F32) @ w_dec.astype(F32) + b_dec.astype(F32)) / GLA_TAU
    return heads(la, GLA_HEADS)


def gla_chunked(q, k, v, log_a, s0):
    B, T, H, DK = q.shape
    DV = v.shape[-1]
    C = GLA_CHUNK
    N = T // C
    qc = q.astype(F32).reshape(B, N, C, H, DK)
    kc = k.astype(F32).reshape(B, N, C, H, DK)
    vc = v.astype(F32).reshape(B, N, C, H, DV)
    bc = jnp.cumsum(log_a.astype(F32).reshape(B, N, C, H, DK), axis=2)
    b_last = bc[:, :, -1:]
    q_dec = qc * jnp.exp(bc)
    k_inv = kc * jnp.exp(-bc)
    k_end = kc * jnp.exp(b_last - bc)
    scores = jnp.einsum('bnihd,bnjhd->bnhij', q_dec, k_inv)
    lower = jnp.tril(jnp.ones((C, C), dtype=bool))
    scores = jnp.where(lower, scores, 0.0)
    o_intra = jnp.einsum('bnhij,bnjhv->bnihv', scores, vc)
    chunk_state = jnp.einsum('bnjhd,bnjhv->nbhdv', k_end, vc)
    chunk_decay = jnp.transpose(jnp.exp(b_last[:, :, 0]), (1, 0, 2, 3))

    def step(s, inp):
        dec, upd = inp
        return dec[..., None] * s + upd, s

    s_final, s_init = lax.scan(step, s0.astype(F32), (chunk_decay, chunk_state))
    o_inter = jnp.einsum('bnihd,nbhdv->bnihv', q_dec, s_init)
    return (o_intra + o_inter).reshape(B, T, H, DV), s_final


def gla_final_state(k, v, log_a):
    b = jnp.cumsum(log_a.astype(F32), axis=1)
    w = jnp.exp(b[:, -1:] - b)
    return jnp.einsum('bthd,bthv->bhdv', k.astype(F32) * w, v.astype(F32))


def gla_output(o, g, norm_g):
    o = rms_norm(o, norm_g)
    B, T = o.shape[:2]
    return (o.reshape(B, T, -1) * jax.nn.silu(g.astype(F32))).astype(g.dtype)


def window_attention(q, k, v, k_ctx, v_ctx, sink):
    B, S, HQ, D = q.shape
    KV = k.shape[2]
    G = HQ // KV
    N = S // Q_BLOCK
    qb = q.astype(F32).reshape(B, N, Q_BLOCK, KV, G, D)
    pad = ((0, 0), (Q_BLOCK, Q_BLOCK), (0, 0), (0, 0))
    kp = jnp.pad(k.astype(F32), pad).reshape(B, N + 2, Q_BLOCK, KV, D)
    vp = jnp.pad(v.astype(F32), pad).reshape(B, N + 2, Q_BLOCK, KV, D)
    kb = jnp.concatenate([kp[:, :-2], kp[:, 1:-1], kp[:, 2:]], axis=2)
    vb = jnp.concatenate([vp[:, :-2], vp[:, 1:-1], vp[:, 2:]], axis=2)
    q_pos = jnp.arange(S).reshape(N, Q_BLOCK)
    k_pos = (jnp.arange(N)[:, None] - 1) * Q_BLOCK + jnp.arange(3 * Q_BLOCK)[None, :]
    dist = q_pos[:, :, None] - k_pos[:, None, :]
    valid = (jnp.abs(dist) <= WINDOW) & (k_pos[:, None, :] >= 0) & (k_pos[:, None, :] < S)
    scale = D ** -0.5
    s_loc = jnp.einsum('bnqkgd,bnjkd->bnkgqj', qb, kb) * scale
    s_loc = jnp.where(valid[None, :, None, None], s_loc, NEG_INF)
    s_ctx = jnp.einsum('bnqkgd,bckd->bnkgqc', qb, k_ctx.astype(F32)) * scale
    s_sink = jnp.broadcast_to(sink.astype(F32).reshape(KV, G)[None, None, :, :, None, None],
                              (B, N, KV, G, Q_BLOCK, 1))
    p = jax.nn.softmax(jnp.concatenate([s_loc, s_ctx, s_sink], axis=-1), axis=-1)
    n_loc = 3 * Q_BLOCK
    n_ctx = k_ctx.shape[1]
    o = (jnp.einsum('bnkgqj,bnjkd->bnqkgd', p[..., :n_loc], vb)
         + jnp.einsum('bnkgqc,bckd->bnqkgd', p[..., n_loc:n_loc + n_ctx], v_ctx.astype(F32)))
    return o.reshape(B, S, HQ * D).astype(q.dtype)


def context_attention(q, k, v, sink):
    B, L, HQ, D = q.shape
    KV = k.shape[2]
    G = HQ // KV
    qg = q.astype(F32).reshape(B, L, KV, G, D)
    s = jnp.einsum('blkgd,bckd->bkglc', qg, k.astype(F32)) * D ** -0.5
    s_sink = jnp.broadcast_to(sink.astype(F32).reshape(KV, G)[None, :, :, None, None], (B, KV, G, L, 1))
    p = jax.nn.softmax(jnp.concatenate([s, s_sink], axis=-1), axis=-1)
    o = jnp.einsum('bkglc,bckd->blkgd', p[..., :L], v.astype(F32))
    return o.reshape(B, L, HQ * D).astype(q.dtype)


def multiscale_pool(u, pool_w, pool_scale):
    T = u.shape[1]
    uf = u.astype(F32)
    prefix = jnp.pad(jnp.cumsum(uf, axis=1), ((0, 0), (1, 0), (0, 0)))
    t = jnp.arange(T)
    outs = []
    for g, w in enumerate(POOL_WINDOWS):
        lo = jnp.clip(t - w // 2, 0, T)
        hi = jnp.clip(t + w // 2, 0, T)
        sl = slice(g * POOL_GROUP, (g + 1) * POOL_GROUP)
        pg = prefix[:, :, sl]
        mean = (pg[:, hi] - pg[:, lo]) / (hi - lo).astype(F32)[None, :, None]
        outs.append((mean - uf[:, :, sl]) @ pool_w[g].astype(F32))
    y = jnp.concatenate(outs, axis=-1) * pool_scale.astype(F32)
    return y.astype(u.dtype)


def conv_ffn(h, w_up, conv_w, conv_b, w_down):
    u = h @ w_up
    up = jnp.pad(u, ((0, 0), (1, 1), (0, 0)))
    u = up[:, :-2] * conv_w[0] + up[:, 1:-1] * conv_w[1] + up[:, 2:] * conv_w[2] + conv_b
    a, g = jnp.split(u, 2, axis=-1)
    return (jax.nn.silu(g) * a) @ w_down


def hybrid_layer(x, ctx, mod, modc, norm1_g, w_in, gla_w_dec, gla_b_dec, gla_norm_g, q_norm_g,
                 k_norm_g, sink_logit, pool_w, pool_scale, w_out, norm2_g, w_up, conv_w, conv_b,
                 w_down, cos, sin, update_ctx):
    sh1, sc1, g1, sh2, sc2, g2 = jnp.split(mod, 6, axis=-1)
    csh1, csc1, cg1, csh2, csc2, cg2 = jnp.split(modc, 6, axis=-1)
    h = rms_norm(x, norm1_g) * (1 + sc1) + sh1
    hc = rms_norm(ctx, norm1_g) * (1 + csc1) + csh1
    gq, gk, gv, gg, zf, zb, aq, ak, av, pu = in_proj_split(h @ w_in)
    cgq, cgk, cgv, cgg, czf, czb, caq, cak, cav, cpu = in_proj_split(hc @ w_in)

    la_f = gla_log_decay(zf, gla_w_dec[0], gla_b_dec[0])
    la_b = gla_log_decay(zb, gla_w_dec[1], gla_b_dec[1])
    cla_f = gla_log_decay(czf, gla_w_dec[0], gla_b_dec[0])
    cla_b = gla_log_decay(czb, gla_w_dec[1], gla_b_dec[1])
    q = heads(gq, GLA_HEADS) * GLA_DK ** -0.5
    k = heads(gk, GLA_HEADS)
    v = heads(gv, GLA_HEADS)
    ck = heads(cgk, GLA_HEADS)
    cv = heads(cgv, GLA_HEADS)
    if update_ctx:
        cq = heads(cgq, GLA_HEADS) * GLA_DK ** -0.5
        s0 = jnp.zeros((ctx.shape[0], GLA_HEADS, GLA_DK, GLA_DV), F32)
        oc_f, st_f = gla_chunked(cq, ck, cv, cla_f, s0)
        oc_b, st_b = gla_chunked(flip(cq), flip(ck), flip(cv), flip(cla_b), s0)
        gla_ctx = gla_output(oc_f + flip(oc_b), cgg, gla_norm_g)
    else:
        st_f = gla_final_state(ck, cv, cla_f)
        st_b = gla_final_state(flip(ck), flip(cv), flip(cla_b))
    o_f, _ = gla_chunked(q, k, v, la_f, st_f)
    o_b, _ = gla_chunked(flip(q), flip(k), flip(v), flip(la_b), st_b)
    gla_lat = gla_output(o_f + flip(o_b), gg, gla_norm_g)

    aqh = apply_rope_2d(rms_norm(heads(aq, SWA_HEADS), q_norm_g), cos, sin)
    akh = apply_rope_2d(rms_norm(heads(ak, SWA_KV_HEADS), k_norm_g), cos, sin)
    avh = heads(av, SWA_KV_HEADS)
    cakh = rms_norm(heads(cak, SWA_KV_HEADS), k_norm_g)
    cavh = heads(cav, SWA_KV_HEADS)
    swa_lat = window_attention(aqh, akh, avh, cakh, cavh, sink_logit)

    pool_lat = multiscale_pool(pu, pool_w, pool_scale)

    y = jnp.concatenate([gla_lat, swa_lat, pool_lat], axis=-1) @ w_out
    x = x + g1 * y
    h2 = rms_norm(x, norm2_g) * (1 + sc2) + sh2
    x = x + g2 * conv_ffn(h2, w_up, conv_w, conv_b, w_down)

    if update_ctx:
        swa_ctx = context_attention(rms_norm(heads(caq, SWA_HEADS), q_norm_g), cakh, cavh, sink_logit)
        pool_ctx = multiscale_pool(cpu, pool_w, pool_scale)
        yc = jnp.concatenate([gla_ctx, swa_ctx, pool_ctx], axis=-1) @ w_out
        ctx = ctx + cg1 * yc
        hc2 = rms_norm(ctx, norm2_g) * (1 + csc2) + csh2
        ctx = ctx + cg2 * conv_ffn(hc2, w_up, conv_w, conv_b, w_down)
    return x, ctx


def setup_inputs(seed: int = 0) -> dict:
    key = jax.random.key(seed)
    ks = jax.random.split(key, 24)
    D = D_MODEL
    nrm = lambda k, shape: jax.random.normal(k, shape, F32)
    return {
        'x': nrm(ks[0], (BATCH, SEQ, D)),
        'c': nrm(ks[1], (BATCH, D)),
        'ctx': nrm(ks[2], (BATCH, CTX_LEN, D)),
        'c_ctx': nrm(ks[3], (D,)),
        'w_ada': nrm(ks[4], (DEPTH, D, 6 * D)) * (0.5 * D ** -0.5),
        'b_ada': nrm(ks[5], (DEPTH, 6 * D)) * 0.02,
        'norm1_g': 1.0 + 0.02 * nrm(ks[6], (DEPTH, D)),
        'w_in': nrm(ks[7], (DEPTH, D, IN_W)) * D ** -0.5,
        'gla_w_dec': nrm(ks[8], (DEPTH, 2, GLA_RANK, GLA_QK)) * GLA_RANK ** -0.5,
        'gla_b_dec': nrm(ks[9], (DEPTH, 2, GLA_QK)) * 0.1,
        'gla_norm_g': 1.0 + 0.02 * nrm(ks[10], (DEPTH, GLA_DV)),
        'q_norm_g': 1.0 + 0.02 * nrm(ks[11], (DEPTH, HEAD_DIM)),
        'k_norm_g': 1.0 + 0.02 * nrm(ks[12], (DEPTH, HEAD_DIM)),
        'sink_logit': nrm(ks[13], (DEPTH, SWA_HEADS)),
        'pool_w': nrm(ks[14], (DEPTH, len(POOL_WINDOWS), POOL_GROUP, POOL_GROUP)) * POOL_GROUP ** -0.5,
        'pool_scale': 1.0 + 0.1 * nrm(ks[15], (DEPTH, POOL_W)),
        'w_out': nrm(ks[16], (DEPTH, MIX_W, D)) * MIX_W ** -0.5,
        'norm2_g': 1.0 + 0.02 * nrm(ks[17], (DEPTH, D)),
        'w_up': nrm(ks[18], (DEPTH, D, 2 * D_FF)) * D ** -0.5,
        'conv_w': nrm(ks[19], (DEPTH, CONV_W, 2 * D_FF)) * CONV_W ** -0.5,
        'conv_b': nrm(ks[20], (DEPTH, 2 * D_FF)) * 0.02,
        'w_down': nrm(ks[21], (DEPTH, D_FF, D)) * D_FF ** -0.5,
    }


def reference(x, c, ctx, c_ctx, w_ada, b_ada, norm1_g, w_in, gla_w_dec, gla_b_dec, gla_norm_g,
              q_norm_g, k_norm_g, sink_logit, pool_w, pool_scale, w_out, norm2_g, w_up, conv_w,
              conv_b, w_down):
    cos, sin = rope_2d_tables(x.shape[1])
    c_act = jax.nn.silu(c)
    cc_act = jax.nn.silu(c_ctx)
    for l in range(DEPTH):
        mod = (c_act @ w_ada[l] + b_ada[l])[:, None, :]
        modc = cc_act @ w_ada[l] + b_ada[l]
        x, ctx = hybrid_layer(x, ctx, mod, modc, norm1_g[l], w_in[l], gla_w_dec[l], gla_b_dec[l],
                              gla_norm_g[l], q_norm_g[l], k_norm_g[l], sink_logit[l], pool_w[l],
                              pool_scale[l], w_out[l], norm2_g[l], w_up[l], conv_w[l], conv_b[l],
                              w_down[l], cos, sin, l < DEPTH - 1)
    return x
```

```python
import numpy as np
from contextlib import ExitStack
import concourse.bass as bass
import concourse.mybir as mybir
from concourse.bass_utils import run_bass_kernel_spmd

F32 = mybir.dt.float32
BF16 = mybir.dt.bfloat16
AF = mybir.ActivationFunctionType
ALU = mybir.AluOpType
AX = mybir.AxisListType

D = 1024
DEPTH = 2
GRID_W = 64
NH_G, DK, DV = 4, 48, 96
D_FF = 2816
EPS = 1e-6
IN_W = 2080
C_Q, C_K, C_Z, C_PU, C_GV, C_GG, C_AQ, C_AK = 0, 256, 512, 576, 832, 1216, 1600, 1984
NW = 2240
FT = 254

SAME_ENGINE_SYNC = True
import os
STOP_AT = float(os.environ.get('MK_STOP', '99'))


class Tok:
    __slots__ = ("name", "w", "r", "excl")

    def __init__(self, name):
        self.name = name
        self.w = []
        self.r = {}
        self.excl = False


class Op:
    __slots__ = ("eng", "fn", "deps", "odeps", "sig", "dma", "key", "ev", "tag")

    def __init__(self, eng, fn, dma, key):
        self.eng, self.fn, self.dma, self.key = eng, fn, dma, key
        self.deps = []
        self.odeps = []
        self.sig = False
        self.ev = None


class Sched:
    ENGS = ("pe", "act", "dve", "pool", "sp")

    def __init__(self, nc):
        self.nc = nc
        self.ops = []
        self.cur = None
        self.log = []
        self.ntok = 0

    def tok(self, name=None):
        self.ntok += 1
        return Tok(name or f"t{self.ntok}")

    def op(self, eng, fn, reads=(), writes=(), dma=False, key=None):
        o = Op(eng, fn, dma, key)
        if os.environ.get('MK_DEBUG'):
            import inspect
            o.tag = inspect.stack()[2].lineno
        deps = {}
        for t in reads:
            for w in t.w:
                deps[id(w)] = w
            if t.excl:
                for re_, rd in t.r.items():
                    if re_ != eng and not isinstance(rd, list):
                        deps[id(rd)] = rd
        for t in writes:
            samekey = dma and t.w and all(w.dma and w.key == key for w in t.w) and not t.r
            if not samekey:
                for w in t.w:
                    deps[id(w)] = w
                for rd in t.r.values():
                    if isinstance(rd, list):
                        for x in rd:
                            deps[id(x)] = x
                    else:
                        deps[id(rd)] = rd
        for d in deps.values():
            if (not d.dma) and (not dma) and d.eng == eng:
                if eng == "pe" or not SAME_ENGINE_SYNC:
                    o.odeps.append(d)
                    continue
            o.deps.append(d)
            d.sig = True
        for t in reads:
            if dma:
                t.r.setdefault("dma", []).append(o)
            else:
                t.r[eng] = o
        for t in writes:
            samekey = dma and t.w and all(w.dma and w.key == key for w in t.w) and not t.r
            if samekey:
                t.w.append(o)
            else:
                t.w = [o]
                t.r = {}
        if dma:
            o.sig = True
        (self.cur if self.cur is not None else self.ops).append(o)
        return o

    def merge_threads(self, lists, ratio=1, ret=False):
        unem = set()
        for L in lists:
            for o in L:
                unem.add(id(o))
        out = []
        k = len(lists)
        pos = [0] * k
        i = 0
        while i < k:
            L = lists[i]
            if pos[i] >= len(L):
                i += 1
                continue
            o = L[pos[i]]
            pos[i] += 1
            out.append(o)
            unem.discard(id(o))
            j = i + 1
            if j < k:
                Lj = lists[j]
                for _ in range(ratio):
                    if pos[j] < len(Lj):
                        o2 = Lj[pos[j]]
                        if all(id(d) not in unem for d in o2.deps) and all(id(d) not in unem for d in o2.odeps):
                            pos[j] += 1
                            out.append(o2)
                            unem.discard(id(o2))
        if ret:
            return out
        self.ops.extend(out)

    def barrier(self):
        last, dmas = {}, {}
        for o in self.ops:
            if o.fn is None:
                continue
            if o.dma:
                dmas[o.key] = o
            else:
                last[o.eng] = o
        for e in self.ENGS:
            b = Op(e, None, False, None)
            for d in list(last.values()) + list(dmas.values()):
                if (not d.dma) and d.eng == e:
                    continue
                b.deps.append(d)
                d.sig = True
            self.ops.append(b)

    def emit(self, stack):
        nc = self.nc
        cnt = {e: 0 for e in self.ENGS}
        keycnt = {}
        for o in self.ops:
            if o.fn is None:
                continue
            if o.dma:
                keycnt[o.key] = keycnt.get(o.key, 0) + 16
                o.ev = (("dma", o.key), keycnt[o.key])
            elif o.sig:
                cnt[o.eng] += 1
                o.ev = (("eng", o.eng), cnt[o.eng])
        sems = {}
        for e in self.ENGS:
            if cnt[e]:
                sems[("eng", e)] = stack.enter_context(nc.semaphore(f"c_{e}"))
        for k in keycnt:
            sems[("dma", k)] = stack.enter_context(nc.semaphore(f"d_{k}"))
        per = {e: [] for e in self.ENGS}
        for o in self.ops:
            per[o.eng].append(o)
        self.stats = {e: len(per[e]) for e in self.ENGS}
        self.stats["sems"] = len(sems)
        self.stats["maxcnt"] = dict(cnt)
        block = stack.enter_context(nc.Block())

        def run(engobj, lst):
            waited = {}
            for o in lst:
                need = {}
                for d in o.deps:
                    if d.ev is None:
                        continue
                    s, v = d.ev
                    if waited.get(s, 0) >= v:
                        continue
                    if need.get(s, 0) < v:
                        need[s] = v
                for s, v in need.items():
                    engobj.wait_ge(sems[s], v)
                    waited[s] = v
                    if os.environ.get('MK_DEBUG'):
                        self.log.append(f"{o.eng} WAIT {s} >= {v}")
                if os.environ.get('MK_DEBUG') and o.fn is not None:
                    self.log.append(f"{o.eng} OP line{getattr(o, 'tag', 0)} ev={o.ev}")
                if o.fn is None:
                    continue
                ins = o.fn(engobj)
                if o.ev is not None:
                    ins.then_inc(sems[o.ev[0]], 16 if o.dma else 1)

        @block.tensor
        def _(e):
            run(e, per["pe"])

        @block.scalar
        def _(e):
            run(e, per["act"])

        @block.vector
        def _(e):
            run(e, per["dve"])

        @block.gpsimd
        def _(e):
            run(e, per["pool"])

        @block.sync
        def _(e):
            run(e, per["sp"])


class Arena:
    def __init__(self, ap):
        self.ap = ap
        self.size = ap.shape[1]
        self.off = 0

    def reset(self, to=0):
        self.off = to

    def alloc(self, shape, dtype=F32):
        n = int(np.prod(shape))
        bpe = 4 if dtype == F32 else 2
        words = (n * bpe + 3) // 4
        words = (words + 7) // 8 * 8
        assert self.off + words <= self.size, f"arena overflow {self.off}+{words}>{self.size}"
        v = self.ap[:, self.off:self.off + words]
        self.off += words
        if dtype != F32:
            v = v.bitcast(dtype)
        v = v[:, 0:n]
        if len(shape) == 1:
            return v
        names = "abcd"[:len(shape)]
        pat = "p (" + " ".join(names) + ") -> p " + " ".join(names)
        return v.rearrange(pat, **{names[i]: shape[i] for i in range(len(shape))})


class B:
    __slots__ = ("ap", "t")

    def __init__(self, ap, t):
        self.ap, self.t = ap, t


class Ring:
    def __init__(self, bufs):
        self.bufs = bufs
        self.i = -1

    def next(self):
        self.i = (self.i + 1) % len(self.bufs)
        return self.bufs[self.i]


def build(SL, LC):
    nc = bass.Bass("TRN2", target_bir_lowering=False)

    def dten(name, shape, kind="ExternalInput"):
        return nc.dram_tensor(name, list(shape), F32, kind=kind).ap()

    x_d = dten("x", [SL, D]); ctx_d = dten("ctx", [LC, D]); cT_d = dten("cT", [128, 8, 2])
    ident_d = dten("ident", [128, 128]); tri_d = dten("tri", [128, 2, 128]); mask_d = dten("mask", [128, 2, 128])
    rope_d = dten("rope", [SL, 128]); pinv_d = dten("pinv", [128, 2]); pedge_d = dten("pedge", [128, 2, 2, 8])
    wada_d = dten("w_ada", [DEPTH, D, 6 * D]); badaT_d = dten("b_adaT", [DEPTH, 128, 48]); badar_d = dten("b_adar", [DEPTH, 1, 6 * D])
    n1T_d = dten("n1T", [DEPTH, 128, 8]); n2T_d = dten("n2T", [DEPTH, 128, 8])
    win_d = dten("w_in", [DEPTH, D, IN_W]); wdec_d = dten("wdec", [DEPTH, 48, 2, 256]); bdec_d = dten("bdec", [DEPTH, 1, 2, 256])
    glag_d = dten("glag", [DEPTH, 128, 384]); qg_d = dten("qg", [DEPTH, 128, 384]); kg_d = dten("kg", [DEPTH, 128, 128])
    sink_d = dten("sink", [DEPTH, 128, 6]); pw_d = dten("pw", [DEPTH, 128, 2, 128]); pscT_d = dten("pscT", [DEPTH, 128, 2])
    wout_d = dten("w_out", [DEPTH, D, D]); wup_d = dten("w_up", [DEPTH, D, 2 * D_FF]); cw_d = dten("cw", [DEPTH, 128, 44, 3])
    cb_d = dten("cb", [DEPTH, 128, 44]); wdn_d = dten("w_down", [DEPTH, D_FF, D])
    out_d = dten("out", [SL, D], kind="ExternalOutput")
    ob_d = {"lat": dten("ob_lat", [SL, 384], "Internal"), "ctx": dten("ob_ctx", [LC, 384], "Internal")}
    xmid_d = {"lat": dten("xmid_lat", [SL, D], "Internal"), "ctx": dten("xmid_ctx", [LC, D], "Internal")}
    x1_d = {"lat": dten("x1_lat", [SL, D], "Internal"), "ctx": dten("x1_ctx", [LC, D], "Internal")}
    pu_d = {"lat": dten("pu_lat", [128, 2, SL + 16], "Internal"), "ctx": dten("pu_ctx", [128, 2, LC + 16], "Internal")}

    st = ExitStack()
    S = Sched(nc)
    PERS_W, ARENA_W = 3904, 49280
    pers_t = st.enter_context(nc.sbuf_tensor("pers", [128, PERS_W], F32))
    arena_t = st.enter_context(nc.sbuf_tensor("arena", [128, ARENA_W], F32))
    PERS = Arena(pers_t[:, :]); AR = Arena(arena_t[:, :])
    psb = [st.enter_context(nc.psum_tensor(f"ps{i}", [128, 512], F32)) for i in range(8)]
    class PRing:
        def __init__(self, bufs):
            self.bufs = bufs
            self.par = 0
            self.i = [-1, -1]

        def set(self, par):
            self.par = par % 2

        def next(self):
            p = self.par
            self.i[p] = (self.i[p] + 1) % 4
            return self.bufs[4 * p + self.i[p]]

    PSR = PRing([B(psb[i][:, :], S.tok(f"ps{i}")) for i in range(8)])
    for b_ in PSR.bufs:
        b_.t.excl = True

    def newbuf(arena, shape, dtype=F32, name=None):
        return B(arena.alloc(shape, dtype), S.tok(name))

    def ring(arena, n, shape, dtype=F32, name="r"):
        return Ring([newbuf(arena, shape, dtype, f"{name}{i}") for i in range(n)])

    def toks(bs):
        return [b.t if isinstance(b, B) else b for b in bs]

    def mm(out, lhsT, rhs, R, W, start=True, stop=True):
        S.op("pe", lambda e: e.matmul(out, lhsT=lhsT, rhs=rhs, start=start, stop=stop), toks(R), toks(W))

    def tp(out, in_, R, W):
        S.op("pe", lambda e: e.transpose(out, in_, ident.ap), toks(R) + [ident.t], toks(W))

    def act(out, in_, func, R, W, **kw):
        S.op("act", lambda e: e.activation(out=out, in_=in_, func=func, **kw), toks(R), toks(W))

    def tt(eng, out, in0, in1, op, R, W):
        S.op(eng, lambda e: e.tensor_tensor(out=out, in0=in0, in1=in1, op=op), toks(R), toks(W))

    def ts(eng, out, in0, s1, s2, op0, op1, R, W):
        if s2 is None:
            S.op(eng, lambda e: e.tensor_scalar(out=out, in0=in0, scalar1=s1, scalar2=None, op0=op0), toks(R), toks(W))
        else:
            S.op(eng, lambda e: e.tensor_scalar(out=out, in0=in0, scalar1=s1, scalar2=s2, op0=op0, op1=op1), toks(R), toks(W))

    def stt(eng, out, in0, scalar, in1, op0, op1, R, W):
        S.op(eng, lambda e: e.scalar_tensor_tensor(out=out, in0=in0, scalar=scalar, in1=in1, op0=op0, op1=op1), toks(R), toks(W))

    def cp(eng, out, in_, R, W):
        if eng == "act":
            act(out, in_, AF.Copy, R, W)
        else:
            S.op(eng, lambda e: e.tensor_copy(out=out, in_=in_), toks(R), toks(W))

    def memset(eng, ap, val, W):
        S.op(eng, lambda e: e.memset(ap, val), [], toks(W))

    def red(eng, out, in_, R, W):
        S.op(eng, lambda e: e.tensor_reduce(out=out, in_=in_, axis=AX.X, op=ALU.add), toks(R), toks(W))

    def recip(out, in_, R, W):
        S.op("dve", lambda e: e.reciprocal(out=out, in_=in_), toks(R), toks(W))

    def dma(eng, out, in_, R, W, key):
        S.op(eng, lambda e: e.dma_start(out=out, in_=in_), toks(R), toks(W), dma=True, key=key)

    def rstd_from_ss(ss, n, R_buf):
        ts("dve", ss, ss, 1.0 / n, EPS, ALU.mult, ALU.add, [R_buf], [R_buf])
        act(ss, ss, AF.Ln, [R_buf], [R_buf])
        act(ss, ss, AF.Exp, [R_buf], [R_buf], scale=-0.5)

    ident = newbuf(PERS, [128], F32, "ident"); ident.ap = ident.ap
    tri = newbuf(PERS, [2, 128]); maskb = newbuf(PERS, [2, 128]); pinv = newbuf(PERS, [2]); pedge = newbuf(PERS, [2, 2, 8])
    cact = newbuf(PERS, [8, 2]); onesr = newbuf(PERS, [128]); zeros = newbuf(PERS, [144])
    MODT = newbuf(PERS, [48, 2]); badaT = newbuf(PERS, [48]); n1T = newbuf(PERS, [8]); n2T = newbuf(PERS, [8])
    G1T = newbuf(PERS, [8, 2]); G2T = newbuf(PERS, [8, 2])
    glag = newbuf(PERS, [384]); qg = newbuf(PERS, [384])
    kg = newbuf(PERS, [128]); esink = newbuf(PERS, [6]); pscT = newbuf(PERS, [2]); cw = newbuf(PERS, [44, 3]); cb = newbuf(PERS, [44])
    pw = newbuf(PERS, [2, 128], BF16)
    Sst = {d: newbuf(PERS, [2, 96]) for d in "fb"}
    Sbf = {d: newbuf(PERS, [2, 96], BF16) for d in "fb"}
    small = Ring([newbuf(PERS, [16], F32, f"small{i}") for i in range(8)])

    dma("sp", ident.ap, ident_d, [], [ident], "c_ident")
    dma("sp", tri.ap, tri_d, [], [tri], "c_tri")
    dma("sp", maskb.ap, mask_d, [], [maskb], "c_mask")
    dma("sp", pinv.ap, pinv_d, [], [pinv], "c_pinv")
    dma("sp", pedge.ap, pedge_d, [], [pedge], "c_pedge")
    dma("sp", cact.ap, cT_d, [], [cact], "c_cact")
    memset("dve", onesr.ap, 1.0, [onesr])
    memset("dve", zeros.ap, 0.0, [zeros])
    act(cact.ap, cact.ap, AF.Silu, [cact], [cact])
    for sname, n in (("lat", SL), ("ctx", LC)):
        dma("sp", pu_d[sname][:, :, 0:8], zeros.ap[:, 0:16].rearrange("p (a b) -> p a b", a=2), [zeros], [], "c_pz")
        dma("sp", pu_d[sname][:, :, 8 + n:16 + n], zeros.ap[:, 0:16].rearrange("p (a b) -> p a b", a=2), [zeros], [], "c_pz")

    seqs = {"lat": dict(name="lat", n=SL, nb=SL // 128, w=0, rope=True),
            "ctx": dict(name="ctx", n=LC, nb=LC // 128, w=1, rope=False)}

    for l in range(DEPTH):
        last = (l == DEPTH - 1)
        x_src = {"lat": x_d if l == 0 else x1_d["lat"], "ctx": ctx_d if l == 0 else x1_d["ctx"]}
        x_dst = {"lat": out_d if last else x1_d["lat"], "ctx": x1_d["ctx"]}
        S.barrier()
        AR.reset()
        G2B = newbuf(AR, [2, D], F32, "G2B")
        gbmark = AR.off
        G1B = newbuf(AR, [2, D], F32, "G1B")
        KT = {"lat": newbuf(AR, [SL], BF16, "KT"), "ctx": newbuf(AR, [LC], BF16, "KTc")}
        VA = {"lat": newbuf(AR, [SL // 128, 2, 65], BF16, "VA"), "ctx": newbuf(AR, [LC // 128, 2, 65], BF16, "VAc")}
        WIN = newbuf(AR, [8, NW], BF16, "WIN"); WOUT = newbuf(AR, [8, D], BF16, "WOUT")
        mark = AR.off
        for bufx, src in ((badaT, badaT_d[l]), (n1T, n1T_d[l]), (n2T, n2T_d[l]), (glag, glag_d[l]), (qg, qg_d[l]),
                          (kg, kg_d[l]), (esink, sink_d[l]), (pscT, pscT_d[l]), (cw, cw_d[l]), (cb, cb_d[l])):
            dma("sp", bufx.ap, src, [], [bufx], "c_small")
        dma("pool", pw.ap, pw_d[l], [], [pw], "c_pw")
        act(esink.ap, esink.ap, AF.Exp, [esink], [esink])
        if STOP_AT <= 0.1:
            break
        memset("pool", WIN.ap, 0.0, [WIN])
        wv = win_d[l].rearrange("(k p) n -> p k n", p=128)

        def stage(dst0, src0, n):
            dma("pool", WIN.ap[:, :, dst0:dst0 + n], wv[:, :, src0:src0 + n], [], [WIN], "w_in")
        for h in range(4):
            stage(C_Q + 64 * h, 48 * h, 48)
            stage(C_K + 64 * h, 192 + 48 * h, 48)
        stage(C_Z, 1152, 16); stage(C_Z + 32, 1168, 16)
        stage(C_PU, 1824, 256)
        stage(C_GV, 384, 384); stage(C_GG, 768, 384)
        for i, h in enumerate((0, 3, 1, 4, 2, 5)):
            stage(C_AQ + 64 * i, 1184 + 64 * h, 64)
        stage(C_AK, 1568, 256)
        dma("pool", WOUT.ap, wout_d[l].rearrange("(k p) n -> p k n", p=128), [], [WOUT], "w_out")
        if STOP_AT <= 0.2:
            break
        crep = newbuf(AR, [8, 2, 128], F32, "crep")
        brow = newbuf(AR, [6 * D], F32, "brow")
        dma("sp", brow.ap[0:1, :], badar_d[l], [], [brow], "c_brow")
        for k in range(8):
            for w in range(2):
                cp("dve", crep.ap[:, k, w, :], cact.ap[:, k, w:w + 1].to_broadcast([128, 128]), [cact], [crep])
        wst = ring(AR, 2, [8, 512], F32, "wst")
        for cbk in range(12):
            if cbk == 6 and os.environ.get('MK_NOBAR', '') == '':
                S.barrier()
            wb = wst.next()
            dma("sp", wb.ap, wada_d[l][:, cbk * 512:(cbk + 1) * 512].rearrange("(k p) n -> p k n", p=128), [], [wb], f"wst{wst.i}")
            which, half = cbk // 2, cbk % 2
            if os.environ.get('MK_SKIP', '') == 'C' and cbk >= 6:
                continue
            if os.environ.get('MK_SKIP', '') == 'D' and cbk < 6:
                continue
            if os.environ.get('MK_SKIP', '') == 'A' and which in (2, 5):
                continue
            if os.environ.get('MK_SKIP', '') == 'B' and which not in (2, 5):
                continue
            if which in (2, 5):
                GB = G1B if which == 2 else G2B
                for w in range(2):
                    pb = PSR.next()
                    for k in range(8):
                        mm(pb.ap, crep.ap[:, k, w, :], wb.ap[:, k, :], [crep, wb], [pb], start=(k == 0), stop=False)
                    mm(pb.ap, onesr.ap[0:1, :], brow.ap[0:1, cbk * 512:(cbk + 1) * 512], [onesr, brow], [pb], start=False, stop=True)
                    cp("act", GB.ap[:, w, half * 512:(half + 1) * 512], pb.ap, [pb], [GB])
            else:
                pb = PSR.next()
                for j in range(4):
                    for k in range(8):
                        mm(pb.ap[:, j * 2:j * 2 + 2], wb.ap[:, k, j * 128:(j + 1) * 128], cact.ap[:, k, :], [wb, cact], [pb],
                           start=(k == 0), stop=(k == 7))
                j0 = cbk * 4
                tt("dve", MODT.ap[:, j0:j0 + 4, :], pb.ap[:, 0:8].rearrange("p (j w) -> p j w", j=4),
                   badaT.ap[:, j0:j0 + 4, None].to_broadcast([128, 4, 2]), ALU.add, [pb, badaT], [MODT])
        ts("dve", G1T.ap, MODT.ap[:, 8:16, :], 1.0, None, ALU.add, None, [MODT], [G1T])
        tt("dve", G1T.ap, G1T.ap, n1T.ap[:, :, None].to_broadcast([128, 8, 2]), ALU.mult, [G1T, n1T], [G1T])
        ts("dve", G2T.ap, MODT.ap[:, 32:40, :], 1.0, None, ALU.add, None, [MODT], [G2T])
        tt("dve", G2T.ap, G2T.ap, n2T.ap[:, :, None].to_broadcast([128, 8, 2]), ALU.mult, [G2T, n2T], [G2T])
        SH1T = MODT.ap[:, 0:8, :]
        SH2T = MODT.ap[:, 24:32, :]

        S.barrier()
        AR.reset(mark)
        xs_r = ring(AR, 2, [D], F32, "xs"); xn = newbuf(AR, [D]); junk = newbuf(AR, [D], BF16); hT_r = ring(AR, 2, [8, 128], BF16, "hT")
        PT = newbuf(AR, [5, 6, 128], BF16); rp_r = ring(AR, 4, [128], F32, "rp")
        qn = newbuf(AR, [384]); qsq = newbuf(AR, [384]); qr = newbuf(AR, [384]); qT = newbuf(AR, [3, 128], BF16); oatt = newbuf(AR, [384])
        kn = B(qn.ap[:, 0:128], qn.t); ksq = B(qsq.ap[:, 0:128], qsq.t); kr = B(qr.ap[:, 0:128], qr.t); ob_sq = qsq
        puT = newbuf(AR, [2, 128]); puw = newbuf(AR, [2, 144]); s2 = newbuf(AR, [2, 144]); s4 = newbuf(AR, [2, 144]); s8 = newbuf(AR, [2, 144])
        s16 = newbuf(AR, [2, 144]); tmp8 = newbuf(AR, [8])
        ytmp = newbuf(AR, [D]); xm_r = ring(AR, 2, [D], F32, "xm")
        WSETS = []
        for _ws in range(2):
            w_ = {}
            w_["zT"] = newbuf(AR, [128]); w_["e1"] = newbuf(AR, [256]); w_["spb"] = newbuf(AR, [256]); w_["epos"] = newbuf(AR, [256]); w_["eneg"] = newbuf(AR, [256])
            w_["qdT"] = newbuf(AR, [256], BF16); w_["kiT"] = newbuf(AR, [256], BF16); w_["keT"] = newbuf(AR, [256]); w_["ke"] = newbuf(AR, [256], BF16)
            w_["scT"] = newbuf(AR, [4, 128], BF16); w_["vbf"] = newbuf(AR, [384], BF16); w_["obl"] = newbuf(AR, [384]); w_["ob_o"] = newbuf(AR, [384])
            w_["sgb"] = newbuf(AR, [384]); w_["dT"] = newbuf(AR, [2, 128], BF16); w_["mixT"] = newbuf(AR, [8, 128], BF16)
            w_["obst"] = w_["ob_o"]
            WSETS.append(w_)
        zT = e1 = spb = epos = eneg = qdT = kiT = keT = ke = scT = vbf = obl = ob_o = sgb = dT = mixT = obst = None

        def select(slot):
            nonlocal zT, e1, spb, epos, eneg, qdT, kiT, keT, ke, scT, vbf, obl, ob_o, sgb, dT, mixT, obst
            w_ = WSETS[slot % 2]
            zT, e1, spb, epos, eneg = w_["zT"], w_["e1"], w_["spb"], w_["epos"], w_["eneg"]
            qdT, kiT, keT, ke, scT, vbf = w_["qdT"], w_["kiT"], w_["keT"], w_["ke"], w_["scT"], w_["vbf"]
            obl, ob_o, sgb, dT, mixT, obst = w_["obl"], w_["ob_o"], w_["sgb"], w_["dT"], w_["mixT"], w_["obst"]
        select(0)
        print("ARENA mixer end", AR.off, "of", AR.size)
        wdec = newbuf(AR, [2, 256]); bdec = newbuf(AR, [2, 256])
        dma("sp", wdec.ap[0:48, :, :], wdec_d[l], [], [wdec], "c_wdec")
        dma("sp", bdec.ap[0:1, :, :], bdec_d[l], [], [bdec], "c_wdec")
        for sname in ("lat", "ctx"):
            memset("pool", VA[sname].ap[:, :, :, 64:65], 1.0, [VA[sname]])

        def front(src_ap, GT, SHT, w, dst_hT=None, col0=0, pre=None):
            xb = xs_r.next()
            if pre is not None:
                pre(xb)
            else:
                dma("sp", xb.ap, src_ap, [], [xb], f"xs{xs_r.i}")
            sm = small.next()
            memset("dve", sm.ap[:, 0:1], 0.0, [sm])
            act(junk.ap, xb.ap, AF.Square, [xb, sm], [junk, sm], accum_out=sm.ap[:, 0:1])
            rstd_from_ss(sm.ap[:, 0:1], D, sm)
            ts("dve", xn.ap, xb.ap, sm.ap[:, 0:1], None, ALU.mult, None, [xb, sm], [xn])
            if dst_hT is None:
                hT = hT_r.next()
            else:
                hT = dst_hT
            for half in range(2):
                pb = PSR.next()
                for j in range(4):
                    k = half * 4 + j
                    tp(pb.ap[:, j * 128:(j + 1) * 128], xn.ap[:, k * 128:(k + 1) * 128], [xn], [pb])
                for j in range(4):
                    k = half * 4 + j
                    act(hT.ap[:, k, col0:col0 + 128], pb.ap[:, j * 128:(j + 1) * 128], AF.Identity, [pb, GT, MODT], [hT],
                        scale=GT.ap[:, k, w:w + 1], bias=SHT[:, k, w:w + 1])
            return xb, hT

        def projF(hT, col0, M, out, pb):
            for k in range(8):
                mm(out, WIN.ap[:, k, col0:col0 + M], hT.ap[:, k, :], [WIN, hT], [pb], start=(k == 0), stop=(k == 7))

        def projT(hT, col0, N, out, pb):
            for k in range(8):
                mm(out, hT.ap[:, k, :], WIN.ap[:, k, col0:col0 + N], [WIN, hT], [pb], start=(k == 0), stop=(k == 7))

        def rope(src, dst, tmp, H, rp):
            tt("dve", tmp.ap.rearrange("p (h d) -> p h d", h=H), src.ap.rearrange("p (h d) -> p h d", h=H),
               rp.ap[:, None, 0:64].to_broadcast([128, H, 64]), ALU.mult, [src, rp], [tmp])
            s5 = src.ap.rearrange("p (h a f c) -> p h a f c", h=H, a=2, f=2)
            d5 = dst.ap.rearrange("p (h a f c) -> p h a f c", h=H, a=2, f=2)
            sneg = rp.ap[:, 64:96].rearrange("p (a c) -> p a c", a=2)[:, None, :, :].to_broadcast([128, H, 2, 16])
            spos = rp.ap[:, 96:128].rearrange("p (a c) -> p a c", a=2)[:, None, :, :].to_broadcast([128, H, 2, 16])
            tt("dve", d5[:, :, :, 0, :], s5[:, :, :, 1, :], sneg, ALU.mult, [src, rp], [dst])
            tt("dve", d5[:, :, :, 1, :], s5[:, :, :, 0, :], spos, ALU.mult, [src, rp], [dst])
            tt("dve", dst.ap, dst.ap, tmp.ap, ALU.add, [dst, tmp], [dst])

        def headnorm(buf, sq, H, dh, gbuf):
            sm = small.next()
            tt("dve", sq.ap, buf.ap, buf.ap, ALU.mult, [buf], [sq])
            red("dve", sm.ap[:, 0:H], sq.ap.rearrange("p (h d) -> p h d", h=H), [sq], [sm])
            rstd_from_ss(sm.ap[:, 0:H], dh, sm)
            tt("dve", buf.ap.rearrange("p (h d) -> p h d", h=H), buf.ap.rearrange("p (h d) -> p h d", h=H),
               sm.ap[:, 0:H, None].to_broadcast([128, H, dh]), ALU.mult, [buf, sm], [buf])
            tt("dve", buf.ap, buf.ap, gbuf.ap, ALU.mult, [buf, gbuf], [buf])

        def gla_block(d, pq, pz, do_out):
            di = 0 if d == "f" else 1
            zr = 0 if d == "f" else 32
            col = 127 if d == "f" else 0
            cp("act", zT.ap[0:48, :], pz.ap[0:48, 0:128], [pz], [zT])
            pl = PSR.next()
            mm(pl.ap[:, 0:256], zT.ap[0:48, :], wdec.ap[0:48, di, :], [zT, wdec], [pl], start=True, stop=False)
            mm(pl.ap[:, 0:256], onesr.ap[0:1, :], bdec.ap[0:1, di, :], [onesr, bdec], [pl], start=False, stop=True)
            act(e1.ap, pl.ap[:, 0:256], AF.Exp, [pl], [e1], scale=-1.0)
            ts("dve", e1.ap, e1.ap, 1.0, None, ALU.add, None, [e1], [e1])
            act(spb.ap, e1.ap, AF.Ln, [e1], [spb])
            if STOP_AT <= 1.41:
                return None
            pbT = PSR.next()
            for pr in range(2):
                mm(pbT.ap[:, pr * 128:(pr + 1) * 128], spb.ap[:, pr * 128:(pr + 1) * 128], tri.ap[:, di, :], [spb, tri], [pbT])
            act(epos.ap, pbT.ap[:, 0:256], AF.Exp, [pbT], [epos])
            act(eneg.ap, pbT.ap[:, 0:256], AF.Exp, [pbT], [eneg], scale=-1.0)
            if do_out:
                stt("dve", qdT.ap, pq.ap[:, 0:256], DK ** -0.5, epos.ap, ALU.mult, ALU.mult, [pq, epos], [qdT])
                tt("dve", kiT.ap, pq.ap[:, 256:512], eneg.ap, ALU.mult, [pq, eneg], [kiT])
            for pr in range(2):
                stt("dve", keT.ap[:, pr * 128:(pr + 1) * 128], eneg.ap[:, pr * 128:(pr + 1) * 128],
                    epos.ap[:, pr * 128 + col:pr * 128 + col + 1], pq.ap[:, 256 + pr * 128:384 + pr * 128], ALU.mult, ALU.mult,
                    [eneg, epos, pq], [keT])
            pke = PSR.next()
            for pr in range(2):
                tp(pke.ap[:, pr * 128:(pr + 1) * 128], keT.ap[:, pr * 128:(pr + 1) * 128], [keT], [pke])
            cp("act", ke.ap, pke.ap[:, 0:256], [pke], [ke])
            po = None
            if STOP_AT <= 1.42:
                return None
            if do_out:
                psc = [PSR.next(), PSR.next()]
                for h in range(4):
                    pr, par, base = h // 2, h % 2, 64 * (h % 2)
                    mm(psc[par].ap[:, pr * 128:(pr + 1) * 128], kiT.ap[base:base + 48, pr * 128:(pr + 1) * 128],
                       qdT.ap[base:base + 48, pr * 128:(pr + 1) * 128], [kiT, qdT], [psc[par]])
                for par in range(2):
                    tt("dve", scT.ap[:, 2 * par:2 * par + 2, :], psc[par].ap[:, 0:256].rearrange("p (h i) -> p h i", h=2),
                       maskb.ap[:, di:di + 1, :].to_broadcast([128, 2, 128]), ALU.mult, [psc[par], maskb], [scT])
                if STOP_AT <= 1.425:
                    return None
                po = PSR.next()
                po2 = [PSR.next(), PSR.next()]
                for h in range(4):
                    pr, par, base = h // 2, h % 2, 64 * (h % 2)
                    mm(po.ap[:, h * 96:(h + 1) * 96], scT.ap[:, 2 * par + pr, :], vbf.ap[:, h * 96:(h + 1) * 96], [scT, vbf], [po], start=True, stop=True)
                for h in range(4):
                    pr, par, base = h // 2, h % 2, 64 * (h % 2)
                    mm(po2[par].ap[:, h * 96:(h + 1) * 96], qdT.ap[base:base + 48, pr * 128:(pr + 1) * 128], Sbf[d].ap[base:base + 48, pr, :],
                       [qdT, Sbf[d]], [po2[par]], start=True, stop=True)
                po = (po, po2[0], po2[1])
            if STOP_AT <= 1.43:
                return None
            pup = PSR.next()
            for pr in range(2):
                mm(pup.ap[:, pr * 192:(pr + 1) * 192], ke.ap[:, pr * 128:(pr + 1) * 128], vbf.ap[:, pr * 192:(pr + 1) * 192], [ke, vbf], [pup])
            for h in range(4):
                pr, base = h // 2, 64 * (h % 2)
                stt("dve", Sst[d].ap[base:base + 48, pr, :], Sst[d].ap[base:base + 48, pr, :],
                    epos.ap[base:base + 48, pr * 128 + col:pr * 128 + col + 1],
                    pup.ap[base:base + 48, pr * 192 + (h % 2) * 96:pr * 192 + (h % 2) * 96 + 96], ALU.mult, ALU.add,
                    [Sst[d], epos, pup], [Sst[d]])
            cp("pool", Sbf[d].ap, Sst[d].ap, [Sst[d]], [Sbf[d]])
            return po

        def pass_B(sq, do_out):
            name, nb, w = sq["name"], sq["nb"], sq["w"]
            lists = []
            for n in reversed(range(nb)):
                select(n)
                PSR.set(n)
                S.cur = []
                lists.append(S.cur)
                pass_B_block(sq, do_out, n)
                S.cur = None
            S.merge_threads(lists)

        def pass_B_block(sq, do_out, n):
            name, nb, w = sq["name"], sq["nb"], sq["w"]
            if True:
                if STOP_AT <= 1.1:
                    return
                xb, hT = front(x_src[name][n * 128:(n + 1) * 128, :], G1T, SH1T, w)
                pk = PSR.next()
                projT(hT, C_AK, 256, pk.ap[:, 0:256], pk)
                cp("act", kn.ap, pk.ap[:, 0:128], [pk], [kn])
                cp("act", VA[name].ap[:, n, :, 0:64], pk.ap[:, 128:256].rearrange("p (g d) -> p g d", g=2), [pk], [VA[name]])
                pv = PSR.next()
                projT(hT, C_GV, 384, pv.ap[:, 0:384], pv)
                cp("act", vbf.ap, pv.ap[:, 0:384], [pv], [vbf])
                headnorm(kn, ksq, 2, 64, kg)
                ksrc = kn
                if sq["rope"]:
                    rp = rp_r.next()
                    dma("sp", rp.ap, rope_d[n * 128:(n + 1) * 128, :], [], [rp], f"rp{rp_r.i}")
                    rope(kn, kr, ksq, 2, rp)
                    ksrc = kr
                pkt = PSR.next()
                tp(pkt.ap[:, 0:128], ksrc.ap, [ksrc], [pkt])
                cp("act", KT[name].ap[:, n * 128:(n + 1) * 128], pkt.ap[:, 0:128], [pkt], [KT[name]])
                pq = PSR.next()
                for c in range(4):
                    if c < 2 and not do_out:
                        continue
                    projF(hT, C_Q + 128 * c, 128, pq.ap[:, c * 128:(c + 1) * 128], pq)
                pz = PSR.next()
                projF(hT, C_Z, 48, pz.ap[0:48, 0:128], pz)
                if do_out:
                    projF(hT, C_PU, 128, pz.ap[:, 128:256], pz)
                    projF(hT, C_PU + 128, 128, pz.ap[:, 256:384], pz)
                    cp("act", puT.ap, pz.ap[:, 128:384].rearrange("p (c t) -> p c t", c=2), [pz], [puT])
                    dma("sp", pu_d[name][:, :, 8 + n * 128:8 + (n + 1) * 128], puT.ap, [puT], [], "st_pu")
                po = gla_block("b", pq, pz, do_out)
                if STOP_AT <= 1.5:
                    return
                if do_out:
                    cp("act", obst.ap, po[0].ap[:, 0:384], [po[0]], [obst])
                    for par in range(2):
                        ov = obst.ap.rearrange("p (pr x) -> p pr x", pr=2)[:, :, par * 96:(par + 1) * 96]
                        pv2 = po[1 + par].ap[:, 0:384].rearrange("p (pr x) -> p pr x", pr=2)[:, :, par * 96:(par + 1) * 96]
                        tt("dve", ov, ov, pv2, ALU.add, [obst, po[1 + par]], [obst])
                    dma("sp", ob_d[name][n * 128:(n + 1) * 128, :], obst.ap, [obst], [], "st_ob")

        def pass_F(sq, do_out):
            name, nb, w = sq["name"], sq["nb"], sq["w"]
            lists = []
            for n in range(nb):
                select(n)
                PSR.set(n)
                S.cur = []
                lists.append(S.cur)
                pass_F_block(sq, do_out, n)
                S.cur = None
            S.merge_threads(lists)

        def pass_F_block(sq, do_out, n):
            name, nb, w = sq["name"], sq["nb"], sq["w"]
            if True:
                xb, hT = front(x_src[name][n * 128:(n + 1) * 128, :], G1T, SH1T, w)
                pv = PSR.next()
                projT(hT, C_GV, 384, pv.ap[:, 0:384], pv)
                cp("act", vbf.ap, pv.ap[:, 0:384], [pv], [vbf])
                if do_out:
                    pg = PSR.next()
                    projT(hT, C_GG, 384, pg.ap[:, 0:384], pg)
                    act(sgb.ap, pg.ap[:, 0:384], AF.Silu, [pg], [sgb])
                    pa = PSR.next()
                    projT(hT, C_AQ, 384, pa.ap[:, 0:384], pa)
                    cp("act", qn.ap, pa.ap[:, 0:384], [pa], [qn])
                pq = PSR.next()
                for c in range(4):
                    if c < 2 and not do_out:
                        continue
                    projF(hT, C_Q + 128 * c, 128, pq.ap[:, c * 128:(c + 1) * 128], pq)
                pz = PSR.next()
                projF(hT, C_Z, 48, pz.ap[0:48, 0:128], pz)
                if not do_out:
                    gla_block("f", pq, pz, False)
                    return
                dma("sp", obl.ap, ob_d[name][n * 128:(n + 1) * 128, :], [], [obl], "ld_ob")
                po = gla_block("f", pq, pz, True)
                tt("dve", ob_o.ap, po[0].ap[:, 0:384], obl.ap, ALU.add, [po[0], obl], [ob_o])
                for par in range(2):
                    ov = ob_o.ap.rearrange("p (pr x) -> p pr x", pr=2)[:, :, par * 96:(par + 1) * 96]
                    pv2 = po[1 + par].ap[:, 0:384].rearrange("p (pr x) -> p pr x", pr=2)[:, :, par * 96:(par + 1) * 96]
                    tt("dve", ov, ov, pv2, ALU.add, [ob_o, po[1 + par]], [ob_o])
                headnorm(ob_o, ob_sq, 4, 96, glag)
                tt("dve", ob_o.ap, ob_o.ap, sgb.ap, ALU.mult, [ob_o, sgb], [ob_o])
                pmt = PSR.next()
                for c in range(3):
                    tp(pmt.ap[:, c * 128:(c + 1) * 128], ob_o.ap[:, c * 128:(c + 1) * 128], [ob_o], [pmt])
                cp("act", mixT.ap[:, 0:3, :], pmt.ap[:, 0:384].rearrange("p (c t) -> p c t", c=3), [pmt], [mixT])
                headnorm(qn, qsq, 6, 64, qg)
                qsrc = qn
                if sq["rope"]:
                    rp = rp_r.next()
                    dma("sp", rp.ap, rope_d[n * 128:(n + 1) * 128, :], [], [rp], f"rp{rp_r.i}")
                    rope(qn, qr, qsq, 6, rp)
                    qsrc = qr
                pqt = PSR.next()
                for c in range(3):
                    tp(pqt.ap[:, c * 128:(c + 1) * 128], qsrc.ap[:, c * 128:(c + 1) * 128], [qsrc], [pqt])
                cp("act", qT.ap, pqt.ap[:, 0:384].rearrange("p (c t) -> p c t", c=3), [pqt], [qT])
                kbs = []
                if name == "lat":
                    if n > 0:
                        kbs.append(("lat", n - 1, 1))
                    kbs.append(("lat", n, None))
                    if n < nb - 1:
                        kbs.append(("lat", n + 1, 0))
                for cbk in range(LC // 128):
                    kbs.append(("ctx", cbk, None))
                for ki, (ks, kn_, mk) in enumerate(kbs):
                    for g in range(2):
                        ps_ = PSR.next()
                        for p_ in range(3):
                            mm(ps_.ap[:, p_ * 128:(p_ + 1) * 128], KT[ks].ap[g * 64:(g + 1) * 64, kn_ * 128:(kn_ + 1) * 128],
                               qT.ap[g * 64:(g + 1) * 64, p_, :], [KT[ks], qT], [ps_])
                        act(PT.ap[:, ki, 3 * g:3 * g + 3, :], ps_.ap[:, 0:384].rearrange("p (h i) -> p h i", h=3), AF.Exp, [ps_], [PT], scale=0.125)
                        if mk is not None:
                            tt("dve", PT.ap[:, ki, 3 * g:3 * g + 3, :], PT.ap[:, ki, 3 * g:3 * g + 3, :],
                               maskb.ap[:, mk:mk + 1, :].to_broadcast([128, 3, 128]), ALU.mult, [PT, maskb], [PT])
                pov = PSR.next()
                for h in range(6):
                    g = h // 3
                    for ki, (ks, kn_, mk) in enumerate(kbs):
                        mm(pov.ap[:, h * 65:(h + 1) * 65], PT.ap[:, ki, h, :], VA[ks].ap[:, kn_, g, :], [PT, VA[ks]], [pov],
                           start=(ki == 0), stop=(ki == len(kbs) - 1))
                sm = small.next()
                pov3 = pov.ap[:, 0:390].rearrange("p (h e) -> p h e", e=65)
                tt("dve", sm.ap[:, 0:6], pov3[:, :, 64], esink.ap, ALU.add, [pov, esink], [sm])
                recip(sm.ap[:, 0:6], sm.ap[:, 0:6], [sm], [sm])
                tt("dve", oatt.ap.rearrange("p (h d) -> p h d", h=6), pov3[:, :, 0:64], sm.ap[:, 0:6, None].to_broadcast([128, 6, 64]),
                   ALU.mult, [pov, sm], [oatt])
                pat = PSR.next()
                for c in range(3):
                    tp(pat.ap[:, c * 128:(c + 1) * 128], oatt.ap[:, c * 128:(c + 1) * 128], [oatt], [pat])
                cp("act", mixT.ap[:, 3:6, :], pat.ap[:, 0:384].rearrange("p (c t) -> p c t", c=3), [pat], [mixT])
                dma("sp", puw.ap, pu_d[name][:, :, n * 128:n * 128 + 144], [], [puw], "ld_pu")
                tt("pool", s2.ap[:, :, 1:144], puw.ap[:, :, 0:143], puw.ap[:, :, 1:144], ALU.add, [puw], [s2])
                tt("pool", s4.ap[:, :, 2:143], s2.ap[:, :, 1:142], s2.ap[:, :, 3:144], ALU.add, [s2], [s4])
                tt("pool", s8.ap[:, 1, 4:141], s4.ap[:, 1, 2:139], s4.ap[:, 1, 6:143], ALU.add, [s4], [s8])
                tt("pool", s16.ap[:, 1, 8:136], s8.ap[:, 1, 4:132], s8.ap[:, 1, 12:140], ALU.add, [s8], [s16])
                combos = ((0, 64, 0, s2), (64, 128, 0, s4), (0, 64, 1, s8), (64, 128, 1, s16))
                for (r0, r1, c, sb_) in combos:
                    stt("dve", dT.ap[r0:r1, c, :], sb_.ap[r0:r1, c, 8:136], pinv.ap[r0:r1, c:c + 1], puw.ap[r0:r1, c, 8:136],
                        ALU.mult, ALU.subtract, [sb_, pinv, puw], [dT])
                for edge, cols, dcols in ((0, slice(8, 16), slice(0, 8)), (1, slice(128, 136), slice(120, 128))):
                    if (edge == 0 and n == 0) or (edge == 1 and n == nb - 1):
                        for (r0, r1, c, sb_) in combos:
                            tt("pool", tmp8.ap[r0:r1, :], sb_.ap[r0:r1, c, cols], pedge.ap[r0:r1, edge, c, :], ALU.mult, [sb_, pedge], [tmp8])
                            tt("pool", dT.ap[r0:r1, c, dcols], tmp8.ap[r0:r1, :], puw.ap[r0:r1, c, cols], ALU.subtract, [tmp8, puw], [dT])
                pp = PSR.next()
                for c in range(2):
                    mm(pp.ap[:, c * 128:(c + 1) * 128], pw.ap[:, c, :], dT.ap[:, c, :], [pw, dT], [pp])
                for c in range(2):
                    act(mixT.ap[:, 6 + c, :], pp.ap[:, c * 128:(c + 1) * 128], AF.Copy, [pp, pscT], [mixT], scale=pscT.ap[:, c:c + 1])
                xm = xm_r.next()
                for half in range(2):
                    py = PSR.next()
                    for kc in range(8):
                        mm(py.ap, mixT.ap[:, kc, :], WOUT.ap[:, kc, half * 512:(half + 1) * 512], [mixT, WOUT], [py], start=(kc == 0), stop=(kc == 7))
                    hs = slice(half * 512, (half + 1) * 512)
                    tt("dve", ytmp.ap[:, hs], py.ap, G1B.ap[:, w, hs], ALU.mult, [py, G1B], [ytmp])
                    tt("pool", xm.ap[:, hs], ytmp.ap[:, hs], xb.ap[:, hs], ALU.add, [ytmp, xb], [xm])
                dma("sp", xmid_d[name][n * 128:(n + 1) * 128, :], xm.ap, [xm], [], f"st_xm{xm_r.i}")

        for d in "fb":
            memset("dve", Sst[d].ap, 0.0, [Sst[d]])
            memset("pool", Sbf[d].ap, 0.0, [Sbf[d]])
        ctx_out = not last
        if STOP_AT <= 1:
            break
        pass_B(seqs["ctx"], ctx_out)
        if STOP_AT <= 2:
            break
        pass_F(seqs["ctx"], ctx_out)
        if STOP_AT <= 3:
            break
        pass_B(seqs["lat"], True)
        if STOP_AT <= 4:
            break
        pass_F(seqs["lat"], True)
        if STOP_AT <= 5:
            break

        S.barrier()
        AR.reset(gbmark)
        WUP = newbuf(AR, [8, 2 * D_FF], BF16, "WUP"); WDN = newbuf(AR, [22, D], BF16, "WDN")
        wuv = wup_d[l].rearrange("(k p) n -> p k n", p=128)
        wdv = wdn_d[l].rearrange("(k p) n -> p k n", p=128)
        for k in range(8):
            for cc in range(11):
                dma("pool", WUP.ap[:, k, cc * 512:(cc + 1) * 512], wuv[:, k, cc * 512:(cc + 1) * 512], [], [WUP], "w_up")
        for k in range(22):
            for cc in range(2):
                dma("pool", WDN.ap[:, k, cc * 512:(cc + 1) * 512], wdv[:, k, cc * 512:(cc + 1) * 512], [], [WDN], "w_dn")
        xs_r = ring(AR, 2, [D], F32, "xs"); xn = newbuf(AR, [D]); junk = newbuf(AR, [D], BF16)
        h2T_r = [newbuf(AR, [8, 256], BF16, "h2Ta"), newbuf(AR, [8, 256], BF16, "h2Tb")]; actT = newbuf(AR, [22, 256], BF16, "actT")
        cva = ring(AR, 3, [256], F32, "cva"); cvb = ring(AR, 1, [256], F32, "cvb"); cga = ring(AR, 3, [256], F32, "cga")
        cgb = ring(AR, 1, [256], F32, "cgb"); csg = ring(AR, 2, [256], F32, "csg")
        xr_r = ring(AR, 1, [D], F32, "xr"); yt_r = ring(AR, 1, [D], F32, "yt")
        print("ARENA ffn end", AR.off, "of", AR.size)

        def ffn_pass(sq):
            name, n_tok, w = sq["name"], sq["n"], sq["w"]
            starts = sorted(set(min(FT * j, n_tok - FT) for j in range((n_tok + FT - 1) // FT)))
            tiles = []
            for ti, s in enumerate(starts):
                h2T = h2T_r[ti % 2]
                PSR.set(ti)
                T = []
                S.cur = T
                for bi in range(2):
                    r0 = s - 1 + 128 * bi
                    lo, hi = max(r0, 0), min(r0 + 128, n_tok)

                    def pre(xb, r0=r0, lo=lo, hi=hi):
                        if lo > r0:
                            memset("dve", xb.ap[0:32, :], 0.0, [xb])
                        if hi < r0 + 128:
                            memset("dve", xb.ap[96:128, :], 0.0, [xb])
                        dma("sp", xb.ap[lo - r0:hi - r0, :], xmid_d[name][lo:hi, :], [], [xb], f"xs{xs_r.i}")
                    front(None, G2T, SH2T, w, dst_hT=h2T, col0=bi * 128, pre=pre)
                if s == 0:
                    memset("dve", h2T.ap[:, :, 0:1], 0.0, [h2T])
                if s + FT == n_tok:
                    memset("dve", h2T.ap[:, :, 255:256], 0.0, [h2T])
                clists = []
                for c in range(22):
                    S.cur = []
                    clists.append(S.cur)
                    pus = (PSR.next(), PSR.next())
                    for pu_, col in ((pus[0], c * 128), (pus[1], D_FF + c * 128)):
                        for k in range(8):
                            mm(pu_.ap[:, 0:256], WUP.ap[:, k, col:col + 128], h2T.ap[:, k, :], [WUP, h2T], [pu_], start=(k == 0), stop=(k == 7))
                    res = []
                    for fc, pu_, ra, rb in ((c, pus[0], cva, cvb), (22 + c, pus[1], cga, cgb)):
                        t1, t2 = ra.next(), rb.next()
                        act(t1.ap[:, 0:FT], pu_.ap[:, 1:1 + FT], AF.Identity, [pu_, cw, cb], [t1], scale=cw.ap[:, fc, 1:2], bias=cb.ap[:, fc:fc + 1])
                        stt("dve", t2.ap[:, 0:FT], pu_.ap[:, 0:FT], cw.ap[:, fc, 0:1], t1.ap[:, 0:FT], ALU.mult, ALU.add, [pu_, cw, t1], [t2])
                        stt("dve", t1.ap[:, 0:FT], pu_.ap[:, 2:2 + FT], cw.ap[:, fc, 2:3], t2.ap[:, 0:FT], ALU.mult, ALU.add, [pu_, cw, t2], [t1])
                        res.append(t1)
                    sg_ = csg.next()
                    act(sg_.ap[:, 0:FT], res[1].ap[:, 0:FT], AF.Silu, [res[1]], [sg_])
                    tt("pool", actT.ap[:, c, 0:FT], sg_.ap[:, 0:FT], res[0].ap[:, 0:FT], ALU.mult, [sg_, res[0]], [actT])
                S.cur = None
                T.extend(S.merge_threads(clists, ret=True))
                S.cur = T
                for sub, m in ((0, 128), (1, FT - 128)):
                    t0 = s + sub * 128
                    xr = xr_r.next()
                    dma("sp", xr.ap[0:m, :], xmid_d[name][t0:t0 + m, :], [], [xr], f"xr{xr_r.i}")
                    yt = yt_r.next()
                    for half in range(2):
                        py = PSR.next()
                        for c in range(22):
                            mm(py.ap[0:m, :], actT.ap[:, c, sub * 128:sub * 128 + m], WDN.ap[:, c, half * 512:(half + 1) * 512], [actT, WDN], [py],
                               start=(c == 0), stop=(c == 21))
                        hs = slice(half * 512, (half + 1) * 512)
                        tt("dve", yt.ap[0:m, hs], py.ap[0:m, :], G2B.ap[0:m, w, hs], ALU.mult, [py, G2B], [yt])
                        tt("pool", yt.ap[0:m, hs], yt.ap[0:m, hs], xr.ap[0:m, hs], ALU.add, [yt, xr], [yt])
                    dma("sp", x_dst[name][t0:t0 + m, :], yt.ap[0:m, :], [yt], [], f"st_xo{yt_r.i}")
                S.cur = None
                tiles.append(T)
            S.merge_threads(tiles)

        if STOP_AT <= 6:
            break
        if not last:
            ffn_pass(seqs["ctx"])
        if STOP_AT <= 7:
            break
        ffn_pass(seqs["lat"])
        if STOP_AT <= 8:
            break

    S.barrier()
    S.emit(st)
    st.close()
    return nc, S


def host_consts(SL):
    ident = np.eye(128, dtype=np.float32)
    j = np.arange(128)[:, None]
    i = np.arange(128)[None, :]
    le = (j <= i).astype(np.float32)
    ge = (j >= i).astype(np.float32)
    tri = np.stack([le, ge], axis=1) * np.float32(-1.0 / 16.0)
    mask = np.stack([le, ge], axis=1)
    rows_n = SL // GRID_W
    rows = np.repeat(np.arange(rows_n), GRID_W).astype(np.float32)
    cols = np.tile(np.arange(GRID_W), rows_n).astype(np.float32)
    nf = 16
    inv = (np.float32(10000.0) ** (-np.arange(nf, dtype=np.float32) / np.float32(nf))).astype(np.float32)
    ang = np.concatenate([rows[:, None] * inv, cols[:, None] * inv], axis=-1).astype(np.float32)
    cos, sin = np.cos(ang).astype(np.float32), np.sin(ang).astype(np.float32)
    cr, cc, sr, sc = cos[:, :16], cos[:, 16:], sin[:, :16], sin[:, 16:]
    rope = np.concatenate([cr, cr, cc, cc, -sr, -sc, sr, sc], axis=-1).astype(np.float32)
    wins = (2, 4, 8, 16)
    pinv = np.zeros((128, 2), np.float32)
    pedge = np.zeros((128, 2, 2, 8), np.float32)
    T = 1 << 20
    for c in range(2):
        for gi in range(2):
            wd = wins[2 * c + gi]
            sl = slice(gi * 64, (gi + 1) * 64)
            pinv[sl, c] = 1.0 / wd
            for t in range(8):
                cnt_first = min(t + wd // 2, T) - max(t - wd // 2, 0)
                tl = T - 8 + t
                cnt_last = min(tl + wd // 2, T) - max(tl - wd // 2, 0)
                pedge[sl, 0, c, t] = 1.0 / cnt_first
                pedge[sl, 1, c, t] = 1.0 / cnt_last
    return dict(ident=ident, tri=np.ascontiguousarray(tri), mask=np.ascontiguousarray(mask), rope=rope, pinv=pinv, pedge=pedge)


def host_weights(inp):
    f = lambda a: np.ascontiguousarray(np.asarray(a, dtype=np.float32))
    L = DEPTH
    out = {}
    out["w_ada"] = f(inp["w_ada"])
    out["b_adaT"] = f(np.asarray(inp["b_ada"]).reshape(L, 48, 128).transpose(0, 2, 1))
    out["b_adar"] = f(np.asarray(inp["b_ada"]).reshape(L, 1, 6 * D))
    out["n1T"] = f(np.asarray(inp["norm1_g"]).reshape(L, 8, 128).transpose(0, 2, 1))
    out["n2T"] = f(np.asarray(inp["norm2_g"]).reshape(L, 8, 128).transpose(0, 2, 1))
    out["w_in"] = f(inp["w_in"])
    wd = np.asarray(inp["gla_w_dec"])
    bd = np.asarray(inp["gla_b_dec"])
    wdec = np.zeros((L, 48, 2, 256), np.float32)
    bdec = np.zeros((L, 1, 2, 256), np.float32)
    for d in range(2):
        for h in range(4):
            wdec[:, 32 * d:32 * d + 16, d, 64 * h:64 * h + 48] = wd[:, d, :, 48 * h:48 * h + 48]
            bdec[:, 0, d, 64 * h:64 * h + 48] = bd[:, d, 48 * h:48 * h + 48]
    out["wdec"], out["bdec"] = wdec, bdec
    out["glag"] = f(np.broadcast_to(np.tile(np.asarray(inp["gla_norm_g"]), (1, 4))[:, None, :], (L, 128, 384)))
    out["qg"] = f(np.broadcast_to(np.tile(np.asarray(inp["q_norm_g"]), (1, 6))[:, None, :], (L, 128, 384)))
    out["kg"] = f(np.broadcast_to(np.tile(np.asarray(inp["k_norm_g"]), (1, 2))[:, None, :], (L, 128, 128)))
    out["sink"] = f(np.broadcast_to(np.asarray(inp["sink_logit"])[:, None, :], (L, 128, 6)))
    pwi = np.asarray(inp["pool_w"])
    pw = np.zeros((L, 128, 2, 128), np.float32)
    for c in range(2):
        for gi in range(2):
            pw[:, gi * 64:(gi + 1) * 64, c, gi * 64:(gi + 1) * 64] = pwi[:, 2 * c + gi]
    out["pw"] = pw
    out["pscT"] = f(np.asarray(inp["pool_scale"]).reshape(L, 2, 128).transpose(0, 2, 1))
    out["w_out"] = f(inp["w_out"])
    out["w_up"] = f(inp["w_up"])
    out["cw"] = f(np.asarray(inp["conv_w"]).reshape(L, 3, 44, 128).transpose(0, 3, 2, 1))
    out["cb"] = f(np.asarray(inp["conv_b"]).reshape(L, 44, 128).transpose(0, 2, 1))
    out["w_down"] = f(inp["w_down"])
    return out


_CACHE = {}


def kernel(**inputs):
    x = np.asarray(inputs["x"], dtype=np.float32)
    ctx = np.asarray(inputs["ctx"], dtype=np.float32)
    c = np.asarray(inputs["c"], dtype=np.float32)
    c_ctx = np.asarray(inputs["c_ctx"], dtype=np.float32)
    Bn, SL, _ = x.shape
    LC = ctx.shape[1]
    key = (SL, LC)
    if key not in _CACHE:
        _CACHE[key] = build(SL, LC)
    nc, _ = _CACHE[key]
    shared = dict(host_consts(SL))
    shared.update(host_weights(inputs))
    in_maps = []
    for b in range(Bn):
        m = dict(shared)
        m["x"] = np.ascontiguousarray(x[b])
        m["ctx"] = np.ascontiguousarray(ctx[b])
        cT = np.stack([c[b].reshape(8, 128).T, c_ctx.reshape(8, 128).T], axis=-1)
        m["cT"] = np.ascontiguousarray(cT.astype(np.float32))
        in_maps.append(m)
    res = run_bass_kernel_spmd(nc, in_maps, core_ids=list(range(Bn)))
    return np.stack([np.asarray(r["out"], dtype=np.float32) for r in res.results], axis=0)
```

```python
import numpy as np
from contextlib import ExitStack
import concourse.bass as bass
import concourse.mybir as mybir
from concourse.bass_utils import run_bass_kernel_spmd

F32 = mybir.dt.float32
BF16 = mybir.dt.bfloat16
AF = mybir.ActivationFunctionType
ALU = mybir.AluOpType
AX = mybir.AxisListType

D = 1024
DEPTH = 2
GRID_W = 64
NH_G, DK, DV = 4, 48, 96
D_FF = 2816
EPS = 1e-6
IN_W = 2080
C_Q, C_K, C_Z, C_PU, C_GV, C_GG, C_AQ, C_AK = 0, 256, 512, 576, 832, 1216, 1600, 1984
NW = 2240
FT = 254

SAME_ENGINE_SYNC = True
import os
STOP_AT = float(os.environ.get('MK_STOP', '99'))


class Tok:
    __slots__ = ("name", "w", "r", "excl")

    def __init__(self, name):
        self.name = name
        self.w = []
        self.r = {}
        self.excl = False


class Op:
    __slots__ = ("eng", "fn", "deps", "odeps", "sig", "dma", "key", "ev", "tag")

    def __init__(self, eng, fn, dma, key):
        self.eng, self.fn, self.dma, self.key = eng, fn, dma, key
        self.deps = []
        self.odeps = []
        self.sig = False
        self.ev = None


class Sched:
    ENGS = ("pe", "act", "dve", "pool", "sp")

    def __init__(self, nc):
        self.nc = nc
        self.ops = []
        self.cur = None
        self.log = []
        self.ntok = 0

    def tok(self, name=None):
        self.ntok += 1
        return Tok(name or f"t{self.ntok}")

    def op(self, eng, fn, reads=(), writes=(), dma=False, key=None):
        o = Op(eng, fn, dma, key)
        if os.environ.get('MK_DEBUG'):
            import inspect
            o.tag = inspect.stack()[2].lineno
        deps = {}
        for t in reads:
            for w in t.w:
                deps[id(w)] = w
            if t.excl:
                for re_, rd in t.r.items():
                    if re_ != eng and not isinstance(rd, list):
                        deps[id(rd)] = rd
        for t in writes:
            samekey = dma and t.w and all(w.dma and w.key == key for w in t.w) and not t.r
            if not samekey:
                for w in t.w:
                    deps[id(w)] = w
                for rd in t.r.values():
                    if isinstance(rd, list):
                        for x in rd:
                            deps[id(x)] = x
                    else:
                        deps[id(rd)] = rd
        for d in deps.values():
            if (not d.dma) and (not dma) and d.eng == eng:
                if eng == "pe" or not SAME_ENGINE_SYNC:
                    o.odeps.append(d)
                    continue
            o.deps.append(d)
            d.sig = True
        for t in reads:
            if dma:
                t.r.setdefault("dma", []).append(o)
            else:
                t.r[eng] = o
        for t in writes:
            samekey = dma and t.w and all(w.dma and w.key == key for w in t.w) and not t.r
            if samekey:
                t.w.append(o)
            else:
                t.w = [o]
                t.r = {}
        if dma:
            o.sig = True
        (self.cur if self.cur is not None else self.ops).append(o)
        return o

    def merge_threads(self, lists, ratio=1, ret=False):
        unem = set()
        for L in lists:
            for o in L:
                unem.add(id(o))
        out = []
        k = len(lists)
        pos = [0] * k
        i = 0
        while i < k:
            L = lists[i]
            if pos[i] >= len(L):
                i += 1
                continue
            o = L[pos[i]]
            pos[i] += 1
            out.append(o)
            unem.discard(id(o))
            j = i + 1
            if j < k:
                Lj = lists[j]
                for _ in range(ratio):
                    if pos[j] < len(Lj):
                        o2 = Lj[pos[j]]
                        if all(id(d) not in unem for d in o2.deps) and all(id(d) not in unem for d in o2.odeps):
                            pos[j] += 1
                            out.append(o2)
                            unem.discard(id(o2))
        if ret:
            return out
        self.ops.extend(out)

    def barrier(self):
        last, dmas = {}, {}
        for o in self.ops:
            if o.fn is None:
                continue
            if o.dma:
                dmas[o.key] = o
            else:
                last[o.eng] = o
        for e in self.ENGS:
            b = Op(e, None, False, None)
            for d in list(last.values()) + list(dmas.values()):
                if (not d.dma) and d.eng == e:
                    continue
                b.deps.append(d)
                d.sig = True
            self.ops.append(b)

    def emit(self, stack):
        nc = self.nc
        cnt = {e: 0 for e in self.ENGS}
        keycnt = {}
        for o in self.ops:
            if o.fn is None:
                continue
            if o.dma:
                keycnt[o.key] = keycnt.get(o.key, 0) + 16
                o.ev = (("dma", o.key), keycnt[o.key])
            elif o.sig:
                cnt[o.eng] += 1
                o.ev = (("eng", o.eng), cnt[o.eng])
        sems = {}
        for e in self.ENGS:
            if cnt[e]:
                sems[("eng", e)] = stack.enter_context(nc.semaphore(f"c_{e}"))
        for k in keycnt:
            sems[("dma", k)] = stack.enter_context(nc.semaphore(f"d_{k}"))
        per = {e: [] for e in self.ENGS}
        for o in self.ops:
            per[o.eng].append(o)
        self.stats = {e: len(per[e]) for e in self.ENGS}
        self.stats["sems"] = len(sems)
        self.stats["maxcnt"] = dict(cnt)
        block = stack.enter_context(nc.Block())

        def run(engobj, lst):
            waited = {}
            for o in lst:
                need = {}
                for d in o.deps:
                    if d.ev is None:
                        continue
                    s, v = d.ev
                    if waited.get(s, 0) >= v:
                        continue
                    if need.get(s, 0) < v:
                        need[s] = v
                for s, v in need.items():
                    engobj.wait_ge(sems[s], v)
                    waited[s] = v
                    if os.environ.get('MK_DEBUG'):
                        self.log.append(f"{o.eng} WAIT {s} >= {v}")
                if os.environ.get('MK_DEBUG') and o.fn is not None:
                    self.log.append(f"{o.eng} OP line{getattr(o, 'tag', 0)} ev={o.ev}")
                if o.fn is None:
                    continue
                ins = o.fn(engobj)
                if o.ev is not None:
                    ins.then_inc(sems[o.ev[0]], 16 if o.dma else 1)

        @block.tensor
        def _(e):
            run(e, per["pe"])

        @block.scalar
        def _(e):
            run(e, per["act"])

        @block.vector
        def _(e):
            run(e, per["dve"])

        @block.gpsimd
        def _(e):
            run(e, per["pool"])

        @block.sync
        def _(e):
            run(e, per["sp"])


class Arena:
    def __init__(self, ap):
        self.ap = ap
        self.size = ap.shape[1]
        self.off = 0

    def reset(self, to=0):
        self.off = to

    def alloc(self, shape, dtype=F32):
        n = int(np.prod(shape))
        bpe = 4 if dtype == F32 else 2
        words = (n * bpe + 3) // 4
        words = (words + 7) // 8 * 8
        assert self.off + words <= self.size, f"arena overflow {self.off}+{words}>{self.size}"
        v = self.ap[:, self.off:self.off + words]
        self.off += words
        if dtype != F32:
            v = v.bitcast(dtype)
        v = v[:, 0:n]
        if len(shape) == 1:
            return v
        names = "abcd"[:len(shape)]
        pat = "p (" + " ".join(names) + ") -> p " + " ".join(names)
        return v.rearrange(pat, **{names[i]: shape[i] for i in range(len(shape))})


class B:
    __slots__ = ("ap", "t")

    def __init__(self, ap, t):
        self.ap, self.t = ap, t


class Ring:
    def __init__(self, bufs):
        self.bufs = bufs
        self.i = -1

    def next(self):
        self.i = (self.i + 1) % len(self.bufs)
        return self.bufs[self.i]


def build(SL, LC):
    nc = bass.Bass("TRN2", target_bir_lowering=False)

    def dten(name, shape, kind="ExternalInput"):
        return nc.dram_tensor(name, list(shape), F32, kind=kind).ap()

    x_d = dten("x", [SL, D]); ctx_d = dten("ctx", [LC, D]); cT_d = dten("cT", [128, 8, 2])
    ident_d = dten("ident", [128, 128]); tri_d = dten("tri", [128, 2, 128]); mask_d = dten("mask", [128, 2, 128])
    rope_d = dten("rope", [SL, 128]); pinv_d = dten("pinv", [128, 2]); pedge_d = dten("pedge", [128, 2, 2, 8])
    wada_d = dten("w_ada", [DEPTH, D, 6 * D]); badaT_d = dten("b_adaT", [DEPTH, 128, 48]); badar_d = dten("b_adar", [DEPTH, 1, 6 * D])
    n1T_d = dten("n1T", [DEPTH, 128, 8]); n2T_d = dten("n2T", [DEPTH, 128, 8])
    win_d = dten("w_in", [DEPTH, D, IN_W]); wdec_d = dten("wdec", [DEPTH, 48, 2, 256]); bdec_d = dten("bdec", [DEPTH, 1, 2, 256])
    glag_d = dten("glag", [DEPTH, 128, 384]); qg_d = dten("qg", [DEPTH, 128, 384]); kg_d = dten("kg", [DEPTH, 128, 128])
    sink_d = dten("sink", [DEPTH, 128, 6]); pw_d = dten("pw", [DEPTH, 128, 2, 128]); pscT_d = dten("pscT", [DEPTH, 128, 2])
    wout_d = dten("w_out", [DEPTH, D, D]); wup_d = dten("w_up", [DEPTH, D, 2 * D_FF]); cw_d = dten("cw", [DEPTH, 128, 44, 3])
    cb_d = dten("cb", [DEPTH, 128, 44]); wdn_d = dten("w_down", [DEPTH, D_FF, D])
    out_d = dten("out", [SL, D], kind="ExternalOutput")
    ob_d = {"lat": dten("ob_lat", [SL, 384], "Internal"), "ctx": dten("ob_ctx", [LC, 384], "Internal")}
    xmid_d = {"lat": dten("xmid_lat", [SL, D], "Internal"), "ctx": dten("xmid_ctx", [LC, D], "Internal")}
    x1_d = {"lat": dten("x1_lat", [SL, D], "Internal"), "ctx": dten("x1_ctx", [LC, D], "Internal")}
    pu_d = {"lat": dten("pu_lat", [128, 2, SL + 16], "Internal"), "ctx": dten("pu_ctx", [128, 2, LC + 16], "Internal")}

    st = ExitStack()
    S = Sched(nc)
    PERS_W, ARENA_W = 3904, 49280
    pers_t = st.enter_context(nc.sbuf_tensor("pers", [128, PERS_W], F32))
    arena_t = st.enter_context(nc.sbuf_tensor("arena", [128, ARENA_W], F32))
    PERS = Arena(pers_t[:, :]); AR = Arena(arena_t[:, :])
    psb = [st.enter_context(nc.psum_tensor(f"ps{i}", [128, 512], F32)) for i in range(8)]
    class PRing:
        def __init__(self, bufs):
            self.bufs = bufs
            self.par = 0
            self.i = [-1, -1]

        def set(self, par):
            self.par = par % 2

        def next(self):
            p = self.par
            self.i[p] = (self.i[p] + 1) % 4
            return self.bufs[4 * p + self.i[p]]

    PSR = PRing([B(psb[i][:, :], S.tok(f"ps{i}")) for i in range(8)])
    for b_ in PSR.bufs:
        b_.t.excl = True

    def newbuf(arena, shape, dtype=F32, name=None):
        return B(arena.alloc(shape, dtype), S.tok(name))

    def ring(arena, n, shape, dtype=F32, name="r"):
        return Ring([newbuf(arena, shape, dtype, f"{name}{i}") for i in range(n)])

    def toks(bs):
        return [b.t if isinstance(b, B) else b for b in bs]

    def mm(out, lhsT, rhs, R, W, start=True, stop=True):
        S.op("pe", lambda e: e.matmul(out, lhsT=lhsT, rhs=rhs, start=start, stop=stop), toks(R), toks(W))

    def tp(out, in_, R, W):
        S.op("pe", lambda e: e.transpose(out, in_, ident.ap), toks(R) + [ident.t], toks(W))

    def act(out, in_, func, R, W, **kw):
        S.op("act", lambda e: e.activation(out=out, in_=in_, func=func, **kw), toks(R), toks(W))

    def tt(eng, out, in0, in1, op, R, W):
        S.op(eng, lambda e: e.tensor_tensor(out=out, in0=in0, in1=in1, op=op), toks(R), toks(W))

    def ts(eng, out, in0, s1, s2, op0, op1, R, W):
        if s2 is None:
            S.op(eng, lambda e: e.tensor_scalar(out=out, in0=in0, scalar1=s1, scalar2=None, op0=op0), toks(R), toks(W))
        else:
            S.op(eng, lambda e: e.tensor_scalar(out=out, in0=in0, scalar1=s1, scalar2=s2, op0=op0, op1=op1), toks(R), toks(W))

    def stt(eng, out, in0, scalar, in1, op0, op1, R, W):
        S.op(eng, lambda e: e.scalar_tensor_tensor(out=out, in0=in0, scalar=scalar, in1=in1, op0=op0, op1=op1), toks(R), toks(W))

    def cp(eng, out, in_, R, W):
        if eng == "act":
            act(out, in_, AF.Copy, R, W)
        else:
            S.op(eng, lambda e: e.tensor_copy(out=out, in_=in_), toks(R), toks(W))

    def memset(eng, ap, val, W):
        S.op(eng, lambda e: e.memset(ap, val), [], toks(W))

    def red(eng, out, in_, R, W):
        S.op(eng, lambda e: e.tensor_reduce(out=out, in_=in_, axis=AX.X, op=ALU.add), toks(R), toks(W))

    def recip(out, in_, R, W):
        S.op("dve", lambda e: e.reciprocal(out=out, in_=in_), toks(R), toks(W))

    def dma(eng, out, in_, R, W, key):
        S.op(eng, lambda e: e.dma_start(out=out, in_=in_), toks(R), toks(W), dma=True, key=key)

    def rstd_from_ss(ss, n, R_buf):
        ts("dve", ss, ss, 1.0 / n, EPS, ALU.mult, ALU.add, [R_buf], [R_buf])
        act(ss, ss, AF.Ln, [R_buf], [R_buf])
        act(ss, ss, AF.Exp, [R_buf], [R_buf], scale=-0.5)

    ident = newbuf(PERS, [128], F32, "ident"); ident.ap = ident.ap
    tri = newbuf(PERS, [2, 128]); maskb = newbuf(PERS, [2, 128]); pinv = newbuf(PERS, [2]); pedge = newbuf(PERS, [2, 2, 8])
    cact = newbuf(PERS, [8, 2]); onesr = newbuf(PERS, [128]); zeros = newbuf(PERS, [144])
    MODT = newbuf(PERS, [48, 2]); badaT = newbuf(PERS, [48]); n1T = newbuf(PERS, [8]); n2T = newbuf(PERS, [8])
    G1T = newbuf(PERS, [8, 2]); G2T = newbuf(PERS, [8, 2])
    glag = newbuf(PERS, [384]); qg = newbuf(PERS, [384])
    kg = newbuf(PERS, [128]); esink = newbuf(PERS, [6]); pscT = newbuf(PERS, [2]); cw = newbuf(PERS, [44, 3]); cb = newbuf(PERS, [44])
    pw = newbuf(PERS, [2, 128], BF16)
    Sst = {d: newbuf(PERS, [2, 96]) for d in "fb"}
    Sbf = {d: newbuf(PERS, [2, 96], BF16) for d in "fb"}
    small = Ring([newbuf(PERS, [16], F32, f"small{i}") for i in range(8)])

    dma("sp", ident.ap, ident_d, [], [ident], "c_ident")
    dma("sp", tri.ap, tri_d, [], [tri], "c_tri")
    dma("sp", maskb.ap, mask_d, [], [maskb], "c_mask")
    dma("sp", pinv.ap, pinv_d, [], [pinv], "c_pinv")
    dma("sp", pedge.ap, pedge_d, [], [pedge], "c_pedge")
    dma("sp", cact.ap, cT_d, [], [cact], "c_cact")
    memset("dve", onesr.ap, 1.0, [onesr])
    memset("dve", zeros.ap, 0.0, [zeros])
    act(cact.ap, cact.ap, AF.Silu, [cact], [cact])
    for sname, n in (("lat", SL), ("ctx", LC)):
        dma("sp", pu_d[sname][:, :, 0:8], zeros.ap[:, 0:16].rearrange("p (a b) -> p a b", a=2), [zeros], [], "c_pz")
        dma("sp", pu_d[sname][:, :, 8 + n:16 + n], zeros.ap[:, 0:16].rearrange("p (a b) -> p a b", a=2), [zeros], [], "c_pz")

    seqs = {"lat": dict(name="lat", n=SL, nb=SL // 128, w=0, rope=True),
            "ctx": dict(name="ctx", n=LC, nb=LC // 128, w=1, rope=False)}

    for l in range(DEPTH):
        last = (l == DEPTH - 1)
        x_src = {"lat": x_d if l == 0 else x1_d["lat"], "ctx": ctx_d if l == 0 else x1_d["ctx"]}
        x_dst = {"lat": out_d if last else x1_d["lat"], "ctx": x1_d["ctx"]}
        S.barrier()
        AR.reset()
        G2B = newbuf(AR, [2, D], F32, "G2B")
        gbmark = AR.off
        G1B = newbuf(AR, [2, D], F32, "G1B")
        KT = {"lat": newbuf(AR, [SL], BF16, "KT"), "ctx": newbuf(AR, [LC], BF16, "KTc")}
        VA = {"lat": newbuf(AR, [SL // 128, 2, 65], BF16, "VA"), "ctx": newbuf(AR, [LC // 128, 2, 65], BF16, "VAc")}
        WIN = newbuf(AR, [8, NW], BF16, "WIN"); WOUT = newbuf(AR, [8, D], BF16, "WOUT")
        mark = AR.off
        for bufx, src in ((badaT, badaT_d[l]), (n1T, n1T_d[l]), (n2T, n2T_d[l]), (glag, glag_d[l]), (qg, qg_d[l]),
                          (kg, kg_d[l]), (esink, sink_d[l]), (pscT, pscT_d[l]), (cw, cw_d[l]), (cb, cb_d[l])):
            dma("sp", bufx.ap, src, [], [bufx], "c_small")
        dma("pool", pw.ap, pw_d[l], [], [pw], "c_pw")
        act(esink.ap, esink.ap, AF.Exp, [esink], [esink])
        if STOP_AT <= 0.1:
            break
        memset("pool", WIN.ap, 0.0, [WIN])
        wv = win_d[l].rearrange("(k p) n -> p k n", p=128)

        def stage(dst0, src0, n):
            dma("pool", WIN.ap[:, :, dst0:dst0 + n], wv[:, :, src0:src0 + n], [], [WIN], "w_in")
        for h in range(4):
            stage(C_Q + 64 * h, 48 * h, 48)
            stage(C_K + 64 * h, 192 + 48 * h, 48)
        stage(C_Z, 1152, 16); stage(C_Z + 32, 1168, 16)
        stage(C_PU, 1824, 256)
        stage(C_GV, 384, 384); stage(C_GG, 768, 384)
        for i, h in enumerate((0, 3, 1, 4, 2, 5)):
            stage(C_AQ + 64 * i, 1184 + 64 * h, 64)
        stage(C_AK, 1568, 256)
        dma("pool", WOUT.ap, wout_d[l].rearrange("(k p) n -> p k n", p=128), [], [WOUT], "w_out")
        if STOP_AT <= 0.2:
            break
        crep = newbuf(AR, [8, 2, 128], F32, "crep")
        brow = newbuf(AR, [6 * D], F32, "brow")
        dma("sp", brow.ap[0:1, :], badar_d[l], [], [brow], "c_brow")
        for k in range(8):
            for w in range(2):
                cp("dve", crep.ap[:, k, w, :], cact.ap[:, k, w:w + 1].to_broadcast([128, 128]), [cact], [crep])
        wst = ring(AR, 2, [8, 512], F32, "wst")
        for cbk in range(12):
            if cbk == 6 and os.environ.get('MK_NOBAR', '') == '':
                S.barrier()
            wb = wst.next()
            dma("sp", wb.ap, wada_d[l][:, cbk * 512:(cbk + 1) * 512].rearrange("(k p) n -> p k n", p=128), [], [wb], f"wst{wst.i}")
            which, half = cbk // 2, cbk % 2
            if os.environ.get('MK_SKIP', '') == 'C' and cbk >= 6:
                continue
            if os.environ.get('MK_SKIP', '') == 'D' and cbk < 6:
                continue
            if os.environ.get('MK_SKIP', '') == 'A' and which in (2, 5):
                continue
            if os.environ.get('MK_SKIP', '') == 'B' and which not in (2, 5):
                continue
            if which in (2, 5):
                GB = G1B if which == 2 else G2B
                for w in range(2):
                    pb = PSR.next()
                    for k in range(8):
                        mm(pb.ap, crep.ap[:, k, w, :], wb.ap[:, k, :], [crep, wb], [pb], start=(k == 0), stop=False)
                    mm(pb.ap, onesr.ap[0:1, :], brow.ap[0:1, cbk * 512:(cbk + 1) * 512], [onesr, brow], [pb], start=False, stop=True)
                    cp("act", GB.ap[:, w, half * 512:(half + 1) * 512], pb.ap, [pb], [GB])
            else:
                pb = PSR.next()
                for j in range(4):
                    for k in range(8):
                        mm(pb.ap[:, j * 2:j * 2 + 2], wb.ap[:, k, j * 128:(j + 1) * 128], cact.ap[:, k, :], [wb, cact], [pb],
                           start=(k == 0), stop=(k == 7))
                j0 = cbk * 4
                tt("dve", MODT.ap[:, j0:j0 + 4, :], pb.ap[:, 0:8].rearrange("p (j w) -> p j w", j=4),
                   badaT.ap[:, j0:j0 + 4, None].to_broadcast([128, 4, 2]), ALU.add, [pb, badaT], [MODT])
        ts("dve", G1T.ap, MODT.ap[:, 8:16, :], 1.0, None, ALU.add, None, [MODT], [G1T])
        tt("dve", G1T.ap, G1T.ap, n1T.ap[:, :, None].to_broadcast([128, 8, 2]), ALU.mult, [G1T, n1T], [G1T])
        ts("dve", G2T.ap, MODT.ap[:, 32:40, :], 1.0, None, ALU.add, None, [MODT], [G2T])
        tt("dve", G2T.ap, G2T.ap, n2T.ap[:, :, None].to_broadcast([128, 8, 2]), ALU.mult, [G2T, n2T], [G2T])
        SH1T = MODT.ap[:, 0:8, :]
        SH2T = MODT.ap[:, 24:32, :]

        S.barrier()
        AR.reset(mark)
        xs_r = ring(AR, 2, [D], F32, "xs"); xn = newbuf(AR, [D]); junk = newbuf(AR, [D], BF16); hT_r = ring(AR, 2, [8, 128], BF16, "hT")
        PT = newbuf(AR, [5, 6, 128], BF16); rp_r = ring(AR, 4, [128], F32, "rp")
        qn = newbuf(AR, [384]); qsq = newbuf(AR, [384]); qr = newbuf(AR, [384]); qT = newbuf(AR, [3, 128], BF16); oatt = newbuf(AR, [384])
        kn = B(qn.ap[:, 0:128], qn.t); ksq = B(qsq.ap[:, 0:128], qsq.t); kr = B(qr.ap[:, 0:128], qr.t); ob_sq = qsq
        puT = newbuf(AR, [2, 128]); puw = newbuf(AR, [2, 144]); s2 = newbuf(AR, [2, 144]); s4 = newbuf(AR, [2, 144]); s8 = newbuf(AR, [2, 144])
        s16 = newbuf(AR, [2, 144]); tmp8 = newbuf(AR, [8])
        ytmp = newbuf(AR, [D]); xm_r = ring(AR, 2, [D], F32, "xm")
        qk_sb = newbuf(AR, [512]); zpu_sb = newbuf(AR, [320])
        WSETS = []
        for _ws in range(2):
            w_ = {}
            w_["zT"] = newbuf(AR, [128]); w_["e1"] = newbuf(AR, [256]); w_["spb"] = newbuf(AR, [256]); w_["epos"] = newbuf(AR, [256]); w_["eneg"] = newbuf(AR, [256])
            w_["qdT"] = newbuf(AR, [256], BF16); w_["kiT"] = newbuf(AR, [256], BF16); w_["keT"] = newbuf(AR, [256]); w_["ke"] = newbuf(AR, [256], BF16)
            w_["scT"] = newbuf(AR, [4, 128], BF16); w_["vbf"] = newbuf(AR, [384], BF16); w_["obl"] = newbuf(AR, [384]); w_["ob_o"] = newbuf(AR, [384])
            w_["sgb"] = newbuf(AR, [384]); w_["dT"] = newbuf(AR, [2, 128], BF16); w_["mixT"] = newbuf(AR, [8, 128], BF16)
            w_["obst"] = w_["ob_o"]
            WSETS.append(w_)
        zT = e1 = spb = epos = eneg = qdT = kiT = keT = ke = scT = vbf = obl = ob_o = sgb = dT = mixT = obst = None

        def select(slot):
            nonlocal zT, e1, spb, epos, eneg, qdT, kiT, keT, ke, scT, vbf, obl, ob_o, sgb, dT, mixT, obst
            w_ = WSETS[slot % 2]
            zT, e1, spb, epos, eneg = w_["zT"], w_["e1"], w_["spb"], w_["epos"], w_["eneg"]
            qdT, kiT, keT, ke, scT, vbf = w_["qdT"], w_["kiT"], w_["keT"], w_["ke"], w_["scT"], w_["vbf"]
            obl, ob_o, sgb, dT, mixT, obst = w_["obl"], w_["ob_o"], w_["sgb"], w_["dT"], w_["mixT"], w_["obst"]
        select(0)
        print("ARENA mixer end", AR.off, "of", AR.size)
        wdec = newbuf(AR, [2, 256]); bdec = newbuf(AR, [2, 256])
        dma("sp", wdec.ap[0:48, :, :], wdec_d[l], [], [wdec], "c_wdec")
        dma("sp", bdec.ap[0:1, :, :], bdec_d[l], [], [bdec], "c_wdec")
        for sname in ("lat", "ctx"):
            memset("pool", VA[sname].ap[:, :, :, 64:65], 1.0, [VA[sname]])

        def front(src_ap, GT, SHT, w, dst_hT=None, col0=0, pre=None):
            xb = xs_r.next()
            if pre is not None:
                pre(xb)
            else:
                dma("sp", xb.ap, src_ap, [], [xb], f"xs{xs_r.i}")
            sm = small.next()
            memset("dve", sm.ap[:, 0:1], 0.0, [sm])
            act(junk.ap, xb.ap, AF.Square, [xb, sm], [junk, sm], accum_out=sm.ap[:, 0:1])
            rstd_from_ss(sm.ap[:, 0:1], D, sm)
            ts("dve", xn.ap, xb.ap, sm.ap[:, 0:1], None, ALU.mult, None, [xb, sm], [xn])
            if dst_hT is None:
                hT = hT_r.next()
            else:
                hT = dst_hT
            for half in range(2):
                pb = PSR.next()
                for j in range(4):
                    k = half * 4 + j
                    tp(pb.ap[:, j * 128:(j + 1) * 128], xn.ap[:, k * 128:(k + 1) * 128], [xn], [pb])
                for j in range(4):
                    k = half * 4 + j
                    act(hT.ap[:, k, col0:col0 + 128], pb.ap[:, j * 128:(j + 1) * 128], AF.Identity, [pb, GT, MODT], [hT],
                        scale=GT.ap[:, k, w:w + 1], bias=SHT[:, k, w:w + 1])
            return xb, hT

        def projF(hT, col0, M, out, pb):
            for k in range(8):
                mm(out, WIN.ap[:, k, col0:col0 + M], hT.ap[:, k, :], [WIN, hT], [pb], start=(k == 0), stop=(k == 7))

        def projT(hT, col0, N, out, pb):
            for k in range(8):
                mm(out, hT.ap[:, k, :], WIN.ap[:, k, col0:col0 + N], [WIN, hT], [pb], start=(k == 0), stop=(k == 7))

        def qk_proj(hT, do_out):
            pqk = PSR.next()
            if do_out:
                projT(hT, C_Q, 512, pqk.ap[:, 0:512], pqk)
                cp("act", qk_sb.ap, pqk.ap, [pqk], [qk_sb])
            else:
                projT(hT, C_K, 256, pqk.ap[:, 256:512], pqk)
                cp("act", qk_sb.ap[:, 256:512], pqk.ap[:, 256:512], [pqk], [qk_sb])
            pq = PSR.next()
            for c in range(4):
                if c < 2 and not do_out:
                    continue
                tp(pq.ap[:, c * 128:(c + 1) * 128], qk_sb.ap[:, c * 128:(c + 1) * 128], [qk_sb], [pq])
            return pq

        def rope(src, dst, tmp, H, rp):
            tt("dve", tmp.ap.rearrange("p (h d) -> p h d", h=H), src.ap.rearrange("p (h d) -> p h d", h=H),
               rp.ap[:, None, 0:64].to_broadcast([128, H, 64]), ALU.mult, [src, rp], [tmp])
            s5 = src.ap.rearrange("p (h a f c) -> p h a f c", h=H, a=2, f=2)
            d5 = dst.ap.rearrange("p (h a f c) -> p h a f c", h=H, a=2, f=2)
            sneg = rp.ap[:, 64:96].rearrange("p (a c) -> p a c", a=2)[:, None, :, :].to_broadcast([128, H, 2, 16])
            spos = rp.ap[:, 96:128].rearrange("p (a c) -> p a c", a=2)[:, None, :, :].to_broadcast([128, H, 2, 16])
            tt("dve", d5[:, :, :, 0, :], s5[:, :, :, 1, :], sneg, ALU.mult, [src, rp], [dst])
            tt("dve", d5[:, :, :, 1, :], s5[:, :, :, 0, :], spos, ALU.mult, [src, rp], [dst])
            tt("dve", dst.ap, dst.ap, tmp.ap, ALU.add, [dst, tmp], [dst])

        def headnorm(buf, sq, H, dh, gbuf):
            sm = small.next()
            tt("dve", sq.ap, buf.ap, buf.ap, ALU.mult, [buf], [sq])
            red("dve", sm.ap[:, 0:H], sq.ap.rearrange("p (h d) -> p h d", h=H), [sq], [sm])
            rstd_from_ss(sm.ap[:, 0:H], dh, sm)
            tt("dve", buf.ap.rearrange("p (h d) -> p h d", h=H), buf.ap.rearrange("p (h d) -> p h d", h=H),
               sm.ap[:, 0:H, None].to_broadcast([128, H, dh]), ALU.mult, [buf, sm], [buf])
            tt("dve", buf.ap, buf.ap, gbuf.ap, ALU.mult, [buf, gbuf], [buf])

        def gla_block(d, pq, pz, do_out):
            di = 0 if d == "f" else 1
            zr = 0 if d == "f" else 32
            col = 127 if d == "f" else 0
            cp("act", zT.ap[0:48, :], pz.ap[0:48, 0:128], [pz], [zT])
            pl = PSR.next()
            mm(pl.ap[:, 0:256], zT.ap[0:48, :], wdec.ap[0:48, di, :], [zT, wdec], [pl], start=True, stop=False)
            mm(pl.ap[:, 0:256], onesr.ap[0:1, :], bdec.ap[0:1, di, :], [onesr, bdec], [pl], start=False, stop=True)
            act(e1.ap, pl.ap[:, 0:256], AF.Exp, [pl], [e1], scale=-1.0)
            ts("dve", e1.ap, e1.ap, 1.0, None, ALU.add, None, [e1], [e1])
            act(spb.ap, e1.ap, AF.Ln, [e1], [spb])
            if STOP_AT <= 1.41:
                return None
            pbT = PSR.next()
            for pr in range(2):
                mm(pbT.ap[:, pr * 128:(pr + 1) * 128], spb.ap[:, pr * 128:(pr + 1) * 128], tri.ap[:, di, :], [spb, tri], [pbT])
            act(epos.ap, pbT.ap[:, 0:256], AF.Exp, [pbT], [epos])
            act(eneg.ap, pbT.ap[:, 0:256], AF.Exp, [pbT], [eneg], scale=-1.0)
            if do_out:
                stt("dve", qdT.ap, pq.ap[:, 0:256], DK ** -0.5, epos.ap, ALU.mult, ALU.mult, [pq, epos], [qdT])
                tt("dve", kiT.ap, pq.ap[:, 256:512], eneg.ap, ALU.mult, [pq, eneg], [kiT])
            for pr in range(2):
                stt("dve", keT.ap[:, pr * 128:(pr + 1) * 128], eneg.ap[:, pr * 128:(pr + 1) * 128],
                    epos.ap[:, pr * 128 + col:pr * 128 + col + 1], pq.ap[:, 256 + pr * 128:384 + pr * 128], ALU.mult, ALU.mult,
                    [eneg, epos, pq], [keT])
            pke = PSR.next()
            for pr in range(2):
                tp(pke.ap[:, pr * 128:(pr + 1) * 128], keT.ap[:, pr * 128:(pr + 1) * 128], [keT], [pke])
            cp("act", ke.ap, pke.ap[:, 0:256], [pke], [ke])
            po = None
            if STOP_AT <= 1.42:
                return None
            if do_out:
                psc = [PSR.next(), PSR.next()]
                for h in range(4):
                    pr, par, base = h // 2, h % 2, 64 * (h % 2)
                    mm(psc[par].ap[:, pr * 128:(pr + 1) * 128], kiT.ap[base:base + 48, pr * 128:(pr + 1) * 128],
                       qdT.ap[base:base + 48, pr * 128:(pr + 1) * 128], [kiT, qdT], [psc[par]])
                for par in range(2):
                    tt("dve", scT.ap[:, 2 * par:2 * par + 2, :], psc[par].ap[:, 0:256].rearrange("p (h i) -> p h i", h=2),
                       maskb.ap[:, di:di + 1, :].to_broadcast([128, 2, 128]), ALU.mult, [psc[par], maskb], [scT])
                if STOP_AT <= 1.425:
                    return None
                po = PSR.next()
                po2 = [PSR.next(), PSR.next()]
                for h in range(4):
                    pr, par, base = h // 2, h % 2, 64 * (h % 2)
                    mm(po.ap[:, h * 96:(h + 1) * 96], scT.ap[:, 2 * par + pr, :], vbf.ap[:, h * 96:(h + 1) * 96], [scT, vbf], [po], start=True, stop=True)
                for h in range(4):
                    pr, par, base = h // 2, h % 2, 64 * (h % 2)
                    mm(po2[par].ap[:, h * 96:(h + 1) * 96], qdT.ap[base:base + 48, pr * 128:(pr + 1) * 128], Sbf[d].ap[base:base + 48, pr, :],
                       [qdT, Sbf[d]], [po2[par]], start=True, stop=True)
                po = (po, po2[0], po2[1])
            if STOP_AT <= 1.43:
                return None
            pup = PSR.next()
            for pr in range(2):
                mm(pup.ap[:, pr * 192:(pr + 1) * 192], ke.ap[:, pr * 128:(pr + 1) * 128], vbf.ap[:, pr * 192:(pr + 1) * 192], [ke, vbf], [pup])
            for h in range(4):
                pr, base = h // 2, 64 * (h % 2)
                stt("dve", Sst[d].ap[base:base + 48, pr, :], Sst[d].ap[base:base + 48, pr, :],
                    epos.ap[base:base + 48, pr * 128 + col:pr * 128 + col + 1],
                    pup.ap[base:base + 48, pr * 192 + (h % 2) * 96:pr * 192 + (h % 2) * 96 + 96], ALU.mult, ALU.add,
                    [Sst[d], epos, pup], [Sst[d]])
            cp("pool", Sbf[d].ap, Sst[d].ap, [Sst[d]], [Sbf[d]])
            return po

        def pass_B(sq, do_out):
            name, nb, w = sq["name"], sq["nb"], sq["w"]
            lists = []
            for n in reversed(range(nb)):
                select(n)
                PSR.set(n)
                S.cur = []
                lists.append(S.cur)
                pass_B_block(sq, do_out, n)
                S.cur = None
            S.merge_threads(lists)

        def pass_B_block(sq, do_out, n):
            name, nb, w = sq["name"], sq["nb"], sq["w"]
            if True:
                if STOP_AT <= 1.1:
                    return
                xb, hT = front(x_src[name][n * 128:(n + 1) * 128, :], G1T, SH1T, w)
                pk = PSR.next()
                projT(hT, C_AK, 256, pk.ap[:, 0:256], pk)
                cp("act", kn.ap, pk.ap[:, 0:128], [pk], [kn])
                cp("act", VA[name].ap[:, n, :, 0:64], pk.ap[:, 128:256].rearrange("p (g d) -> p g d", g=2), [pk], [VA[name]])
                pv = PSR.next()
                projT(hT, C_GV, 384, pv.ap[:, 0:384], pv)
                cp("act", vbf.ap, pv.ap[:, 0:384], [pv], [vbf])
                headnorm(kn, ksq, 2, 64, kg)
                ksrc = kn
                if sq["rope"]:
                    rp = rp_r.next()
                    dma("sp", rp.ap, rope_d[n * 128:(n + 1) * 128, :], [], [rp], f"rp{rp_r.i}")
                    rope(kn, kr, ksq, 2, rp)
                    ksrc = kr
                pkt = PSR.next()
                tp(pkt.ap[:, 0:128], ksrc.ap, [ksrc], [pkt])
                cp("act", KT[name].ap[:, n * 128:(n + 1) * 128], pkt.ap[:, 0:128], [pkt], [KT[name]])
                nz = 320 if do_out else 64
                pzt = PSR.next()
                projT(hT, C_Z, nz, pzt.ap[:, 0:nz], pzt)
                cp("act", zpu_sb.ap[:, 0:nz], pzt.ap[:, 0:nz], [pzt], [zpu_sb])
                pz = PSR.next()
                tp(pz.ap[0:64, 0:128], zpu_sb.ap[:, 0:64], [zpu_sb], [pz])
                if do_out:
                    tp(pz.ap[:, 128:256], zpu_sb.ap[:, 64:192], [zpu_sb], [pz])
                    tp(pz.ap[:, 256:384], zpu_sb.ap[:, 192:320], [zpu_sb], [pz])
                    cp("act", puT.ap, pz.ap[:, 128:384].rearrange("p (c t) -> p c t", c=2), [pz], [puT])
                    dma("sp", pu_d[name][:, :, 8 + n * 128:8 + (n + 1) * 128], puT.ap, [puT], [], "st_pu")
                pq = qk_proj(hT, do_out)
                po = gla_block("b", pq, pz, do_out)
                if STOP_AT <= 1.5:
                    return
                if do_out:
                    cp("act", obst.ap, po[0].ap[:, 0:384], [po[0]], [obst])
                    for par in range(2):
                        ov = obst.ap.rearrange("p (pr x) -> p pr x", pr=2)[:, :, par * 96:(par + 1) * 96]
                        pv2 = po[1 + par].ap[:, 0:384].rearrange("p (pr x) -> p pr x", pr=2)[:, :, par * 96:(par + 1) * 96]
                        tt("dve", ov, ov, pv2, ALU.add, [obst, po[1 + par]], [obst])
                    dma("sp", ob_d[name][n * 128:(n + 1) * 128, :], obst.ap, [obst], [], "st_ob")

        def pass_F(sq, do_out):
            name, nb, w = sq["name"], sq["nb"], sq["w"]
            lists = []
            for n in range(nb):
                select(n)
                PSR.set(n)
                S.cur = []
                lists.append(S.cur)
                pass_F_block(sq, do_out, n)
                S.cur = None
            S.merge_threads(lists)

        def pass_F_block(sq, do_out, n):
            name, nb, w = sq["name"], sq["nb"], sq["w"]
            if True:
                xb, hT = front(x_src[name][n * 128:(n + 1) * 128, :], G1T, SH1T, w)
                pv = PSR.next()
                projT(hT, C_GV, 384, pv.ap[:, 0:384], pv)
                cp("act", vbf.ap, pv.ap[:, 0:384], [pv], [vbf])
                if do_out:
                    pg = PSR.next()
                    projT(hT, C_GG, 384, pg.ap[:, 0:384], pg)
                    act(sgb.ap, pg.ap[:, 0:384], AF.Silu, [pg], [sgb])
                    pa = PSR.next()
                    projT(hT, C_AQ, 384, pa.ap[:, 0:384], pa)
                    cp("act", qn.ap, pa.ap[:, 0:384], [pa], [qn])
                pz = PSR.next()
                projF(hT, C_Z, 48, pz.ap[0:48, 0:128], pz)
                pq = qk_proj(hT, do_out)
                if not do_out:
                    gla_block("f", pq, pz, False)
                    return
                dma("sp", obl.ap, ob_d[name][n * 128:(n + 1) * 128, :], [], [obl], "ld_ob")
                po = gla_block("f", pq, pz, True)
                tt("dve", ob_o.ap, po[0].ap[:, 0:384], obl.ap, ALU.add, [po[0], obl], [ob_o])
                for par in range(2):
                    ov = ob_o.ap.rearrange("p (pr x) -> p pr x", pr=2)[:, :, par * 96:(par + 1) * 96]
                    pv2 = po[1 + par].ap[:, 0:384].rearrange("p (pr x) -> p pr x", pr=2)[:, :, par * 96:(par + 1) * 96]
                    tt("dve", ov, ov, pv2, ALU.add, [ob_o, po[1 + par]], [ob_o])
                headnorm(ob_o, ob_sq, 4, 96, glag)
                tt("dve", ob_o.ap, ob_o.ap, sgb.ap, ALU.mult, [ob_o, sgb], [ob_o])
                pmt = PSR.next()
                for c in range(3):
                    tp(pmt.ap[:, c * 128:(c + 1) * 128], ob_o.ap[:, c * 128:(c + 1) * 128], [ob_o], [pmt])
                cp("act", mixT.ap[:, 0:3, :], pmt.ap[:, 0:384].rearrange("p (c t) -> p c t", c=3), [pmt], [mixT])
                headnorm(qn, qsq, 6, 64, qg)
                qsrc = qn
                if sq["rope"]:
                    rp = rp_r.next()
                    dma("sp", rp.ap, rope_d[n * 128:(n + 1) * 128, :], [], [rp], f"rp{rp_r.i}")
                    rope(qn, qr, qsq, 6, rp)
                    qsrc = qr
                pqt = PSR.next()
                for c in range(3):
                    tp(pqt.ap[:, c * 128:(c + 1) * 128], qsrc.ap[:, c * 128:(c + 1) * 128], [qsrc], [pqt])
                cp("act", qT.ap, pqt.ap[:, 0:384].rearrange("p (c t) -> p c t", c=3), [pqt], [qT])
                kbs = []
                if name == "lat":
                    if n > 0:
                        kbs.append(("lat", n - 1, 1))
                    kbs.append(("lat", n, None))
                    if n < nb - 1:
                        kbs.append(("lat", n + 1, 0))
                for cbk in range(LC // 128):
                    kbs.append(("ctx", cbk, None))
                for ki, (ks, kn_, mk) in enumerate(kbs):
                    for g in range(2):
                        ps_ = PSR.next()
                        mm(ps_.ap[:, 0:384].rearrange("p (h i) -> p h i", h=3), KT[ks].ap[g * 64:(g + 1) * 64, kn_ * 128:(kn_ + 1) * 128],
                           qT.ap[g * 64:(g + 1) * 64, :, :], [KT[ks], qT], [ps_])
                        act(PT.ap[:, ki, 3 * g:3 * g + 3, :], ps_.ap[:, 0:384].rearrange("p (h i) -> p h i", h=3), AF.Exp, [ps_], [PT], scale=0.125)
                        if mk is not None:
                            tt("dve", PT.ap[:, ki, 3 * g:3 * g + 3, :], PT.ap[:, ki, 3 * g:3 * g + 3, :],
                               maskb.ap[:, mk:mk + 1, :].to_broadcast([128, 3, 128]), ALU.mult, [PT, maskb], [PT])
                pov = PSR.next()
                for h in range(6):
                    g = h // 3
                    for ki, (ks, kn_, mk) in enumerate(kbs):
                        mm(pov.ap[:, h * 65:(h + 1) * 65], PT.ap[:, ki, h, :], VA[ks].ap[:, kn_, g, :], [PT, VA[ks]], [pov],
                           start=(ki == 0), stop=(ki == len(kbs) - 1))
                sm = small.next()
                pov3 = pov.ap[:, 0:390].rearrange("p (h e) -> p h e", e=65)
                tt("dve", sm.ap[:, 0:6], pov3[:, :, 64], esink.ap, ALU.add, [pov, esink], [sm])
                recip(sm.ap[:, 0:6], sm.ap[:, 0:6], [sm], [sm])
                tt("dve", oatt.ap.rearrange("p (h d) -> p h d", h=6), pov3[:, :, 0:64], sm.ap[:, 0:6, None].to_broadcast([128, 6, 64]),
                   ALU.mult, [pov, sm], [oatt])
                pat = PSR.next()
                for c in range(3):
                    tp(pat.ap[:, c * 128:(c + 1) * 128], oatt.ap[:, c * 128:(c + 1) * 128], [oatt], [pat])
                cp("act", mixT.ap[:, 3:6, :], pat.ap[:, 0:384].rearrange("p (c t) -> p c t", c=3), [pat], [mixT])
                dma("sp", puw.ap, pu_d[name][:, :, n * 128:n * 128 + 144], [], [puw], "ld_pu")
                tt("pool", s2.ap[:, :, 1:144], puw.ap[:, :, 0:143], puw.ap[:, :, 1:144], ALU.add, [puw], [s2])
                tt("pool", s4.ap[:, :, 2:143], s2.ap[:, :, 1:142], s2.ap[:, :, 3:144], ALU.add, [s2], [s4])
                tt("pool", s8.ap[:, 1, 4:141], s4.ap[:, 1, 2:139], s4.ap[:, 1, 6:143], ALU.add, [s4], [s8])
                tt("pool", s16.ap[:, 1, 8:136], s8.ap[:, 1, 4:132], s8.ap[:, 1, 12:140], ALU.add, [s8], [s16])
                combos = ((0, 64, 0, s2), (64, 128, 0, s4), (0, 64, 1, s8), (64, 128, 1, s16))
                for (r0, r1, c, sb_) in combos:
                    stt("dve", dT.ap[r0:r1, c, :], sb_.ap[r0:r1, c, 8:136], pinv.ap[r0:r1, c:c + 1], puw.ap[r0:r1, c, 8:136],
                        ALU.mult, ALU.subtract, [sb_, pinv, puw], [dT])
                for edge, cols, dcols in ((0, slice(8, 16), slice(0, 8)), (1, slice(128, 136), slice(120, 128))):
                    if (edge == 0 and n == 0) or (edge == 1 and n == nb - 1):
                        for (r0, r1, c, sb_) in combos:
                            tt("pool", tmp8.ap[r0:r1, :], sb_.ap[r0:r1, c, cols], pedge.ap[r0:r1, edge, c, :], ALU.mult, [sb_, pedge], [tmp8])
                            tt("pool", dT.ap[r0:r1, c, dcols], tmp8.ap[r0:r1, :], puw.ap[r0:r1, c, cols], ALU.subtract, [tmp8, puw], [dT])
                pp = PSR.next()
                for c in range(2):
                    mm(pp.ap[:, c * 128:(c + 1) * 128], pw.ap[:, c, :], dT.ap[:, c, :], [pw, dT], [pp])
                for c in range(2):
                    act(mixT.ap[:, 6 + c, :], pp.ap[:, c * 128:(c + 1) * 128], AF.Copy, [pp, pscT], [mixT], scale=pscT.ap[:, c:c + 1])
                xm = xm_r.next()
                for half in range(2):
                    py = PSR.next()
                    for kc in range(8):
                        mm(py.ap, mixT.ap[:, kc, :], WOUT.ap[:, kc, half * 512:(half + 1) * 512], [mixT, WOUT], [py], start=(kc == 0), stop=(kc == 7))
                    hs = slice(half * 512, (half + 1) * 512)
                    tt("dve", ytmp.ap[:, hs], py.ap, G1B.ap[:, w, hs], ALU.mult, [py, G1B], [ytmp])
                    tt("pool", xm.ap[:, hs], ytmp.ap[:, hs], xb.ap[:, hs], ALU.add, [ytmp, xb], [xm])
                dma("sp", xmid_d[name][n * 128:(n + 1) * 128, :], xm.ap, [xm], [], f"st_xm{xm_r.i}")

        for d in "fb":
            memset("dve", Sst[d].ap, 0.0, [Sst[d]])
            memset("pool", Sbf[d].ap, 0.0, [Sbf[d]])
        ctx_out = not last
        if STOP_AT <= 1:
            break
        pass_B(seqs["ctx"], ctx_out)
        if STOP_AT <= 2:
            break
        pass_F(seqs["ctx"], ctx_out)
        if STOP_AT <= 3:
            break
        pass_B(seqs["lat"], True)
        if STOP_AT <= 4:
            break
        pass_F(seqs["lat"], True)
        if STOP_AT <= 5:
            break

        S.barrier()
        AR.reset(gbmark)
        WUP = newbuf(AR, [8, 2 * D_FF], BF16, "WUP"); WDN = newbuf(AR, [22, D], BF16, "WDN")
        wuv = wup_d[l].rearrange("(k p) n -> p k n", p=128)
        wdv = wdn_d[l].rearrange("(k p) n -> p k n", p=128)
        for k in range(8):
            for cc in range(11):
                dma("pool", WUP.ap[:, k, cc * 512:(cc + 1) * 512], wuv[:, k, cc * 512:(cc + 1) * 512], [], [WUP], "w_up")
        for k in range(22):
            for cc in range(2):
                dma("pool", WDN.ap[:, k, cc * 512:(cc + 1) * 512], wdv[:, k, cc * 512:(cc + 1) * 512], [], [WDN], "w_dn")
        xs_r = ring(AR, 2, [D], F32, "xs"); xn = newbuf(AR, [D]); junk = newbuf(AR, [D], BF16)
        h2T_r = [newbuf(AR, [8, 256], BF16, "h2Ta"), newbuf(AR, [8, 256], BF16, "h2Tb")]; actT = newbuf(AR, [22, 256], BF16, "actT")
        cva = ring(AR, 3, [256], F32, "cva"); cvb = ring(AR, 1, [256], F32, "cvb"); cga = ring(AR, 3, [256], F32, "cga")
        cgb = ring(AR, 1, [256], F32, "cgb"); csg = ring(AR, 2, [256], F32, "csg")
        xr_r = ring(AR, 1, [D], F32, "xr"); yt_r = ring(AR, 1, [D], F32, "yt")
        print("ARENA ffn end", AR.off, "of", AR.size)

        def ffn_pass(sq):
            name, n_tok, w = sq["name"], sq["n"], sq["w"]
            starts = sorted(set(min(FT * j, n_tok - FT) for j in range((n_tok + FT - 1) // FT)))
            tiles = []
            for ti, s in enumerate(starts):
                h2T = h2T_r[ti % 2]
                PSR.set(ti)
                T = []
                S.cur = T
                for bi in range(2):
                    r0 = s - 1 + 128 * bi
                    lo, hi = max(r0, 0), min(r0 + 128, n_tok)

                    def pre(xb, r0=r0, lo=lo, hi=hi):
                        if lo > r0:
                            memset("dve", xb.ap[0:32, :], 0.0, [xb])
                        if hi < r0 + 128:
                            memset("dve", xb.ap[96:128, :], 0.0, [xb])
                        dma("sp", xb.ap[lo - r0:hi - r0, :], xmid_d[name][lo:hi, :], [], [xb], f"xs{xs_r.i}")
                    front(None, G2T, SH2T, w, dst_hT=h2T, col0=bi * 128, pre=pre)
                if s == 0:
                    memset("dve", h2T.ap[:, :, 0:1], 0.0, [h2T])
                if s + FT == n_tok:
                    memset("dve", h2T.ap[:, :, 255:256], 0.0, [h2T])
                clists = []
                for c in range(22):
                    S.cur = []
                    clists.append(S.cur)
                    pus = (PSR.next(), PSR.next())
                    for pu_, col in ((pus[0], c * 128), (pus[1], D_FF + c * 128)):
                        for k in range(8):
                            mm(pu_.ap[:, 0:256], WUP.ap[:, k, col:col + 128], h2T.ap[:, k, :], [WUP, h2T], [pu_], start=(k == 0), stop=(k == 7))
                    res = []
                    for fc, pu_, ra, rb in ((c, pus[0], cva, cvb), (22 + c, pus[1], cga, cgb)):
                        t1, t2 = ra.next(), rb.next()
                        act(t1.ap[:, 0:FT], pu_.ap[:, 1:1 + FT], AF.Identity, [pu_, cw, cb], [t1], scale=cw.ap[:, fc, 1:2], bias=cb.ap[:, fc:fc + 1])
                        stt("dve", t2.ap[:, 0:FT], pu_.ap[:, 0:FT], cw.ap[:, fc, 0:1], t1.ap[:, 0:FT], ALU.mult, ALU.add, [pu_, cw, t1], [t2])
                        stt("dve", t1.ap[:, 0:FT], pu_.ap[:, 2:2 + FT], cw.ap[:, fc, 2:3], t2.ap[:, 0:FT], ALU.mult, ALU.add, [pu_, cw, t2], [t1])
                        res.append(t1)
                    sg_ = csg.next()
                    act(sg_.ap[:, 0:FT], res[1].ap[:, 0:FT], AF.Silu, [res[1]], [sg_])
                    tt("pool", actT.ap[:, c, 0:FT], sg_.ap[:, 0:FT], res[0].ap[:, 0:FT], ALU.mult, [sg_, res[0]], [actT])
                S.cur = None
                T.extend(S.merge_threads(clists, ret=True))
                S.cur = T
                for sub, m in ((0, 128), (1, FT - 128)):
                    t0 = s + sub * 128
                    xr = xr_r.next()
                    dma("sp", xr.ap[0:m, :], xmid_d[name][t0:t0 + m, :], [], [xr], f"xr{xr_r.i}")
                    yt = yt_r.next()
                    for half in range(2):
                        py = PSR.next()
                        for c in range(22):
                            mm(py.ap[0:m, :], actT.ap[:, c, sub * 128:sub * 128 + m], WDN.ap[:, c, half * 512:(half + 1) * 512], [actT, WDN], [py],
                               start=(c == 0), stop=(c == 21))
                        hs = slice(half * 512, (half + 1) * 512)
                        tt("dve", yt.ap[0:m, hs], py.ap[0:m, :], G2B.ap[0:m, w, hs], ALU.mult, [py, G2B], [yt])
                        tt("pool", yt.ap[0:m, hs], yt.ap[0:m, hs], xr.ap[0:m, hs], ALU.add, [yt, xr], [yt])
                    dma("sp", x_dst[name][t0:t0 + m, :], yt.ap[0:m, :], [yt], [], f"st_xo{yt_r.i}")
                S.cur = None
                tiles.append(T)
            S.merge_threads(tiles)

        if STOP_AT <= 6:
            break
        if not last:
            ffn_pass(seqs["ctx"])
        if STOP_AT <= 7:
            break
        ffn_pass(seqs["lat"])
        if STOP_AT <= 8:
            break

    S.barrier()
    S.emit(st)
    st.close()
    return nc, S


def host_consts(SL):
    ident = np.eye(128, dtype=np.float32)
    j = np.arange(128)[:, None]
    i = np.arange(128)[None, :]
    le = (j <= i).astype(np.float32)
    ge = (j >= i).astype(np.float32)
    tri = np.stack([le, ge], axis=1) * np.float32(-1.0 / 16.0)
    mask = np.stack([le, ge], axis=1)
    rows_n = SL // GRID_W
    rows = np.repeat(np.arange(rows_n), GRID_W).astype(np.float32)
    cols = np.tile(np.arange(GRID_W), rows_n).astype(np.float32)
    nf = 16
    inv = (np.float32(10000.0) ** (-np.arange(nf, dtype=np.float32) / np.float32(nf))).astype(np.float32)
    ang = np.concatenate([rows[:, None] * inv, cols[:, None] * inv], axis=-1).astype(np.float32)
    cos, sin = np.cos(ang).astype(np.float32), np.sin(ang).astype(np.float32)
    cr, cc, sr, sc = cos[:, :16], cos[:, 16:], sin[:, :16], sin[:, 16:]
    rope = np.concatenate([cr, cr, cc, cc, -sr, -sc, sr, sc], axis=-1).astype(np.float32)
    wins = (2, 4, 8, 16)
    pinv = np.zeros((128, 2), np.float32)
    pedge = np.zeros((128, 2, 2, 8), np.float32)
    T = 1 << 20
    for c in range(2):
        for gi in range(2):
            wd = wins[2 * c + gi]
            sl = slice(gi * 64, (gi + 1) * 64)
            pinv[sl, c] = 1.0 / wd
            for t in range(8):
                cnt_first = min(t + wd // 2, T) - max(t - wd // 2, 0)
                tl = T - 8 + t
                cnt_last = min(tl + wd // 2, T) - max(tl - wd // 2, 0)
                pedge[sl, 0, c, t] = 1.0 / cnt_first
                pedge[sl, 1, c, t] = 1.0 / cnt_last
    return dict(ident=ident, tri=np.ascontiguousarray(tri), mask=np.ascontiguousarray(mask), rope=rope, pinv=pinv, pedge=pedge)


def host_weights(inp):
    f = lambda a: np.ascontiguousarray(np.asarray(a, dtype=np.float32))
    L = DEPTH
    out = {}
    out["w_ada"] = f(inp["w_ada"])
    out["b_adaT"] = f(np.asarray(inp["b_ada"]).reshape(L, 48, 128).transpose(0, 2, 1))
    out["b_adar"] = f(np.asarray(inp["b_ada"]).reshape(L, 1, 6 * D))
    out["n1T"] = f(np.asarray(inp["norm1_g"]).reshape(L, 8, 128).transpose(0, 2, 1))
    out["n2T"] = f(np.asarray(inp["norm2_g"]).reshape(L, 8, 128).transpose(0, 2, 1))
    out["w_in"] = f(inp["w_in"])
    wd = np.asarray(inp["gla_w_dec"])
    bd = np.asarray(inp["gla_b_dec"])
    wdec = np.zeros((L, 48, 2, 256), np.float32)
    bdec = np.zeros((L, 1, 2, 256), np.float32)
    for d in range(2):
        for h in range(4):
            wdec[:, 32 * d:32 * d + 16, d, 64 * h:64 * h + 48] = wd[:, d, :, 48 * h:48 * h + 48]
            bdec[:, 0, d, 64 * h:64 * h + 48] = bd[:, d, 48 * h:48 * h + 48]
    out["wdec"], out["bdec"] = wdec, bdec
    out["glag"] = f(np.broadcast_to(np.tile(np.asarray(inp["gla_norm_g"]), (1, 4))[:, None, :], (L, 128, 384)))
    out["qg"] = f(np.broadcast_to(np.tile(np.asarray(inp["q_norm_g"]), (1, 6))[:, None, :], (L, 128, 384)))
    out["kg"] = f(np.broadcast_to(np.tile(np.asarray(inp["k_norm_g"]), (1, 2))[:, None, :], (L, 128, 128)))
    out["sink"] = f(np.broadcast_to(np.asarray(inp["sink_logit"])[:, None, :], (L, 128, 6)))
    pwi = np.asarray(inp["pool_w"])
    pw = np.zeros((L, 128, 2, 128), np.float32)
    for c in range(2):
        for gi in range(2):
            pw[:, gi * 64:(gi + 1) * 64, c, gi * 64:(gi + 1) * 64] = pwi[:, 2 * c + gi]
    out["pw"] = pw
    out["pscT"] = f(np.asarray(inp["pool_scale"]).reshape(L, 2, 128).transpose(0, 2, 1))
    out["w_out"] = f(inp["w_out"])
    out["w_up"] = f(inp["w_up"])
    out["cw"] = f(np.asarray(inp["conv_w"]).reshape(L, 3, 44, 128).transpose(0, 3, 2, 1))
    out["cb"] = f(np.asarray(inp["conv_b"]).reshape(L, 44, 128).transpose(0, 2, 1))
    out["w_down"] = f(inp["w_down"])
    return out


_CACHE = {}


def kernel(**inputs):
    x = np.asarray(inputs["x"], dtype=np.float32)
    ctx = np.asarray(inputs["ctx"], dtype=np.float32)
    c = np.asarray(inputs["c"], dtype=np.float32)
    c_ctx = np.asarray(inputs["c_ctx"], dtype=np.float32)
    Bn, SL, _ = x.shape
    LC = ctx.shape[1]
    key = (SL, LC)
    if key not in _CACHE:
        _CACHE[key] = build(SL, LC)
    nc, _ = _CACHE[key]
    shared = dict(host_consts(SL))
    shared.update(host_weights(inputs))
    in_maps = []
    for b in range(Bn):
        m = dict(shared)
        m["x"] = np.ascontiguousarray(x[b])
        m["ctx"] = np.ascontiguousarray(ctx[b])
        cT = np.stack([c[b].reshape(8, 128).T, c_ctx.reshape(8, 128).T], axis=-1)
        m["cT"] = np.ascontiguousarray(cT.astype(np.float32))
        in_maps.append(m)
    res = run_bass_kernel_spmd(nc, in_maps, core_ids=list(range(Bn)))
    return np.stack([np.asarray(r["out"], dtype=np.float32) for r in res.results], axis=0)
```

```python
import numpy as np
from contextlib import ExitStack
import concourse.bass as bass
import concourse.mybir as mybir
from concourse.bass_utils import run_bass_kernel_spmd

F32 = mybir.dt.float32
BF16 = mybir.dt.bfloat16
AF = mybir.ActivationFunctionType
ALU = mybir.AluOpType
AX = mybir.AxisListType

D = 1024
DEPTH = 2
GRID_W = 64
NH_G, DK, DV = 4, 48, 96
D_FF = 2816
EPS = 1e-6
IN_W = 2080
C_Q, C_K, C_Z, C_PU, C_GV, C_GG, C_AQ, C_AK = 0, 256, 512, 576, 832, 1216, 1600, 1984
NW = 2240
FT = 254

SAME_ENGINE_SYNC = True
import os
STOP_AT = float(os.environ.get('MK_STOP', '99'))


class Tok:
    __slots__ = ("name", "w", "r", "excl")

    def __init__(self, name):
        self.name = name
        self.w = []
        self.r = {}
        self.excl = False


class Op:
    __slots__ = ("eng", "fn", "deps", "odeps", "sig", "dma", "key", "ev", "tag")

    def __init__(self, eng, fn, dma, key):
        self.eng, self.fn, self.dma, self.key = eng, fn, dma, key
        self.deps = []
        self.odeps = []
        self.sig = False
        self.ev = None


class Sched:
    ENGS = ("pe", "act", "dve", "pool", "sp")

    def __init__(self, nc):
        self.nc = nc
        self.ops = []
        self.cur = None
        self.log = []
        self.ntok = 0

    def tok(self, name=None):
        self.ntok += 1
        return Tok(name or f"t{self.ntok}")

    def op(self, eng, fn, reads=(), writes=(), dma=False, key=None):
        o = Op(eng, fn, dma, key)
        if os.environ.get('MK_DEBUG'):
            import inspect
            o.tag = inspect.stack()[2].lineno
        deps = {}
        for t in reads:
            for w in t.w:
                deps[id(w)] = w
            if t.excl:
                for re_, rd in t.r.items():
                    if re_ != eng and not isinstance(rd, list):
                        deps[id(rd)] = rd
        for t in writes:
            samekey = dma and t.w and all(w.dma and w.key == key for w in t.w) and not t.r
            if not samekey:
                for w in t.w:
                    deps[id(w)] = w
                for rd in t.r.values():
                    if isinstance(rd, list):
                        for x in rd:
                            deps[id(x)] = x
                    else:
                        deps[id(rd)] = rd
        for d in deps.values():
            if (not d.dma) and (not dma) and d.eng == eng:
                if eng == "pe" or not SAME_ENGINE_SYNC:
                    o.odeps.append(d)
                    continue
            o.deps.append(d)
            d.sig = True
        for t in reads:
            if dma:
                t.r.setdefault("dma", []).append(o)
            else:
                t.r[eng] = o
        for t in writes:
            samekey = dma and t.w and all(w.dma and w.key == key for w in t.w) and not t.r
            if samekey:
                t.w.append(o)
            else:
                t.w = [o]
                t.r = {}
        if dma:
            o.sig = True
        (self.cur if self.cur is not None else self.ops).append(o)
        return o

    def merge_threads(self, lists, ratio=1, ret=False):
        unem = set()
        for L in lists:
            for o in L:
                unem.add(id(o))
        out = []
        k = len(lists)
        pos = [0] * k
        i = 0
        while i < k:
            L = lists[i]
            if pos[i] >= len(L):
                i += 1
                continue
            o = L[pos[i]]
            pos[i] += 1
            out.append(o)
            unem.discard(id(o))
            j = i + 1
            if j < k:
                Lj = lists[j]
                for _ in range(ratio):
                    if pos[j] < len(Lj):
                        o2 = Lj[pos[j]]
                        if all(id(d) not in unem for d in o2.deps) and all(id(d) not in unem for d in o2.odeps):
                            pos[j] += 1
                            out.append(o2)
                            unem.discard(id(o2))
        if ret:
            return out
        self.ops.extend(out)

    def barrier(self):
        last, dmas = {}, {}
        for o in self.ops:
            if o.fn is None:
                continue
            if o.dma:
                dmas[o.key] = o
            else:
                last[o.eng] = o
        for e in self.ENGS:
            b = Op(e, None, False, None)
            for d in list(last.values()) + list(dmas.values()):
                if (not d.dma) and d.eng == e:
                    continue
                b.deps.append(d)
                d.sig = True
            self.ops.append(b)

    def emit(self, stack):
        nc = self.nc
        cnt = {e: 0 for e in self.ENGS}
        keycnt = {}
        for o in self.ops:
            if o.fn is None:
                continue
            if o.dma:
                keycnt[o.key] = keycnt.get(o.key, 0) + 16
                o.ev = (("dma", o.key), keycnt[o.key])
            elif o.sig:
                cnt[o.eng] += 1
                o.ev = (("eng", o.eng), cnt[o.eng])
        sems = {}
        for e in self.ENGS:
            if cnt[e]:
                sems[("eng", e)] = stack.enter_context(nc.semaphore(f"c_{e}"))
        for k in keycnt:
            sems[("dma", k)] = stack.enter_context(nc.semaphore(f"d_{k}"))
        per = {e: [] for e in self.ENGS}
        for o in self.ops:
            per[o.eng].append(o)
        self.stats = {e: len(per[e]) for e in self.ENGS}
        self.stats["sems"] = len(sems)
        self.stats["maxcnt"] = dict(cnt)
        block = stack.enter_context(nc.Block())

        def run(engobj, lst):
            waited = {}
            for o in lst:
                need = {}
                for d in o.deps:
                    if d.ev is None:
                        continue
                    s, v = d.ev
                    if waited.get(s, 0) >= v:
                        continue
                    if need.get(s, 0) < v:
                        need[s] = v
                for s, v in need.items():
                    engobj.wait_ge(sems[s], v)
                    waited[s] = v
                    if os.environ.get('MK_DEBUG'):
                        self.log.append(f"{o.eng} WAIT {s} >= {v}")
                if os.environ.get('MK_DEBUG') and o.fn is not None:
                    self.log.append(f"{o.eng} OP line{getattr(o, 'tag', 0)} ev={o.ev}")
                if o.fn is None:
                    continue
                ins = o.fn(engobj)
                if o.ev is not None:
                    ins.then_inc(sems[o.ev[0]], 16 if o.dma else 1)

        @block.tensor
        def _(e):
            run(e, per["pe"])

        @block.scalar
        def _(e):
            run(e, per["act"])

        @block.vector
        def _(e):
            run(e, per["dve"])

        @block.gpsimd
        def _(e):
            run(e, per["pool"])

        @block.sync
        def _(e):
            run(e, per["sp"])


class Arena:
    def __init__(self, ap):
        self.ap = ap
        self.size = ap.shape[1]
        self.off = 0

    def reset(self, to=0):
        self.off = to

    def alloc(self, shape, dtype=F32):
        n = int(np.prod(shape))
        bpe = 4 if dtype == F32 else 2
        words = (n * bpe + 3) // 4
        words = (words + 7) // 8 * 8
        assert self.off + words <= self.size, f"arena overflow {self.off}+{words}>{self.size}"
        v = self.ap[:, self.off:self.off + words]
        self.off += words
        if dtype != F32:
            v = v.bitcast(dtype)
        v = v[:, 0:n]
        if len(shape) == 1:
            return v
        names = "abcd"[:len(shape)]
        pat = "p (" + " ".join(names) + ") -> p " + " ".join(names)
        return v.rearrange(pat, **{names[i]: shape[i] for i in range(len(shape))})


class B:
    __slots__ = ("ap", "t")

    def __init__(self, ap, t):
        self.ap, self.t = ap, t


class Ring:
    def __init__(self, bufs):
        self.bufs = bufs
        self.i = -1

    def next(self):
        self.i = (self.i + 1) % len(self.bufs)
        return self.bufs[self.i]


def build(SL, LC):
    nc = bass.Bass("TRN2", target_bir_lowering=False)

    def dten(name, shape, kind="ExternalInput"):
        return nc.dram_tensor(name, list(shape), F32, kind=kind).ap()

    x_d = dten("x", [SL, D]); ctx_d = dten("ctx", [LC, D]); cT_d = dten("cT", [128, 8, 2])
    ident_d = dten("ident", [128, 128]); tri_d = dten("tri", [128, 2, 128]); mask_d = dten("mask", [128, 2, 128])
    rope_d = dten("rope", [SL, 128]); pinv_d = dten("pinv", [128, 2]); pedge_d = dten("pedge", [128, 2, 2, 8])
    wada_d = dten("w_ada", [DEPTH, D, 6 * D]); badaT_d = dten("b_adaT", [DEPTH, 128, 48]); badar_d = dten("b_adar", [DEPTH, 1, 6 * D])
    n1T_d = dten("n1T", [DEPTH, 128, 8]); n2T_d = dten("n2T", [DEPTH, 128, 8])
    win_d = dten("w_in", [DEPTH, D, IN_W]); wdec_d = dten("wdec", [DEPTH, 48, 2, 256]); bdec_d = dten("bdec", [DEPTH, 1, 2, 256])
    glag_d = dten("glag", [DEPTH, 128, 384]); qg_d = dten("qg", [DEPTH, 128, 384]); kg_d = dten("kg", [DEPTH, 128, 128])
    sink_d = dten("sink", [DEPTH, 128, 6]); pw_d = dten("pw", [DEPTH, 128, 2, 128]); pscT_d = dten("pscT", [DEPTH, 128, 2])
    wout_d = dten("w_out", [DEPTH, D, D]); wup_d = dten("w_up", [DEPTH, D, 2 * D_FF]); cw_d = dten("cw", [DEPTH, 128, 44, 3])
    cb_d = dten("cb", [DEPTH, 128, 44]); wdn_d = dten("w_down", [DEPTH, D_FF, D])
    out_d = dten("out", [SL, D], kind="ExternalOutput")
    ob_d = {"lat": dten("ob_lat", [SL, 384], "Internal"), "ctx": dten("ob_ctx", [LC, 384], "Internal")}
    xmid_d = {"lat": dten("xmid_lat", [SL, D], "Internal"), "ctx": dten("xmid_ctx", [LC, D], "Internal")}
    x1_d = {"lat": dten("x1_lat", [SL, D], "Internal"), "ctx": dten("x1_ctx", [LC, D], "Internal")}
    pu_d = {"lat": dten("pu_lat", [128, 2, SL + 16], "Internal"), "ctx": dten("pu_ctx", [128, 2, LC + 16], "Internal")}

    st = ExitStack()
    S = Sched(nc)
    PERS_W, ARENA_W = 3904, 49280
    pers_t = st.enter_context(nc.sbuf_tensor("pers", [128, PERS_W], F32))
    arena_t = st.enter_context(nc.sbuf_tensor("arena", [128, ARENA_W], F32))
    PERS = Arena(pers_t[:, :]); AR = Arena(arena_t[:, :])
    psb = [st.enter_context(nc.psum_tensor(f"ps{i}", [128, 512], F32)) for i in range(8)]
    class PRing:
        def __init__(self, bufs):
            self.bufs = bufs
            self.par = 0
            self.i = [-1, -1]

        def set(self, par):
            self.par = par % 2

        def next(self):
            p = self.par
            self.i[p] = (self.i[p] + 1) % 4
            return self.bufs[4 * p + self.i[p]]

    PSR = PRing([B(psb[i][:, :], S.tok(f"ps{i}")) for i in range(8)])
    for b_ in PSR.bufs:
        b_.t.excl = True

    def newbuf(arena, shape, dtype=F32, name=None):
        return B(arena.alloc(shape, dtype), S.tok(name))

    def ring(arena, n, shape, dtype=F32, name="r"):
        return Ring([newbuf(arena, shape, dtype, f"{name}{i}") for i in range(n)])

    def toks(bs):
        return [b.t if isinstance(b, B) else b for b in bs]

    def mm(out, lhsT, rhs, R, W, start=True, stop=True):
        S.op("pe", lambda e: e.matmul(out, lhsT=lhsT, rhs=rhs, start=start, stop=stop), toks(R), toks(W))

    def tp(out, in_, R, W):
        idn = ident_bf if in_.dtype == BF16 else ident
        S.op("pe", lambda e: e.transpose(out, in_, idn.ap), toks(R) + [idn.t], toks(W))

    def bfv(pb):
        return pb.ap.bitcast(BF16)

    def act(out, in_, func, R, W, **kw):
        S.op("act", lambda e: e.activation(out=out, in_=in_, func=func, **kw), toks(R), toks(W))

    def tt(eng, out, in0, in1, op, R, W):
        S.op(eng, lambda e: e.tensor_tensor(out=out, in0=in0, in1=in1, op=op), toks(R), toks(W))

    def ts(eng, out, in0, s1, s2, op0, op1, R, W):
        if s2 is None:
            S.op(eng, lambda e: e.tensor_scalar(out=out, in0=in0, scalar1=s1, scalar2=None, op0=op0), toks(R), toks(W))
        else:
            S.op(eng, lambda e: e.tensor_scalar(out=out, in0=in0, scalar1=s1, scalar2=s2, op0=op0, op1=op1), toks(R), toks(W))

    def stt(eng, out, in0, scalar, in1, op0, op1, R, W):
        S.op(eng, lambda e: e.scalar_tensor_tensor(out=out, in0=in0, scalar=scalar, in1=in1, op0=op0, op1=op1), toks(R), toks(W))

    def cp(eng, out, in_, R, W):
        if eng == "act":
            act(out, in_, AF.Copy, R, W)
        else:
            S.op(eng, lambda e: e.tensor_copy(out=out, in_=in_), toks(R), toks(W))

    def memset(eng, ap, val, W):
        S.op(eng, lambda e: e.memset(ap, val), [], toks(W))

    def red(eng, out, in_, R, W):
        S.op(eng, lambda e: e.tensor_reduce(out=out, in_=in_, axis=AX.X, op=ALU.add), toks(R), toks(W))

    def recip(out, in_, R, W):
        S.op("dve", lambda e: e.reciprocal(out=out, in_=in_), toks(R), toks(W))

    def dma(eng, out, in_, R, W, key):
        S.op(eng, lambda e: e.dma_start(out=out, in_=in_), toks(R), toks(W), dma=True, key=key)

    def rstd_from_ss(ss, n, R_buf):
        ts("dve", ss, ss, 1.0 / n, EPS, ALU.mult, ALU.add, [R_buf], [R_buf])
        act(ss, ss, AF.Ln, [R_buf], [R_buf])
        act(ss, ss, AF.Exp, [R_buf], [R_buf], scale=-0.5)

    ident = newbuf(PERS, [128], F32, "ident"); ident.ap = ident.ap
    tri = newbuf(PERS, [2, 128]); maskb = newbuf(PERS, [2, 128]); pinv = newbuf(PERS, [2]); pedge = newbuf(PERS, [2, 2, 8])
    cact = newbuf(PERS, [8, 2]); onesr = newbuf(PERS, [128]); zeros = newbuf(PERS, [144])
    MODT = newbuf(PERS, [48, 2]); badaT = newbuf(PERS, [48]); n1T = newbuf(PERS, [8]); n2T = newbuf(PERS, [8])
    G1T = newbuf(PERS, [8, 2]); G2T = newbuf(PERS, [8, 2])
    glag = newbuf(PERS, [384]); qg = newbuf(PERS, [384])
    kg = newbuf(PERS, [128]); esink = newbuf(PERS, [6]); pscT = newbuf(PERS, [2]); cw = newbuf(PERS, [44, 3]); cb = newbuf(PERS, [44])
    pw = newbuf(PERS, [2, 128], BF16)
    Sst = {d: newbuf(PERS, [2, 96]) for d in "fb"}
    Sbf = {d: newbuf(PERS, [2, 96], BF16) for d in "fb"}
    small = Ring([newbuf(PERS, [16], F32, f"small{i}") for i in range(8)])

    dma("sp", ident.ap, ident_d, [], [ident], "c_ident")
    ident_bf = newbuf(PERS, [128], BF16, "ident_bf")
    cp("dve", ident_bf.ap, ident.ap, [ident], [ident_bf])
    dma("sp", tri.ap, tri_d, [], [tri], "c_tri")
    dma("sp", maskb.ap, mask_d, [], [maskb], "c_mask")
    dma("sp", pinv.ap, pinv_d, [], [pinv], "c_pinv")
    dma("sp", pedge.ap, pedge_d, [], [pedge], "c_pedge")
    dma("sp", cact.ap, cT_d, [], [cact], "c_cact")
    memset("dve", onesr.ap, 1.0, [onesr])
    memset("dve", zeros.ap, 0.0, [zeros])
    act(cact.ap, cact.ap, AF.Silu, [cact], [cact])
    for sname, n in (("lat", SL), ("ctx", LC)):
        dma("sp", pu_d[sname][:, :, 0:8], zeros.ap[:, 0:16].rearrange("p (a b) -> p a b", a=2), [zeros], [], "c_pz")
        dma("sp", pu_d[sname][:, :, 8 + n:16 + n], zeros.ap[:, 0:16].rearrange("p (a b) -> p a b", a=2), [zeros], [], "c_pz")

    seqs = {"lat": dict(name="lat", n=SL, nb=SL // 128, w=0, rope=True),
            "ctx": dict(name="ctx", n=LC, nb=LC // 128, w=1, rope=False)}

    for l in range(DEPTH):
        last = (l == DEPTH - 1)
        x_src = {"lat": x_d if l == 0 else x1_d["lat"], "ctx": ctx_d if l == 0 else x1_d["ctx"]}
        x_dst = {"lat": out_d if last else x1_d["lat"], "ctx": x1_d["ctx"]}
        S.barrier()
        AR.reset()
        G2B = newbuf(AR, [2, D], F32, "G2B")
        gbmark = AR.off
        G1B = newbuf(AR, [2, D], F32, "G1B")
        KT = {"lat": newbuf(AR, [SL], BF16, "KT"), "ctx": newbuf(AR, [LC], BF16, "KTc")}
        VA = {"lat": newbuf(AR, [SL // 128, 2, 65], BF16, "VA"), "ctx": newbuf(AR, [LC // 128, 2, 65], BF16, "VAc")}
        WIN = newbuf(AR, [8, NW], BF16, "WIN"); WOUT = newbuf(AR, [8, D], BF16, "WOUT")
        mark = AR.off
        for bufx, src in ((badaT, badaT_d[l]), (n1T, n1T_d[l]), (n2T, n2T_d[l]), (glag, glag_d[l]), (qg, qg_d[l]),
                          (kg, kg_d[l]), (esink, sink_d[l]), (pscT, pscT_d[l]), (cw, cw_d[l]), (cb, cb_d[l])):
            dma("sp", bufx.ap, src, [], [bufx], "c_small")
        dma("pool", pw.ap, pw_d[l], [], [pw], "c_pw")
        act(esink.ap, esink.ap, AF.Exp, [esink], [esink])
        if STOP_AT <= 0.1:
            break
        memset("pool", WIN.ap, 0.0, [WIN])
        wv = win_d[l].rearrange("(k p) n -> p k n", p=128)

        def stage(dst0, src0, n):
            dma("pool", WIN.ap[:, :, dst0:dst0 + n], wv[:, :, src0:src0 + n], [], [WIN], "w_in")
        for h in range(4):
            stage(C_Q + 64 * h, 48 * h, 48)
            stage(C_K + 64 * h, 192 + 48 * h, 48)
        stage(C_Z, 1152, 16); stage(C_Z + 32, 1168, 16)
        stage(C_PU, 1824, 256)
        stage(C_GV, 384, 384); stage(C_GG, 768, 384)
        for i, h in enumerate((0, 3, 1, 4, 2, 5)):
            stage(C_AQ + 64 * i, 1184 + 64 * h, 64)
        stage(C_AK, 1568, 256)
        dma("pool", WOUT.ap, wout_d[l].rearrange("(k p) n -> p k n", p=128), [], [WOUT], "w_out")
        if STOP_AT <= 0.2:
            break
        crep = newbuf(AR, [8, 2, 128], F32, "crep")
        brow = newbuf(AR, [6 * D], F32, "brow")
        dma("sp", brow.ap[0:1, :], badar_d[l], [], [brow], "c_brow")
        for k in range(8):
            for w in range(2):
                cp("dve", crep.ap[:, k, w, :], cact.ap[:, k, w:w + 1].to_broadcast([128, 128]), [cact], [crep])
        wst = ring(AR, 2, [8, 512], F32, "wst")
        for cbk in range(12):
            if cbk == 6 and os.environ.get('MK_NOBAR', '') == '':
                S.barrier()
            wb = wst.next()
            dma("sp", wb.ap, wada_d[l][:, cbk * 512:(cbk + 1) * 512].rearrange("(k p) n -> p k n", p=128), [], [wb], f"wst{wst.i}")
            which, half = cbk // 2, cbk % 2
            if os.environ.get('MK_SKIP', '') == 'C' and cbk >= 6:
                continue
            if os.environ.get('MK_SKIP', '') == 'D' and cbk < 6:
                continue
            if os.environ.get('MK_SKIP', '') == 'A' and which in (2, 5):
                continue
            if os.environ.get('MK_SKIP', '') == 'B' and which not in (2, 5):
                continue
            if which in (2, 5):
                GB = G1B if which == 2 else G2B
                for w in range(2):
                    pb = PSR.next()
                    for k in range(8):
                        mm(pb.ap, crep.ap[:, k, w, :], wb.ap[:, k, :], [crep, wb], [pb], start=(k == 0), stop=False)
                    mm(pb.ap, onesr.ap[0:1, :], brow.ap[0:1, cbk * 512:(cbk + 1) * 512], [onesr, brow], [pb], start=False, stop=True)
                    cp("act", GB.ap[:, w, half * 512:(half + 1) * 512], pb.ap, [pb], [GB])
            else:
                pb = PSR.next()
                for j in range(4):
                    for k in range(8):
                        mm(pb.ap[:, j * 2:j * 2 + 2], wb.ap[:, k, j * 128:(j + 1) * 128], cact.ap[:, k, :], [wb, cact], [pb],
                           start=(k == 0), stop=(k == 7))
                j0 = cbk * 4
                tt("dve", MODT.ap[:, j0:j0 + 4, :], pb.ap[:, 0:8].rearrange("p (j w) -> p j w", j=4),
                   badaT.ap[:, j0:j0 + 4, None].to_broadcast([128, 4, 2]), ALU.add, [pb, badaT], [MODT])
        ts("dve", G1T.ap, MODT.ap[:, 8:16, :], 1.0, None, ALU.add, None, [MODT], [G1T])
        tt("dve", G1T.ap, G1T.ap, n1T.ap[:, :, None].to_broadcast([128, 8, 2]), ALU.mult, [G1T, n1T], [G1T])
        ts("dve", G2T.ap, MODT.ap[:, 32:40, :], 1.0, None, ALU.add, None, [MODT], [G2T])
        tt("dve", G2T.ap, G2T.ap, n2T.ap[:, :, None].to_broadcast([128, 8, 2]), ALU.mult, [G2T, n2T], [G2T])
        SH1T = MODT.ap[:, 0:8, :]
        SH2T = MODT.ap[:, 24:32, :]

        S.barrier()
        AR.reset(mark)
        xs_r = ring(AR, 2, [D], F32, "xs"); xn = newbuf(AR, [D], BF16); junk = newbuf(AR, [D], BF16); hT_r = ring(AR, 2, [8, 128], BF16, "hT")
        PT = newbuf(AR, [5, 6, 128], BF16); rp_r = ring(AR, 4, [128], F32, "rp")
        qn = newbuf(AR, [384]); qsq = newbuf(AR, [384]); qr = newbuf(AR, [384]); qT = newbuf(AR, [3, 128], BF16); oatt = newbuf(AR, [384], BF16)
        kn = B(qn.ap[:, 0:128], qn.t); ksq = B(qsq.ap[:, 0:128], qsq.t); kr = B(qr.ap[:, 0:128], qr.t); ob_sq = qsq
        qrb = newbuf(AR, [384], BF16); krb = B(qrb.ap[:, 0:128], qrb.t)
        puT = newbuf(AR, [2, 128]); puw = newbuf(AR, [2, 144]); s2 = newbuf(AR, [2, 144]); s4 = newbuf(AR, [2, 144]); s8 = newbuf(AR, [2, 144])
        s16 = newbuf(AR, [2, 144]); tmp8 = newbuf(AR, [8])
        ytmp = newbuf(AR, [D]); xm_r = ring(AR, 2, [D], F32, "xm")
        qk_sb = newbuf(AR, [512], BF16); zpu_sb = newbuf(AR, [320])
        WSETS = []
        for _ws in range(2):
            w_ = {}
            w_["zT"] = newbuf(AR, [128]); w_["e1"] = newbuf(AR, [256]); w_["spb"] = newbuf(AR, [256]); w_["epos"] = newbuf(AR, [256]); w_["eneg"] = newbuf(AR, [256])
            w_["qdT"] = newbuf(AR, [256], BF16); w_["kiT"] = newbuf(AR, [256], BF16); w_["keT"] = newbuf(AR, [256], BF16); w_["ke"] = newbuf(AR, [256], BF16)
            w_["scT"] = newbuf(AR, [4, 128], BF16); w_["vbf"] = newbuf(AR, [384], BF16); w_["obl"] = newbuf(AR, [384]); w_["ob_o"] = newbuf(AR, [384])
            w_["sgb"] = newbuf(AR, [384]); w_["dT"] = newbuf(AR, [2, 128], BF16); w_["mixT"] = newbuf(AR, [8, 128], BF16)
            w_["obst"] = w_["ob_o"]
            w_["ob_bf"] = newbuf(AR, [384], BF16)
            WSETS.append(w_)
        zT = e1 = spb = epos = eneg = qdT = kiT = keT = ke = scT = vbf = obl = ob_o = sgb = dT = mixT = obst = ob_bf = None

        def select(slot):
            nonlocal zT, e1, spb, epos, eneg, qdT, kiT, keT, ke, scT, vbf, obl, ob_o, sgb, dT, mixT, obst, ob_bf
            w_ = WSETS[slot % 2]
            zT, e1, spb, epos, eneg = w_["zT"], w_["e1"], w_["spb"], w_["epos"], w_["eneg"]
            qdT, kiT, keT, ke, scT, vbf = w_["qdT"], w_["kiT"], w_["keT"], w_["ke"], w_["scT"], w_["vbf"]
            obl, ob_o, sgb, dT, mixT, obst = w_["obl"], w_["ob_o"], w_["sgb"], w_["dT"], w_["mixT"], w_["obst"]
            ob_bf = w_["ob_bf"]
        select(0)
        print("ARENA mixer end", AR.off, "of", AR.size)
        wdec = newbuf(AR, [2, 256]); bdec = newbuf(AR, [2, 256])
        dma("sp", wdec.ap[0:48, :, :], wdec_d[l], [], [wdec], "c_wdec")
        dma("sp", bdec.ap[0:1, :, :], bdec_d[l], [], [bdec], "c_wdec")
        for sname in ("lat", "ctx"):
            memset("pool", VA[sname].ap[:, :, :, 64:65], 1.0, [VA[sname]])

        def front(src_ap, GT, SHT, w, dst_hT=None, col0=0, pre=None):
            xb = xs_r.next()
            if pre is not None:
                pre(xb)
            else:
                dma("sp", xb.ap, src_ap, [], [xb], f"xs{xs_r.i}")
            sm = small.next()
            memset("dve", sm.ap[:, 0:1], 0.0, [sm])
            act(junk.ap, xb.ap, AF.Square, [xb, sm], [junk, sm], accum_out=sm.ap[:, 0:1])
            rstd_from_ss(sm.ap[:, 0:1], D, sm)
            ts("dve", xn.ap, xb.ap, sm.ap[:, 0:1], None, ALU.mult, None, [xb, sm], [xn])
            if dst_hT is None:
                hT = hT_r.next()
            else:
                hT = dst_hT
            for half in range(2):
                pb = PSR.next()
                pbv = bfv(pb)
                for j in range(4):
                    k = half * 4 + j
                    tp(pbv[:, j * 128:(j + 1) * 128], xn.ap[:, k * 128:(k + 1) * 128], [xn], [pb])
                for j in range(4):
                    k = half * 4 + j
                    act(hT.ap[:, k, col0:col0 + 128], pbv[:, j * 128:(j + 1) * 128], AF.Identity, [pb, GT, MODT], [hT],
                        scale=GT.ap[:, k, w:w + 1], bias=SHT[:, k, w:w + 1])
            return xb, hT

        def projF(hT, col0, M, out, pb):
            for k in range(8):
                mm(out, WIN.ap[:, k, col0:col0 + M], hT.ap[:, k, :], [WIN, hT], [pb], start=(k == 0), stop=(k == 7))

        def projT(hT, col0, N, out, pb):
            for k in range(8):
                mm(out, hT.ap[:, k, :], WIN.ap[:, k, col0:col0 + N], [WIN, hT], [pb], start=(k == 0), stop=(k == 7))

        def qk_proj(hT, do_out):
            pqk = PSR.next()
            if do_out:
                projT(hT, C_Q, 512, pqk.ap[:, 0:512], pqk)
                cp("act", qk_sb.ap, pqk.ap, [pqk], [qk_sb])
            else:
                projT(hT, C_K, 256, pqk.ap[:, 256:512], pqk)
                cp("act", qk_sb.ap[:, 256:512], pqk.ap[:, 256:512], [pqk], [qk_sb])
            pq = PSR.next()
            for c in range(4):
                if c < 2 and not do_out:
                    continue
                tp(bfv(pq)[:, c * 128:(c + 1) * 128], qk_sb.ap[:, c * 128:(c + 1) * 128], [qk_sb], [pq])
            return B(bfv(pq)[:, 0:512], pq.t)

        def rope(src, dst, tmp, H, rp, outb):
            tt("dve", tmp.ap.rearrange("p (h d) -> p h d", h=H), src.ap.rearrange("p (h d) -> p h d", h=H),
               rp.ap[:, None, 0:64].to_broadcast([128, H, 64]), ALU.mult, [src, rp], [tmp])
            s5 = src.ap.rearrange("p (h a f c) -> p h a f c", h=H, a=2, f=2)
            d5 = dst.ap.rearrange("p (h a f c) -> p h a f c", h=H, a=2, f=2)
            sneg = rp.ap[:, 64:96].rearrange("p (a c) -> p a c", a=2)[:, None, :, :].to_broadcast([128, H, 2, 16])
            spos = rp.ap[:, 96:128].rearrange("p (a c) -> p a c", a=2)[:, None, :, :].to_broadcast([128, H, 2, 16])
            tt("dve", d5[:, :, :, 0, :], s5[:, :, :, 1, :], sneg, ALU.mult, [src, rp], [dst])
            tt("dve", d5[:, :, :, 1, :], s5[:, :, :, 0, :], spos, ALU.mult, [src, rp], [dst])
            tt("dve", outb.ap, dst.ap, tmp.ap, ALU.add, [dst, tmp], [outb])

        def headnorm(buf, sq, H, dh, gbuf):
            sm = small.next()
            tt("dve", sq.ap, buf.ap, buf.ap, ALU.mult, [buf], [sq])
            red("dve", sm.ap[:, 0:H], sq.ap.rearrange("p (h d) -> p h d", h=H), [sq], [sm])
            rstd_from_ss(sm.ap[:, 0:H], dh, sm)
            tt("dve", buf.ap.rearrange("p (h d) -> p h d", h=H), buf.ap.rearrange("p (h d) -> p h d", h=H),
               sm.ap[:, 0:H, None].to_broadcast([128, H, dh]), ALU.mult, [buf, sm], [buf])
            tt("dve", buf.ap, buf.ap, gbuf.ap, ALU.mult, [buf, gbuf], [buf])

        def gla_block(d, pq, pz, do_out):
            di = 0 if d == "f" else 1
            zr = 0 if d == "f" else 32
            col = 127 if d == "f" else 0
            cp("act", zT.ap[0:48, :], pz.ap[0:48, 0:128], [pz], [zT])
            pl = PSR.next()
            mm(pl.ap[:, 0:256], zT.ap[0:48, :], wdec.ap[0:48, di, :], [zT, wdec], [pl], start=True, stop=False)
            mm(pl.ap[:, 0:256], onesr.ap[0:1, :], bdec.ap[0:1, di, :], [onesr, bdec], [pl], start=False, stop=True)
            act(e1.ap, pl.ap[:, 0:256], AF.Exp, [pl], [e1], scale=-1.0)
            ts("dve", e1.ap, e1.ap, 1.0, None, ALU.add, None, [e1], [e1])
            act(spb.ap, e1.ap, AF.Ln, [e1], [spb])
            if STOP_AT <= 1.41:
                return None
            pbT = PSR.next()
            for pr in range(2):
                mm(pbT.ap[:, pr * 128:(pr + 1) * 128], spb.ap[:, pr * 128:(pr + 1) * 128], tri.ap[:, di, :], [spb, tri], [pbT])
            act(epos.ap, pbT.ap[:, 0:256], AF.Exp, [pbT], [epos])
            act(eneg.ap, pbT.ap[:, 0:256], AF.Exp, [pbT], [eneg], scale=-1.0)
            if do_out:
                stt("dve", qdT.ap, pq.ap[:, 0:256], DK ** -0.5, epos.ap, ALU.mult, ALU.mult, [pq, epos], [qdT])
                tt("dve", kiT.ap, pq.ap[:, 256:512], eneg.ap, ALU.mult, [pq, eneg], [kiT])
            for pr in range(2):
                stt("dve", keT.ap[:, pr * 128:(pr + 1) * 128], eneg.ap[:, pr * 128:(pr + 1) * 128],
                    epos.ap[:, pr * 128 + col:pr * 128 + col + 1], pq.ap[:, 256 + pr * 128:384 + pr * 128], ALU.mult, ALU.mult,
                    [eneg, epos, pq], [keT])
            pke = PSR.next()
            for pr in range(2):
                tp(bfv(pke)[:, pr * 128:(pr + 1) * 128], keT.ap[:, pr * 128:(pr + 1) * 128], [keT], [pke])
            cp("act", ke.ap, bfv(pke)[:, 0:256], [pke], [ke])
            po = None
            if STOP_AT <= 1.42:
                return None
            if do_out:
                psc = [PSR.next(), PSR.next()]
                for h in range(4):
                    pr, par, base = h // 2, h % 2, 64 * (h % 2)
                    mm(psc[par].ap[:, pr * 128:(pr + 1) * 128], kiT.ap[base:base + 48, pr * 128:(pr + 1) * 128],
                       qdT.ap[base:base + 48, pr * 128:(pr + 1) * 128], [kiT, qdT], [psc[par]])
                for par in range(2):
                    tt("dve", scT.ap[:, 2 * par:2 * par + 2, :], psc[par].ap[:, 0:256].rearrange("p (h i) -> p h i", h=2),
                       maskb.ap[:, di:di + 1, :].to_broadcast([128, 2, 128]), ALU.mult, [psc[par], maskb], [scT])
                if STOP_AT <= 1.425:
                    return None
                po = PSR.next()
                po2 = [PSR.next(), PSR.next()]
                for h in range(4):
                    pr, par, base = h // 2, h % 2, 64 * (h % 2)
                    mm(po.ap[:, h * 96:(h + 1) * 96], scT.ap[:, 2 * par + pr, :], vbf.ap[:, h * 96:(h + 1) * 96], [scT, vbf], [po], start=True, stop=True)
                for h in range(4):
                    pr, par, base = h // 2, h % 2, 64 * (h % 2)
                    mm(po2[par].ap[:, h * 96:(h + 1) * 96], qdT.ap[base:base + 48, pr * 128:(pr + 1) * 128], Sbf[d].ap[base:base + 48, pr, :],
                       [qdT, Sbf[d]], [po2[par]], start=True, stop=True)
                po = (po, po2[0], po2[1])
            if STOP_AT <= 1.43:
                return None
            pup = PSR.next()
            for pr in range(2):
                mm(pup.ap[:, pr * 192:(pr + 1) * 192], ke.ap[:, pr * 128:(pr + 1) * 128], vbf.ap[:, pr * 192:(pr + 1) * 192], [ke, vbf], [pup])
            for h in range(4):
                pr, base = h // 2, 64 * (h % 2)
                stt("dve", Sst[d].ap[base:base + 48, pr, :], Sst[d].ap[base:base + 48, pr, :],
                    epos.ap[base:base + 48, pr * 128 + col:pr * 128 + col + 1],
                    pup.ap[base:base + 48, pr * 192 + (h % 2) * 96:pr * 192 + (h % 2) * 96 + 96], ALU.mult, ALU.add,
                    [Sst[d], epos, pup], [Sst[d]])
            cp("pool", Sbf[d].ap, Sst[d].ap, [Sst[d]], [Sbf[d]])
            return po

        def pass_B(sq, do_out):
            name, nb, w = sq["name"], sq["nb"], sq["w"]
            lists = []
            for n in reversed(range(nb)):
                select(n)
                PSR.set(n)
                S.cur = []
                lists.append(S.cur)
                pass_B_block(sq, do_out, n)
                S.cur = None
            S.merge_threads(lists)

        def pass_B_block(sq, do_out, n):
            name, nb, w = sq["name"], sq["nb"], sq["w"]
            if True:
                if STOP_AT <= 1.1:
                    return
                xb, hT = front(x_src[name][n * 128:(n + 1) * 128, :], G1T, SH1T, w)
                pk = PSR.next()
                projT(hT, C_AK, 256, pk.ap[:, 0:256], pk)
                cp("act", kn.ap, pk.ap[:, 0:128], [pk], [kn])
                cp("act", VA[name].ap[:, n, :, 0:64], pk.ap[:, 128:256].rearrange("p (g d) -> p g d", g=2), [pk], [VA[name]])
                pv = PSR.next()
                projT(hT, C_GV, 384, pv.ap[:, 0:384], pv)
                cp("act", vbf.ap, pv.ap[:, 0:384], [pv], [vbf])
                headnorm(kn, ksq, 2, 64, kg)
                if sq["rope"]:
                    rp = rp_r.next()
                    dma("sp", rp.ap, rope_d[n * 128:(n + 1) * 128, :], [], [rp], f"rp{rp_r.i}")
                    rope(kn, kr, ksq, 2, rp, krb)
                else:
                    cp("dve", krb.ap, kn.ap, [kn], [krb])
                pkt = PSR.next()
                tp(bfv(pkt)[:, 0:128], krb.ap, [krb], [pkt])
                cp("act", KT[name].ap[:, n * 128:(n + 1) * 128], bfv(pkt)[:, 0:128], [pkt], [KT[name]])
                nz = 320 if do_out else 64
                pzt = PSR.next()
                projT(hT, C_Z, nz, pzt.ap[:, 0:nz], pzt)
                cp("act", zpu_sb.ap[:, 0:nz], pzt.ap[:, 0:nz], [pzt], [zpu_sb])
                pz = PSR.next()
                tp(pz.ap[0:64, 0:128], zpu_sb.ap[:, 0:64], [zpu_sb], [pz])
                if do_out:
                    tp(pz.ap[:, 128:256], zpu_sb.ap[:, 64:192], [zpu_sb], [pz])
                    tp(pz.ap[:, 256:384], zpu_sb.ap[:, 192:320], [zpu_sb], [pz])
                    cp("act", puT.ap, pz.ap[:, 128:384].rearrange("p (c t) -> p c t", c=2), [pz], [puT])
                    dma("sp", pu_d[name][:, :, 8 + n * 128:8 + (n + 1) * 128], puT.ap, [puT], [], "st_pu")
                pq = qk_proj(hT, do_out)
                po = gla_block("b", pq, pz, do_out)
                if STOP_AT <= 1.5:
                    return
                if do_out:
                    cp("act", obst.ap, po[0].ap[:, 0:384], [po[0]], [obst])
                    for par in range(2):
                        ov = obst.ap.rearrange("p (pr x) -> p pr x", pr=2)[:, :, par * 96:(par + 1) * 96]
                        pv2 = po[1 + par].ap[:, 0:384].rearrange("p (pr x) -> p pr x", pr=2)[:, :, par * 96:(par + 1) * 96]
                        tt("dve", ov, ov, pv2, ALU.add, [obst, po[1 + par]], [obst])
                    dma("sp", ob_d[name][n * 128:(n + 1) * 128, :], obst.ap, [obst], [], "st_ob")

        def pass_F(sq, do_out):
            name, nb, w = sq["name"], sq["nb"], sq["w"]
            lists = []
            for n in range(nb):
                select(n)
                PSR.set(n)
                S.cur = []
                lists.append(S.cur)
                pass_F_block(sq, do_out, n)
                S.cur = None
            S.merge_threads(lists)

        def pass_F_block(sq, do_out, n):
            name, nb, w = sq["name"], sq["nb"], sq["w"]
            if True:
                xb, hT = front(x_src[name][n * 128:(n + 1) * 128, :], G1T, SH1T, w)
                pv = PSR.next()
                projT(hT, C_GV, 384, pv.ap[:, 0:384], pv)
                cp("act", vbf.ap, pv.ap[:, 0:384], [pv], [vbf])
                if do_out:
                    pg = PSR.next()
                    projT(hT, C_GG, 384, pg.ap[:, 0:384], pg)
                    act(sgb.ap, pg.ap[:, 0:384], AF.Silu, [pg], [sgb])
                    pa = PSR.next()
                    projT(hT, C_AQ, 384, pa.ap[:, 0:384], pa)
                    cp("act", qn.ap, pa.ap[:, 0:384], [pa], [qn])
                pz = PSR.next()
                projF(hT, C_Z, 48, pz.ap[0:48, 0:128], pz)
                pq = qk_proj(hT, do_out)
                if not do_out:
                    gla_block("f", pq, pz, False)
                    return
                dma("sp", obl.ap, ob_d[name][n * 128:(n + 1) * 128, :], [], [obl], "ld_ob")
                po = gla_block("f", pq, pz, True)
                tt("dve", ob_o.ap, po[0].ap[:, 0:384], obl.ap, ALU.add, [po[0], obl], [ob_o])
                for par in range(2):
                    ov = ob_o.ap.rearrange("p (pr x) -> p pr x", pr=2)[:, :, par * 96:(par + 1) * 96]
                    pv2 = po[1 + par].ap[:, 0:384].rearrange("p (pr x) -> p pr x", pr=2)[:, :, par * 96:(par + 1) * 96]
                    tt("dve", ov, ov, pv2, ALU.add, [ob_o, po[1 + par]], [ob_o])
                headnorm(ob_o, ob_sq, 4, 96, glag)
                tt("dve", ob_bf.ap, ob_o.ap, sgb.ap, ALU.mult, [ob_o, sgb], [ob_bf])
                pmt = PSR.next()
                for c in range(3):
                    tp(bfv(pmt)[:, c * 128:(c + 1) * 128], ob_bf.ap[:, c * 128:(c + 1) * 128], [ob_bf], [pmt])
                cp("act", mixT.ap[:, 0:3, :], bfv(pmt)[:, 0:384].rearrange("p (c t) -> p c t", c=3), [pmt], [mixT])
                headnorm(qn, qsq, 6, 64, qg)
                if sq["rope"]:
                    rp = rp_r.next()
                    dma("sp", rp.ap, rope_d[n * 128:(n + 1) * 128, :], [], [rp], f"rp{rp_r.i}")
                    rope(qn, qr, qsq, 6, rp, qrb)
                else:
                    cp("dve", qrb.ap, qn.ap, [qn], [qrb])
                pqt = PSR.next()
                for c in range(3):
                    tp(bfv(pqt)[:, c * 128:(c + 1) * 128], qrb.ap[:, c * 128:(c + 1) * 128], [qrb], [pqt])
                cp("act", qT.ap, bfv(pqt)[:, 0:384].rearrange("p (c t) -> p c t", c=3), [pqt], [qT])
                kbs = []
                if name == "lat":
                    if n > 0:
                        kbs.append(("lat", n - 1, 1))
                    kbs.append(("lat", n, None))
                    if n < nb - 1:
                        kbs.append(("lat", n + 1, 0))
                for cbk in range(LC // 128):
                    kbs.append(("ctx", cbk, None))
                for ki, (ks, kn_, mk) in enumerate(kbs):
                    for g in range(2):
                        ps_ = PSR.next()
                        mm(ps_.ap[:, 0:384].rearrange("p (h i) -> p h i", h=3), KT[ks].ap[g * 64:(g + 1) * 64, kn_ * 128:(kn_ + 1) * 128],
                           qT.ap[g * 64:(g + 1) * 64, :, :], [KT[ks], qT], [ps_])
                        act(PT.ap[:, ki, 3 * g:3 * g + 3, :], ps_.ap[:, 0:384].rearrange("p (h i) -> p h i", h=3), AF.Exp, [ps_], [PT], scale=0.125)
                        if mk is not None:
                            tt("dve", PT.ap[:, ki, 3 * g:3 * g + 3, :], PT.ap[:, ki, 3 * g:3 * g + 3, :],
                               maskb.ap[:, mk:mk + 1, :].to_broadcast([128, 3, 128]), ALU.mult, [PT, maskb], [PT])
                pov = PSR.next()
                for h in range(6):
                    g = h // 3
                    for ki, (ks, kn_, mk) in enumerate(kbs):
                        mm(pov.ap[:, h * 65:(h + 1) * 65], PT.ap[:, ki, h, :], VA[ks].ap[:, kn_, g, :], [PT, VA[ks]], [pov],
                           start=(ki == 0), stop=(ki == len(kbs) - 1))
                sm = small.next()
                pov3 = pov.ap[:, 0:390].rearrange("p (h e) -> p h e", e=65)
                tt("dve", sm.ap[:, 0:6], pov3[:, :, 64], esink.ap, ALU.add, [pov, esink], [sm])
                recip(sm.ap[:, 0:6], sm.ap[:, 0:6], [sm], [sm])
                tt("dve", oatt.ap.rearrange("p (h d) -> p h d", h=6), pov3[:, :, 0:64], sm.ap[:, 0:6, None].to_broadcast([128, 6, 64]),
                   ALU.mult, [pov, sm], [oatt])
                pat = PSR.next()
                for c in range(3):
                    tp(bfv(pat)[:, c * 128:(c + 1) * 128], oatt.ap[:, c * 128:(c + 1) * 128], [oatt], [pat])
                cp("act", mixT.ap[:, 3:6, :], bfv(pat)[:, 0:384].rearrange("p (c t) -> p c t", c=3), [pat], [mixT])
                dma("sp", puw.ap, pu_d[name][:, :, n * 128:n * 128 + 144], [], [puw], "ld_pu")
                tt("pool", s2.ap[:, :, 1:144], puw.ap[:, :, 0:143], puw.ap[:, :, 1:144], ALU.add, [puw], [s2])
                tt("pool", s4.ap[:, :, 2:143], s2.ap[:, :, 1:142], s2.ap[:, :, 3:144], ALU.add, [s2], [s4])
                tt("pool", s8.ap[:, 1, 4:141], s4.ap[:, 1, 2:139], s4.ap[:, 1, 6:143], ALU.add, [s4], [s8])
                tt("pool", s16.ap[:, 1, 8:136], s8.ap[:, 1, 4:132], s8.ap[:, 1, 12:140], ALU.add, [s8], [s16])
                combos = ((0, 64, 0, s2), (64, 128, 0, s4), (0, 64, 1, s8), (64, 128, 1, s16))
                for (r0, r1, c, sb_) in combos:
                    stt("dve", dT.ap[r0:r1, c, :], sb_.ap[r0:r1, c, 8:136], pinv.ap[r0:r1, c:c + 1], puw.ap[r0:r1, c, 8:136],
                        ALU.mult, ALU.subtract, [sb_, pinv, puw], [dT])
                for edge, cols, dcols in ((0, slice(8, 16), slice(0, 8)), (1, slice(128, 136), slice(120, 128))):
                    if (edge == 0 and n == 0) or (edge == 1 and n == nb - 1):
                        for (r0, r1, c, sb_) in combos:
                            tt("pool", tmp8.ap[r0:r1, :], sb_.ap[r0:r1, c, cols], pedge.ap[r0:r1, edge, c, :], ALU.mult, [sb_, pedge], [tmp8])
                            tt("pool", dT.ap[r0:r1, c, dcols], tmp8.ap[r0:r1, :], puw.ap[r0:r1, c, cols], ALU.subtract, [tmp8, puw], [dT])
                pp = PSR.next()
                for c in range(2):
                    mm(pp.ap[:, c * 128:(c + 1) * 128], pw.ap[:, c, :], dT.ap[:, c, :], [pw, dT], [pp])
                for c in range(2):
                    act(mixT.ap[:, 6 + c, :], pp.ap[:, c * 128:(c + 1) * 128], AF.Copy, [pp, pscT], [mixT], scale=pscT.ap[:, c:c + 1])
                xm = xm_r.next()
                for half in range(2):
                    py = PSR.next()
                    for kc in range(8):
                        mm(py.ap, mixT.ap[:, kc, :], WOUT.ap[:, kc, half * 512:(half + 1) * 512], [mixT, WOUT], [py], start=(kc == 0), stop=(kc == 7))
                    hs = slice(half * 512, (half + 1) * 512)
                    tt("dve", ytmp.ap[:, hs], py.ap, G1B.ap[:, w, hs], ALU.mult, [py, G1B], [ytmp])
                    tt("pool", xm.ap[:, hs], ytmp.ap[:, hs], xb.ap[:, hs], ALU.add, [ytmp, xb], [xm])
                dma("sp", xmid_d[name][n * 128:(n + 1) * 128, :], xm.ap, [xm], [], f"st_xm{xm_r.i}")

        for d in "fb":
            memset("dve", Sst[d].ap, 0.0, [Sst[d]])
            memset("pool", Sbf[d].ap, 0.0, [Sbf[d]])
        ctx_out = not last
        if STOP_AT <= 1:
            break
        pass_B(seqs["ctx"], ctx_out)
        if STOP_AT <= 2:
            break
        pass_F(seqs["ctx"], ctx_out)
        if STOP_AT <= 3:
            break
        pass_B(seqs["lat"], True)
        if STOP_AT <= 4:
            break
        pass_F(seqs["lat"], True)
        if STOP_AT <= 5:
            break

        S.barrier()
        AR.reset(gbmark)
        WUP = newbuf(AR, [8, 2 * D_FF], BF16, "WUP"); WDN = newbuf(AR, [22, D], BF16, "WDN")
        wuv = wup_d[l].rearrange("(k p) n -> p k n", p=128)
        wdv = wdn_d[l].rearrange("(k p) n -> p k n", p=128)
        for k in range(8):
            for cc in range(11):
                dma("pool", WUP.ap[:, k, cc * 512:(cc + 1) * 512], wuv[:, k, cc * 512:(cc + 1) * 512], [], [WUP], "w_up")
        for k in range(22):
            for cc in range(2):
                dma("pool", WDN.ap[:, k, cc * 512:(cc + 1) * 512], wdv[:, k, cc * 512:(cc + 1) * 512], [], [WDN], "w_dn")
        xs_r = ring(AR, 2, [D], F32, "xs"); xn = newbuf(AR, [D], BF16); junk = newbuf(AR, [D], BF16)
        h2T_r = [newbuf(AR, [8, 256], BF16, "h2Ta"), newbuf(AR, [8, 256], BF16, "h2Tb")]; actT = newbuf(AR, [22, 256], BF16, "actT")
        cva = ring(AR, 3, [256], F32, "cva"); cvb = ring(AR, 1, [256], F32, "cvb"); cga = ring(AR, 3, [256], F32, "cga")
        cgb = ring(AR, 1, [256], F32, "cgb"); csg = ring(AR, 2, [256], F32, "csg")
        xr_r = ring(AR, 1, [D], F32, "xr"); yt_r = ring(AR, 1, [D], F32, "yt")
        print("ARENA ffn end", AR.off, "of", AR.size)

        def ffn_pass(sq):
            name, n_tok, w = sq["name"], sq["n"], sq["w"]
            starts = sorted(set(min(FT * j, n_tok - FT) for j in range((n_tok + FT - 1) // FT)))
            tiles = []
            for ti, s in enumerate(starts):
                h2T = h2T_r[ti % 2]
                PSR.set(ti)
                T = []
                S.cur = T
                for bi in range(2):
                    r0 = s - 1 + 128 * bi
                    lo, hi = max(r0, 0), min(r0 + 128, n_tok)

                    def pre(xb, r0=r0, lo=lo, hi=hi):
                        if lo > r0:
                            memset("dve", xb.ap[0:32, :], 0.0, [xb])
                        if hi < r0 + 128:
                            memset("dve", xb.ap[96:128, :], 0.0, [xb])
                        dma("sp", xb.ap[lo - r0:hi - r0, :], xmid_d[name][lo:hi, :], [], [xb], f"xs{xs_r.i}")
                    front(None, G2T, SH2T, w, dst_hT=h2T, col0=bi * 128, pre=pre)
                if s == 0:
                    memset("dve", h2T.ap[:, :, 0:1], 0.0, [h2T])
                if s + FT == n_tok:
                    memset("dve", h2T.ap[:, :, 255:256], 0.0, [h2T])
                clists = []
                for c in range(22):
                    S.cur = []
                    clists.append(S.cur)
                    pus = (PSR.next(), PSR.next())
                    for pu_, col in ((pus[0], c * 128), (pus[1], D_FF + c * 128)):
                        for k in range(8):
                            mm(pu_.ap[:, 0:256], WUP.ap[:, k, col:col + 128], h2T.ap[:, k, :], [WUP, h2T], [pu_], start=(k == 0), stop=(k == 7))
                    res = []
                    for fc, pu_, ra, rb in ((c, pus[0], cva, cvb), (22 + c, pus[1], cga, cgb)):
                        t1, t2 = ra.next(), rb.next()
                        act(t1.ap[:, 0:FT], pu_.ap[:, 1:1 + FT], AF.Identity, [pu_, cw, cb], [t1], scale=cw.ap[:, fc, 1:2], bias=cb.ap[:, fc:fc + 1])
                        stt("dve", t2.ap[:, 0:FT], pu_.ap[:, 0:FT], cw.ap[:, fc, 0:1], t1.ap[:, 0:FT], ALU.mult, ALU.add, [pu_, cw, t1], [t2])
                        stt("dve", t1.ap[:, 0:FT], pu_.ap[:, 2:2 + FT], cw.ap[:, fc, 2:3], t2.ap[:, 0:FT], ALU.mult, ALU.add, [pu_, cw, t2], [t1])
                        res.append(t1)
                    sg_ = csg.next()
                    act(sg_.ap[:, 0:FT], res[1].ap[:, 0:FT], AF.Silu, [res[1]], [sg_])
                    tt("pool", actT.ap[:, c, 0:FT], sg_.ap[:, 0:FT], res[0].ap[:, 0:FT], ALU.mult, [sg_, res[0]], [actT])
                S.cur = None
                T.extend(S.merge_threads(clists, ret=True))
                S.cur = T
                for sub, m in ((0, 128), (1, FT - 128)):
                    t0 = s + sub * 128
                    xr = xr_r.next()
                    dma("sp", xr.ap[0:m, :], xmid_d[name][t0:t0 + m, :], [], [xr], f"xr{xr_r.i}")
                    yt = yt_r.next()
                    for half in range(2):
                        py = PSR.next()
                        for c in range(22):
                            mm(py.ap[0:m, :], actT.ap[:, c, sub * 128:sub * 128 + m], WDN.ap[:, c, half * 512:(half + 1) * 512], [actT, WDN], [py],
                               start=(c == 0), stop=(c == 21))
                        hs = slice(half * 512, (half + 1) * 512)
                        tt("dve", yt.ap[0:m, hs], py.ap[0:m, :], G2B.ap[0:m, w, hs], ALU.mult, [py, G2B], [yt])
                        tt("pool", yt.ap[0:m, hs], yt.ap[0:m, hs], xr.ap[0:m, hs], ALU.add, [yt, xr], [yt])
                    dma("sp", x_dst[name][t0:t0 + m, :], yt.ap[0:m, :], [yt], [], f"st_xo{yt_r.i}")
                S.cur = None
                tiles.append(T)
            S.merge_threads(tiles)

        if STOP_AT <= 6:
            break
        if not last:
            ffn_pass(seqs["ctx"])
        if STOP_AT <= 7:
            break
        ffn_pass(seqs["lat"])
        if STOP_AT <= 8:
            break

    S.barrier()
    S.emit(st)
    st.close()
    return nc, S


def host_consts(SL):
    ident = np.eye(128, dtype=np.float32)
    j = np.arange(128)[:, None]
    i = np.arange(128)[None, :]
    le = (j <= i).astype(np.float32)
    ge = (j >= i).astype(np.float32)
    tri = np.stack([le, ge], axis=1) * np.float32(-1.0 / 16.0)
    mask = np.stack([le, ge], axis=1)
    rows_n = SL // GRID_W
    rows = np.repeat(np.arange(rows_n), GRID_W).astype(np.float32)
    cols = np.tile(np.arange(GRID_W), rows_n).astype(np.float32)
    nf = 16
    inv = (np.float32(10000.0) ** (-np.arange(nf, dtype=np.float32) / np.float32(nf))).astype(np.float32)
    ang = np.concatenate([rows[:, None] * inv, cols[:, None] * inv], axis=-1).astype(np.float32)
    cos, sin = np.cos(ang).astype(np.float32), np.sin(ang).astype(np.float32)
    cr, cc, sr, sc = cos[:, :16], cos[:, 16:], sin[:, :16], sin[:, 16:]
    rope = np.concatenate([cr, cr, cc, cc, -sr, -sc, sr, sc], axis=-1).astype(np.float32)
    wins = (2, 4, 8, 16)
    pinv = np.zeros((128, 2), np.float32)
    pedge = np.zeros((128, 2, 2, 8), np.float32)
    T = 1 << 20
    for c in range(2):
        for gi in range(2):
            wd = wins[2 * c + gi]
            sl = slice(gi * 64, (gi + 1) * 64)
            pinv[sl, c] = 1.0 / wd
            for t in range(8):
                cnt_first = min(t + wd // 2, T) - max(t - wd // 2, 0)
                tl = T - 8 + t
                cnt_last = min(tl + wd // 2, T) - max(tl - wd // 2, 0)
                pedge[sl, 0, c, t] = 1.0 / cnt_first
                pedge[sl, 1, c, t] = 1.0 / cnt_last
    return dict(ident=ident, tri=np.ascontiguousarray(tri), mask=np.ascontiguousarray(mask), rope=rope, pinv=pinv, pedge=pedge)


def host_weights(inp):
    f = lambda a: np.ascontiguousarray(np.asarray(a, dtype=np.float32))
    L = DEPTH
    out = {}
    out["w_ada"] = f(inp["w_ada"])
    out["b_adaT"] = f(np.asarray(inp["b_ada"]).reshape(L, 48, 128).transpose(0, 2, 1))
    out["b_adar"] = f(np.asarray(inp["b_ada"]).reshape(L, 1, 6 * D))
    out["n1T"] = f(np.asarray(inp["norm1_g"]).reshape(L, 8, 128).transpose(0, 2, 1))
    out["n2T"] = f(np.asarray(inp["norm2_g"]).reshape(L, 8, 128).transpose(0, 2, 1))
    out["w_in"] = f(inp["w_in"])
    wd = np.asarray(inp["gla_w_dec"])
    bd = np.asarray(inp["gla_b_dec"])
    wdec = np.zeros((L, 48, 2, 256), np.float32)
    bdec = np.zeros((L, 1, 2, 256), np.float32)
    for d in range(2):
        for h in range(4):
            wdec[:, 32 * d:32 * d + 16, d, 64 * h:64 * h + 48] = wd[:, d, :, 48 * h:48 * h + 48]
            bdec[:, 0, d, 64 * h:64 * h + 48] = bd[:, d, 48 * h:48 * h + 48]
    out["wdec"], out["bdec"] = wdec, bdec
    out["glag"] = f(np.broadcast_to(np.tile(np.asarray(inp["gla_norm_g"]), (1, 4))[:, None, :], (L, 128, 384)))
    out["qg"] = f(np.broadcast_to(np.tile(np.asarray(inp["q_norm_g"]), (1, 6))[:, None, :], (L, 128, 384)))
    out["kg"] = f(np.broadcast_to(np.tile(np.asarray(inp["k_norm_g"]), (1, 2))[:, None, :], (L, 128, 128)))
    out["sink"] = f(np.broadcast_to(np.asarray(inp["sink_logit"])[:, None, :], (L, 128, 6)))
    pwi = np.asarray(inp["pool_w"])
    pw = np.zeros((L, 128, 2, 128), np.float32)
    for c in range(2):
        for gi in range(2):
            pw[:, gi * 64:(gi + 1) * 64, c, gi * 64:(gi + 1) * 64] = pwi[:, 2 * c + gi]
    out["pw"] = pw
    out["pscT"] = f(np.asarray(inp["pool_scale"]).reshape(L, 2, 128).transpose(0, 2, 1))
    out["w_out"] = f(inp["w_out"])
    out["w_up"] = f(inp["w_up"])
    out["cw"] = f(np.asarray(inp["conv_w"]).reshape(L, 3, 44, 128).transpose(0, 3, 2, 1))
    out["cb"] = f(np.asarray(inp["conv_b"]).reshape(L, 44, 128).transpose(0, 2, 1))
    out["w_down"] = f(inp["w_down"])
    return out


_CACHE = {}


def kernel(**inputs):
    x = np.asarray(inputs["x"], dtype=np.float32)
    ctx = np.asarray(inputs["ctx"], dtype=np.float32)
    c = np.asarray(inputs["c"], dtype=np.float32)
    c_ctx = np.asarray(inputs["c_ctx"], dtype=np.float32)
    Bn, SL, _ = x.shape
    LC = ctx.shape[1]
    key = (SL, LC)
    if key not in _CACHE:
        _CACHE[key] = build(SL, LC)
    nc, _ = _CACHE[key]
    shared = dict(host_consts(SL))
    shared.update(host_weights(inputs))
    in_maps = []
    for b in range(Bn):
        m = dict(shared)
        m["x"] = np.ascontiguousarray(x[b])
        m["ctx"] = np.ascontiguousarray(ctx[b])
        cT = np.stack([c[b].reshape(8, 128).T, c_ctx.reshape(8, 128).T], axis=-1)
        m["cT"] = np.ascontiguousarray(cT.astype(np.float32))
        in_maps.append(m)
    res = run_bass_kernel_spmd(nc, in_maps, core_ids=list(range(Bn)))
    return np.stack([np.asarray(r["out"], dtype=np.float32) for r in res.results], axis=0)
```

```python
import numpy as np
from contextlib import ExitStack
import concourse.bass as bass
import concourse.mybir as mybir
from concourse.bass_utils import run_bass_kernel_spmd

F32 = mybir.dt.float32
BF16 = mybir.dt.bfloat16
AF = mybir.ActivationFunctionType
ALU = mybir.AluOpType
AX = mybir.AxisListType

D = 1024
DEPTH = 2
GRID_W = 64
NH_G, DK, DV = 4, 48, 96
D_FF = 2816
EPS = 1e-6
IN_W = 2080
C_Q, C_K, C_Z, C_PU, C_GV, C_GG, C_AQ, C_AK = 0, 256, 512, 576, 832, 1216, 1600, 1984
NW = 2240
FT = 254

import os
SAME_ENGINE_SYNC = True
RAW_ONLY = os.environ.get('MK_RAWONLY', '1') == '1'
import os
STOP_AT = float(os.environ.get('MK_STOP', '99'))


class Tok:
    __slots__ = ("name", "w", "r", "excl")

    def __init__(self, name):
        self.name = name
        self.w = []
        self.r = {}
        self.excl = False


class Op:
    __slots__ = ("eng", "fn", "deps", "odeps", "sig", "dma", "key", "ev", "tag")

    def __init__(self, eng, fn, dma, key):
        self.eng, self.fn, self.dma, self.key = eng, fn, dma, key
        self.deps = []
        self.odeps = []
        self.sig = False
        self.ev = None


class Sched:
    ENGS = ("pe", "act", "dve", "pool", "sp")

    def __init__(self, nc):
        self.nc = nc
        self.ops = []
        self.cur = None
        self.log = []
        self.ntok = 0

    def tok(self, name=None):
        self.ntok += 1
        return Tok(name or f"t{self.ntok}")

    def op(self, eng, fn, reads=(), writes=(), dma=False, key=None):
        o = Op(eng, fn, dma, key)
        if os.environ.get('MK_DEBUG'):
            import inspect
            o.tag = inspect.stack()[2].lineno
        deps = {}
        raw = set()
        for t in reads:
            for w in t.w:
                deps[id(w)] = w
                raw.add(id(w))
            if t.excl:
                for re_, rd in t.r.items():
                    if re_ != eng and not isinstance(rd, list):
                        deps[id(rd)] = rd
        for t in writes:
            samekey = dma and t.w and all(w.dma and w.key == key for w in t.w) and not t.r
            if not samekey:
                for w in t.w:
                    deps[id(w)] = w
                for rd in t.r.values():
                    if isinstance(rd, list):
                        for x in rd:
                            deps[id(x)] = x
                    else:
                        deps[id(rd)] = rd
        for d in deps.values():
            if (not d.dma) and (not dma) and d.eng == eng:
                if eng == "pe" or not SAME_ENGINE_SYNC or (RAW_ONLY and id(d) not in raw):
                    o.odeps.append(d)
                    continue
            o.deps.append(d)
            d.sig = True
        for t in reads:
            if dma:
                t.r.setdefault("dma", []).append(o)
            else:
                t.r[eng] = o
        for t in writes:
            samekey = dma and t.w and all(w.dma and w.key == key for w in t.w) and not t.r
            if samekey:
                t.w.append(o)
            else:
                t.w = [o]
                t.r = {}
        if dma:
            o.sig = True
        (self.cur if self.cur is not None else self.ops).append(o)
        return o

    def merge_threads(self, lists, ratio=1, ret=False):
        ratio = float(os.environ.get("MK_RATIO", "1"))
        delay = float(os.environ.get("MK_DELAY", "0"))
        unem = set()
        for L in lists:
            for o in L:
                unem.add(id(o))
        out = []
        k = len(lists)
        pos = [0] * k
        i = 0
        credit = 0.0
        while i < k:
            L = lists[i]
            if pos[i] >= len(L):
                i += 1
                continue
            o = L[pos[i]]
            pos[i] += 1
            out.append(o)
            unem.discard(id(o))
            j = i + 1
            if j < k and pos[i] >= delay * len(L):
                Lj = lists[j]
                credit += ratio
                while credit >= 1.0:
                    credit -= 1.0
                    if pos[j] < len(Lj):
                        o2 = Lj[pos[j]]
                        if all(id(d) not in unem for d in o2.deps) and all(id(d) not in unem for d in o2.odeps):
                            pos[j] += 1
                            out.append(o2)
                            unem.discard(id(o2))
        if ret:
            return out
        self.ops.extend(out)

    def barrier(self):
        last, dmas = {}, {}
        for o in self.ops:
            if o.fn is None:
                continue
            if o.dma:
                dmas[o.key] = o
            else:
                last[o.eng] = o
        for e in self.ENGS:
            b = Op(e, None, False, None)
            for d in list(last.values()) + list(dmas.values()):
                if (not d.dma) and d.eng == e:
                    continue
                b.deps.append(d)
                d.sig = True
            self.ops.append(b)

    def emit(self, stack):
        nc = self.nc
        cnt = {e: 0 for e in self.ENGS}
        keycnt = {}
        for o in self.ops:
            if o.fn is None:
                continue
            if o.dma:
                keycnt[o.key] = keycnt.get(o.key, 0) + 16
                o.ev = (("dma", o.key), keycnt[o.key])
            elif o.sig:
                cnt[o.eng] += 1
                o.ev = (("eng", o.eng), cnt[o.eng])
        sems = {}
        for e in self.ENGS:
            if cnt[e]:
                sems[("eng", e)] = stack.enter_context(nc.semaphore(f"c_{e}"))
        for k in keycnt:
            sems[("dma", k)] = stack.enter_context(nc.semaphore(f"d_{k}"))
        per = {e: [] for e in self.ENGS}
        for o in self.ops:
            per[o.eng].append(o)
        self.stats = {e: len(per[e]) for e in self.ENGS}
        self.stats["sems"] = len(sems)
        self.stats["maxcnt"] = dict(cnt)
        block = stack.enter_context(nc.Block())

        def run(engobj, lst):
            waited = {}
            for o in lst:
                need = {}
                for d in o.deps:
                    if d.ev is None:
                        continue
                    s, v = d.ev
                    if waited.get(s, 0) >= v:
                        continue
                    if need.get(s, 0) < v:
                        need[s] = v
                for s, v in need.items():
                    engobj.wait_ge(sems[s], v)
                    waited[s] = v
                    if os.environ.get('MK_DEBUG'):
                        self.log.append(f"{o.eng} WAIT {s} >= {v}")
                if os.environ.get('MK_DEBUG') and o.fn is not None:
                    self.log.append(f"{o.eng} OP line{getattr(o, 'tag', 0)} ev={o.ev}")
                if o.fn is None:
                    continue
                ins = o.fn(engobj)
                if o.ev is not None:
                    ins.then_inc(sems[o.ev[0]], 16 if o.dma else 1)

        @block.tensor
        def _(e):
            run(e, per["pe"])

        @block.scalar
        def _(e):
            run(e, per["act"])

        @block.vector
        def _(e):
            run(e, per["dve"])

        @block.gpsimd
        def _(e):
            run(e, per["pool"])

        @block.sync
        def _(e):
            run(e, per["sp"])


class Arena:
    def __init__(self, ap):
        self.ap = ap
        self.size = ap.shape[1]
        self.off = 0

    def reset(self, to=0):
        self.off = to

    def alloc(self, shape, dtype=F32):
        n = int(np.prod(shape))
        bpe = 4 if dtype == F32 else 2
        words = (n * bpe + 3) // 4
        words = (words + 7) // 8 * 8
        assert self.off + words <= self.size, f"arena overflow {self.off}+{words}>{self.size}"
        v = self.ap[:, self.off:self.off + words]
        self.off += words
        if dtype != F32:
            v = v.bitcast(dtype)
        v = v[:, 0:n]
        if len(shape) == 1:
            return v
        names = "abcd"[:len(shape)]
        pat = "p (" + " ".join(names) + ") -> p " + " ".join(names)
        return v.rearrange(pat, **{names[i]: shape[i] for i in range(len(shape))})


class B:
    __slots__ = ("ap", "t")

    def __init__(self, ap, t):
        self.ap, self.t = ap, t


class Ring:
    def __init__(self, bufs):
        self.bufs = bufs
        self.i = -1

    def next(self):
        self.i = (self.i + 1) % len(self.bufs)
        return self.bufs[self.i]


def build(SL, LC):
    nc = bass.Bass("TRN2", target_bir_lowering=False)

    def dten(name, shape, kind="ExternalInput"):
        return nc.dram_tensor(name, list(shape), F32, kind=kind).ap()

    x_d = dten("x", [SL, D]); ctx_d = dten("ctx", [LC, D]); cT_d = dten("cT", [128, 8, 2])
    ident_d = dten("ident", [128, 128]); tri_d = dten("tri", [128, 2, 128]); mask_d = dten("mask", [128, 2, 128])
    rope_d = dten("rope", [SL, 128]); pinv_d = dten("pinv", [128, 2]); pedge_d = dten("pedge", [128, 2, 2, 8])
    wada_d = dten("w_ada", [DEPTH, D, 6 * D]); badaT_d = dten("b_adaT", [DEPTH, 128, 48]); badar_d = dten("b_adar", [DEPTH, 1, 6 * D])
    n1T_d = dten("n1T", [DEPTH, 128, 8]); n2T_d = dten("n2T", [DEPTH, 128, 8])
    win_d = dten("w_in", [DEPTH, D, IN_W]); wdec_d = dten("wdec", [DEPTH, 48, 2, 256]); bdec_d = dten("bdec", [DEPTH, 1, 2, 256])
    glag_d = dten("glag", [DEPTH, 128, 384]); qg_d = dten("qg", [DEPTH, 128, 384]); kg_d = dten("kg", [DEPTH, 128, 128])
    sink_d = dten("sink", [DEPTH, 128, 6]); pw_d = dten("pw", [DEPTH, 128, 2, 128]); pscT_d = dten("pscT", [DEPTH, 128, 2])
    wout_d = dten("w_out", [DEPTH, D, D]); wup_d = dten("w_up", [DEPTH, D, 2 * D_FF]); cw_d = dten("cw", [DEPTH, 128, 44, 3])
    cb_d = dten("cb", [DEPTH, 128, 44]); wdn_d = dten("w_down", [DEPTH, D_FF, D])
    out_d = dten("out", [SL, D], kind="ExternalOutput")
    ob_d = {"lat": dten("ob_lat", [SL, 384], "Internal"), "ctx": dten("ob_ctx", [LC, 384], "Internal")}
    xmid_d = {"lat": dten("xmid_lat", [SL, D], "Internal"), "ctx": dten("xmid_ctx", [LC, D], "Internal")}
    x1_d = {"lat": dten("x1_lat", [SL, D], "Internal"), "ctx": dten("x1_ctx", [LC, D], "Internal")}
    pu_d = {"lat": dten("pu_lat", [128, 2, SL + 16], "Internal"), "ctx": dten("pu_ctx", [128, 2, LC + 16], "Internal")}

    st = ExitStack()
    S = Sched(nc)
    PERS_W, ARENA_W = 3904, 49280
    pers_t = st.enter_context(nc.sbuf_tensor("pers", [128, PERS_W], F32))
    arena_t = st.enter_context(nc.sbuf_tensor("arena", [128, ARENA_W], F32))
    PERS = Arena(pers_t[:, :]); AR = Arena(arena_t[:, :])
    psb = [st.enter_context(nc.psum_tensor(f"ps{i}", [128, 512], F32)) for i in range(8)]
    class PRing:
        def __init__(self, bufs):
            self.bufs = bufs
            self.par = 0
            self.i = [-1, -1]

        def set(self, par):
            self.par = par % 2

        def next(self):
            p = self.par
            self.i[p] = (self.i[p] + 1) % 4
            return self.bufs[4 * p + self.i[p]]

    PSR = PRing([B(psb[i][:, :], S.tok(f"ps{i}")) for i in range(8)])
    for b_ in PSR.bufs:
        b_.t.excl = True

    def newbuf(arena, shape, dtype=F32, name=None):
        return B(arena.alloc(shape, dtype), S.tok(name))

    def ring(arena, n, shape, dtype=F32, name="r"):
        return Ring([newbuf(arena, shape, dtype, f"{name}{i}") for i in range(n)])

    def toks(bs):
        return [b.t if isinstance(b, B) else b for b in bs]

    def mm(out, lhsT, rhs, R, W, start=True, stop=True):
        S.op("pe", lambda e: e.matmul(out, lhsT=lhsT, rhs=rhs, start=start, stop=stop), toks(R), toks(W))

    def tp(out, in_, R, W):
        idn = ident_bf if in_.dtype == BF16 else ident
        S.op("pe", lambda e: e.transpose(out, in_, idn.ap), toks(R) + [idn.t], toks(W))

    def bfv(pb):
        return pb.ap.bitcast(BF16)

    def act(out, in_, func, R, W, **kw):
        S.op("act", lambda e: e.activation(out=out, in_=in_, func=func, **kw), toks(R), toks(W))

    def tt(eng, out, in0, in1, op, R, W):
        S.op(eng, lambda e: e.tensor_tensor(out=out, in0=in0, in1=in1, op=op), toks(R), toks(W))

    def ts(eng, out, in0, s1, s2, op0, op1, R, W):
        if s2 is None:
            S.op(eng, lambda e: e.tensor_scalar(out=out, in0=in0, scalar1=s1, scalar2=None, op0=op0), toks(R), toks(W))
        else:
            S.op(eng, lambda e: e.tensor_scalar(out=out, in0=in0, scalar1=s1, scalar2=s2, op0=op0, op1=op1), toks(R), toks(W))

    def stt(eng, out, in0, scalar, in1, op0, op1, R, W):
        S.op(eng, lambda e: e.scalar_tensor_tensor(out=out, in0=in0, scalar=scalar, in1=in1, op0=op0, op1=op1), toks(R), toks(W))

    def cp(eng, out, in_, R, W):
        if eng == "act":
            act(out, in_, AF.Copy, R, W)
        else:
            S.op(eng, lambda e: e.tensor_copy(out=out, in_=in_), toks(R), toks(W))

    def memset(eng, ap, val, W):
        S.op(eng, lambda e: e.memset(ap, val), [], toks(W))

    def red(eng, out, in_, R, W):
        S.op(eng, lambda e: e.tensor_reduce(out=out, in_=in_, axis=AX.X, op=ALU.add), toks(R), toks(W))

    def recip(out, in_, R, W):
        S.op("dve", lambda e: e.reciprocal(out=out, in_=in_), toks(R), toks(W))

    def dma(eng, out, in_, R, W, key):
        S.op(eng, lambda e: e.dma_start(out=out, in_=in_), toks(R), toks(W), dma=True, key=key)

    def rstd_from_ss(ss, n, R_buf):
        ts("dve", ss, ss, 1.0 / n, EPS, ALU.mult, ALU.add, [R_buf], [R_buf])
        act(ss, ss, AF.Ln, [R_buf], [R_buf])
        act(ss, ss, AF.Exp, [R_buf], [R_buf], scale=-0.5)

    ident = newbuf(PERS, [128], F32, "ident"); ident.ap = ident.ap
    tri = newbuf(PERS, [2, 128]); maskb = newbuf(PERS, [2, 128]); pinv = newbuf(PERS, [2]); pedge = newbuf(PERS, [2, 2, 8])
    cact = newbuf(PERS, [8, 2]); onesr = newbuf(PERS, [128]); zeros = newbuf(PERS, [144])
    MODT = newbuf(PERS, [48, 2]); badaT = newbuf(PERS, [48]); n1T = newbuf(PERS, [8]); n2T = newbuf(PERS, [8])
    G1T = newbuf(PERS, [8, 2]); G2T = newbuf(PERS, [8, 2])
    glag = newbuf(PERS, [384]); qg = newbuf(PERS, [384])
    kg = newbuf(PERS, [128]); esink = newbuf(PERS, [6]); pscT = newbuf(PERS, [2]); cw = newbuf(PERS, [44, 3]); cb = newbuf(PERS, [44])
    pw = newbuf(PERS, [2, 128], BF16)
    Sst = {d: newbuf(PERS, [2, 96]) for d in "fb"}
    Sbf = {d: newbuf(PERS, [2, 96], BF16) for d in "fb"}
    small = Ring([newbuf(PERS, [16], F32, f"small{i}") for i in range(8)])

    dma("sp", ident.ap, ident_d, [], [ident], "c_ident")
    ident_bf = newbuf(PERS, [128], BF16, "ident_bf")
    cp("dve", ident_bf.ap, ident.ap, [ident], [ident_bf])
    dma("sp", tri.ap, tri_d, [], [tri], "c_tri")
    dma("sp", maskb.ap, mask_d, [], [maskb], "c_mask")
    dma("sp", pinv.ap, pinv_d, [], [pinv], "c_pinv")
    dma("sp", pedge.ap, pedge_d, [], [pedge], "c_pedge")
    dma("sp", cact.ap, cT_d, [], [cact], "c_cact")
    memset("dve", onesr.ap, 1.0, [onesr])
    memset("dve", zeros.ap, 0.0, [zeros])
    act(cact.ap, cact.ap, AF.Silu, [cact], [cact])
    for sname, n in (("lat", SL), ("ctx", LC)):
        dma("sp", pu_d[sname][:, :, 0:8], zeros.ap[:, 0:16].rearrange("p (a b) -> p a b", a=2), [zeros], [], "c_pz")
        dma("sp", pu_d[sname][:, :, 8 + n:16 + n], zeros.ap[:, 0:16].rearrange("p (a b) -> p a b", a=2), [zeros], [], "c_pz")

    seqs = {"lat": dict(name="lat", n=SL, nb=SL // 128, w=0, rope=True),
            "ctx": dict(name="ctx", n=LC, nb=LC // 128, w=1, rope=False)}

    for l in range(DEPTH):
        last = (l == DEPTH - 1)
        x_src = {"lat": x_d if l == 0 else x1_d["lat"], "ctx": ctx_d if l == 0 else x1_d["ctx"]}
        x_dst = {"lat": out_d if last else x1_d["lat"], "ctx": x1_d["ctx"]}
        S.barrier()
        AR.reset()
        G2B = newbuf(AR, [2, D], F32, "G2B")
        gbmark = AR.off
        G1B = newbuf(AR, [2, D], F32, "G1B")
        KT = {"lat": newbuf(AR, [SL], BF16, "KT"), "ctx": newbuf(AR, [LC], BF16, "KTc")}
        VA = {"lat": newbuf(AR, [SL // 128, 2, 65], BF16, "VA"), "ctx": newbuf(AR, [LC // 128, 2, 65], BF16, "VAc")}
        WIN = newbuf(AR, [8, NW], BF16, "WIN"); WOUT = newbuf(AR, [8, D], BF16, "WOUT")
        mark = AR.off
        for bufx, src in ((badaT, badaT_d[l]), (n1T, n1T_d[l]), (n2T, n2T_d[l]), (glag, glag_d[l]), (qg, qg_d[l]),
                          (kg, kg_d[l]), (esink, sink_d[l]), (pscT, pscT_d[l]), (cw, cw_d[l]), (cb, cb_d[l])):
            dma("sp", bufx.ap, src, [], [bufx], "c_small")
        dma("pool", pw.ap, pw_d[l], [], [pw], "c_pw")
        act(esink.ap, esink.ap, AF.Exp, [esink], [esink])
        if STOP_AT <= 0.1:
            break
        memset("pool", WIN.ap, 0.0, [WIN])
        wv = win_d[l].rearrange("(k p) n -> p k n", p=128)

        def stage(dst0, src0, n):
            dma("pool", WIN.ap[:, :, dst0:dst0 + n], wv[:, :, src0:src0 + n], [], [WIN], "w_in")
        for h in range(4):
            stage(C_Q + 64 * h, 48 * h, 48)
            stage(C_K + 64 * h, 192 + 48 * h, 48)
        stage(C_Z, 1152, 16); stage(C_Z + 32, 1168, 16)
        stage(C_PU, 1824, 256)
        stage(C_GV, 384, 384); stage(C_GG, 768, 384)
        for i, h in enumerate((0, 3, 1, 4, 2, 5)):
            stage(C_AQ + 64 * i, 1184 + 64 * h, 64)
        stage(C_AK, 1568, 256)
        dma("pool", WOUT.ap, wout_d[l].rearrange("(k p) n -> p k n", p=128), [], [WOUT], "w_out")
        if STOP_AT <= 0.2:
            break
        crep = newbuf(AR, [8, 2, 128], F32, "crep")
        brow = newbuf(AR, [6 * D], F32, "brow")
        dma("sp", brow.ap[0:1, :], badar_d[l], [], [brow], "c_brow")
        for k in range(8):
            for w in range(2):
                cp("dve", crep.ap[:, k, w, :], cact.ap[:, k, w:w + 1].to_broadcast([128, 128]), [cact], [crep])
        wst = ring(AR, 2, [8, 512], F32, "wst")
        for cbk in range(12):
            if cbk == 6 and os.environ.get('MK_NOBAR', '') == '':
                S.barrier()
            wb = wst.next()
            dma("sp", wb.ap, wada_d[l][:, cbk * 512:(cbk + 1) * 512].rearrange("(k p) n -> p k n", p=128), [], [wb], f"wst{wst.i}")
            which, half = cbk // 2, cbk % 2
            if os.environ.get('MK_SKIP', '') == 'C' and cbk >= 6:
                continue
            if os.environ.get('MK_SKIP', '') == 'D' and cbk < 6:
                continue
            if os.environ.get('MK_SKIP', '') == 'A' and which in (2, 5):
                continue
            if os.environ.get('MK_SKIP', '') == 'B' and which not in (2, 5):
                continue
            if which in (2, 5):
                GB = G1B if which == 2 else G2B
                for w in range(2):
                    pb = PSR.next()
                    for k in range(8):
                        mm(pb.ap, crep.ap[:, k, w, :], wb.ap[:, k, :], [crep, wb], [pb], start=(k == 0), stop=False)
                    mm(pb.ap, onesr.ap[0:1, :], brow.ap[0:1, cbk * 512:(cbk + 1) * 512], [onesr, brow], [pb], start=False, stop=True)
                    cp("act", GB.ap[:, w, half * 512:(half + 1) * 512], pb.ap, [pb], [GB])
            else:
                pb = PSR.next()
                for j in range(4):
                    for k in range(8):
                        mm(pb.ap[:, j * 2:j * 2 + 2], wb.ap[:, k, j * 128:(j + 1) * 128], cact.ap[:, k, :], [wb, cact], [pb],
                           start=(k == 0), stop=(k == 7))
                j0 = cbk * 4
                tt("dve", MODT.ap[:, j0:j0 + 4, :], pb.ap[:, 0:8].rearrange("p (j w) -> p j w", j=4),
                   badaT.ap[:, j0:j0 + 4, None].to_broadcast([128, 4, 2]), ALU.add, [pb, badaT], [MODT])
        ts("dve", G1T.ap, MODT.ap[:, 8:16, :], 1.0, None, ALU.add, None, [MODT], [G1T])
        tt("dve", G1T.ap, G1T.ap, n1T.ap[:, :, None].to_broadcast([128, 8, 2]), ALU.mult, [G1T, n1T], [G1T])
        ts("dve", G2T.ap, MODT.ap[:, 32:40, :], 1.0, None, ALU.add, None, [MODT], [G2T])
        tt("dve", G2T.ap, G2T.ap, n2T.ap[:, :, None].to_broadcast([128, 8, 2]), ALU.mult, [G2T, n2T], [G2T])
        SH1T = MODT.ap[:, 0:8, :]
        SH2T = MODT.ap[:, 24:32, :]

        S.barrier()
        AR.reset(mark)
        xs_r = ring(AR, 3, [D], F32, "xs"); xn = newbuf(AR, [D], BF16); junk = newbuf(AR, [D], BF16); hT_r = ring(AR, 2, [8, 128], BF16, "hT")
        PT = newbuf(AR, [5, 6, 128], BF16); rp_r = ring(AR, 2, [128], F32, "rp")
        qn = newbuf(AR, [384]); qsq = newbuf(AR, [384]); qr = newbuf(AR, [384]); qT = newbuf(AR, [3, 128], BF16); oatt = newbuf(AR, [384], BF16)
        kn = B(qn.ap[:, 0:128], qn.t); ksq = B(qsq.ap[:, 0:128], qsq.t); kr = B(qr.ap[:, 0:128], qr.t); ob_sq = qsq
        qrb = newbuf(AR, [384], BF16); krb = B(qrb.ap[:, 0:128], qrb.t)
        puT = newbuf(AR, [2, 128]); puw = newbuf(AR, [2, 144]); s2 = newbuf(AR, [2, 144]); s4 = newbuf(AR, [2, 144]); s8 = newbuf(AR, [2, 144])
        s16 = newbuf(AR, [2, 144]); tmp8 = newbuf(AR, [8])
        ytmp = newbuf(AR, [D]); xm_r = ring(AR, 2, [D], F32, "xm")
        qk_sb = newbuf(AR, [512], BF16); zpu_sb = newbuf(AR, [320])
        WSETS = []
        for _ws in range(2):
            w_ = {}
            w_["zT"] = newbuf(AR, [128]); w_["e1"] = newbuf(AR, [256]); w_["spb"] = newbuf(AR, [256]); w_["epos"] = newbuf(AR, [256]); w_["eneg"] = newbuf(AR, [256])
            w_["qdT"] = newbuf(AR, [256], BF16); w_["kiT"] = newbuf(AR, [256], BF16); w_["keT"] = newbuf(AR, [256], BF16); w_["ke"] = newbuf(AR, [256], BF16)
            w_["scT"] = newbuf(AR, [4, 128], BF16); w_["vbf"] = newbuf(AR, [384], BF16); w_["obl"] = newbuf(AR, [384]); w_["ob_o"] = newbuf(AR, [384])
            w_["sgb"] = newbuf(AR, [384]); w_["dT"] = newbuf(AR, [2, 128], BF16); w_["mixT"] = newbuf(AR, [8, 128], BF16)
            w_["obst"] = w_["ob_o"]
            w_["ob_bf"] = newbuf(AR, [384], BF16)
            WSETS.append(w_)
        zT = e1 = spb = epos = eneg = qdT = kiT = keT = ke = scT = vbf = obl = ob_o = sgb = dT = mixT = obst = ob_bf = None

        def select(slot):
            nonlocal zT, e1, spb, epos, eneg, qdT, kiT, keT, ke, scT, vbf, obl, ob_o, sgb, dT, mixT, obst, ob_bf
            w_ = WSETS[slot % 2]
            zT, e1, spb, epos, eneg = w_["zT"], w_["e1"], w_["spb"], w_["epos"], w_["eneg"]
            qdT, kiT, keT, ke, scT, vbf = w_["qdT"], w_["kiT"], w_["keT"], w_["ke"], w_["scT"], w_["vbf"]
            obl, ob_o, sgb, dT, mixT, obst = w_["obl"], w_["ob_o"], w_["sgb"], w_["dT"], w_["mixT"], w_["obst"]
            ob_bf = w_["ob_bf"]
        select(0)
        print("ARENA mixer end", AR.off, "of", AR.size)
        wdec = newbuf(AR, [2, 256]); bdec = newbuf(AR, [2, 256])
        dma("sp", wdec.ap[0:48, :, :], wdec_d[l], [], [wdec], "c_wdec")
        dma("sp", bdec.ap[0:1, :, :], bdec_d[l], [], [bdec], "c_wdec")
        for sname in ("lat", "ctx"):
            memset("pool", VA[sname].ap[:, :, :, 64:65], 1.0, [VA[sname]])

        def front(src_ap, GT, SHT, w, dst_hT=None, col0=0, pre=None, xb_pre=None):
            if xb_pre is not None:
                xb = xb_pre
            else:
                xb = xs_r.next()
                if pre is not None:
                    pre(xb)
                else:
                    dma("sp", xb.ap, src_ap, [], [xb], f"xs{xs_r.i}")
            sm = small.next()
            memset("dve", sm.ap[:, 0:1], 0.0, [sm])
            act(junk.ap, xb.ap, AF.Square, [xb, sm], [junk, sm], accum_out=sm.ap[:, 0:1])
            rstd_from_ss(sm.ap[:, 0:1], D, sm)
            ts("dve", xn.ap, xb.ap, sm.ap[:, 0:1], None, ALU.mult, None, [xb, sm], [xn])
            if dst_hT is None:
                hT = hT_r.next()
            else:
                hT = dst_hT
            for half in range(2):
                pb = PSR.next()
                pbv = bfv(pb)
                for j in range(4):
                    k = half * 4 + j
                    tp(pbv[:, j * 128:(j + 1) * 128], xn.ap[:, k * 128:(k + 1) * 128], [xn], [pb])
                for j in range(4):
                    k = half * 4 + j
                    act(hT.ap[:, k, col0:col0 + 128], pbv[:, j * 128:(j + 1) * 128], AF.Identity, [pb, GT, MODT], [hT],
                        scale=GT.ap[:, k, w:w + 1], bias=SHT[:, k, w:w + 1])
            return xb, hT

        def projF(hT, col0, M, out, pb):
            for k in range(8):
                mm(out, WIN.ap[:, k, col0:col0 + M], hT.ap[:, k, :], [WIN, hT], [pb], start=(k == 0), stop=(k == 7))

        def projT(hT, col0, N, out, pb):
            for k in range(8):
                mm(out, hT.ap[:, k, :], WIN.ap[:, k, col0:col0 + N], [WIN, hT], [pb], start=(k == 0), stop=(k == 7))

        def qk_proj(hT, do_out):
            pqk = PSR.next()
            if do_out:
                projT(hT, C_Q, 512, pqk.ap[:, 0:512], pqk)
                cp("act", qk_sb.ap, pqk.ap, [pqk], [qk_sb])
            else:
                projT(hT, C_K, 256, pqk.ap[:, 256:512], pqk)
                cp("act", qk_sb.ap[:, 256:512], pqk.ap[:, 256:512], [pqk], [qk_sb])
            pq = PSR.next()
            for c in range(4):
                if c < 2 and not do_out:
                    continue
                tp(bfv(pq)[:, c * 128:(c + 1) * 128], qk_sb.ap[:, c * 128:(c + 1) * 128], [qk_sb], [pq])
            return B(bfv(pq)[:, 0:512], pq.t)

        def rope(src, dst, tmp, H, rp, outb):
            tt("dve", tmp.ap.rearrange("p (h d) -> p h d", h=H), src.ap.rearrange("p (h d) -> p h d", h=H),
               rp.ap[:, None, 0:64].to_broadcast([128, H, 64]), ALU.mult, [src, rp], [tmp])
            s5 = src.ap.rearrange("p (h a f c) -> p h a f c", h=H, a=2, f=2)
            d5 = dst.ap.rearrange("p (h a f c) -> p h a f c", h=H, a=2, f=2)
            sneg = rp.ap[:, 64:96].rearrange("p (a c) -> p a c", a=2)[:, None, :, :].to_broadcast([128, H, 2, 16])
            spos = rp.ap[:, 96:128].rearrange("p (a c) -> p a c", a=2)[:, None, :, :].to_broadcast([128, H, 2, 16])
            tt("dve", d5[:, :, :, 0, :], s5[:, :, :, 1, :], sneg, ALU.mult, [src, rp], [dst])
            tt("dve", d5[:, :, :, 1, :], s5[:, :, :, 0, :], spos, ALU.mult, [src, rp], [dst])
            tt("dve", outb.ap, dst.ap, tmp.ap, ALU.add, [dst, tmp], [outb])

        def headnorm(buf, sq, H, dh, gbuf):
            sm = small.next()
            tt("dve", sq.ap, buf.ap, buf.ap, ALU.mult, [buf], [sq])
            red("dve", sm.ap[:, 0:H], sq.ap.rearrange("p (h d) -> p h d", h=H), [sq], [sm])
            rstd_from_ss(sm.ap[:, 0:H], dh, sm)
            tt("dve", buf.ap.rearrange("p (h d) -> p h d", h=H), buf.ap.rearrange("p (h d) -> p h d", h=H),
               sm.ap[:, 0:H, None].to_broadcast([128, H, dh]), ALU.mult, [buf, sm], [buf])
            tt("dve", buf.ap, buf.ap, gbuf.ap, ALU.mult, [buf, gbuf], [buf])

        def gla_block(d, pq, pz, do_out):
            di = 0 if d == "f" else 1
            zr = 0 if d == "f" else 32
            col = 127 if d == "f" else 0
            cp("act", zT.ap[0:48, :], pz.ap[0:48, 0:128], [pz], [zT])
            pl = PSR.next()
            mm(pl.ap[:, 0:256], zT.ap[0:48, :], wdec.ap[0:48, di, :], [zT, wdec], [pl], start=True, stop=False)
            mm(pl.ap[:, 0:256], onesr.ap[0:1, :], bdec.ap[0:1, di, :], [onesr, bdec], [pl], start=False, stop=True)
            act(e1.ap, pl.ap[:, 0:256], AF.Exp, [pl], [e1], scale=-1.0)
            ts("dve", e1.ap, e1.ap, 1.0, None, ALU.add, None, [e1], [e1])
            act(spb.ap, e1.ap, AF.Ln, [e1], [spb])
            if STOP_AT <= 1.41:
                return None
            pbT = PSR.next()
            for pr in range(2):
                mm(pbT.ap[:, pr * 128:(pr + 1) * 128], spb.ap[:, pr * 128:(pr + 1) * 128], tri.ap[:, di, :], [spb, tri], [pbT])
            act(epos.ap, pbT.ap[:, 0:256], AF.Exp, [pbT], [epos])
            act(eneg.ap, pbT.ap[:, 0:256], AF.Exp, [pbT], [eneg], scale=-1.0)
            if do_out:
                stt("dve", qdT.ap, pq.ap[:, 0:256], DK ** -0.5, epos.ap, ALU.mult, ALU.mult, [pq, epos], [qdT])
                tt("dve", kiT.ap, pq.ap[:, 256:512], eneg.ap, ALU.mult, [pq, eneg], [kiT])
            for pr in range(2):
                stt("dve", keT.ap[:, pr * 128:(pr + 1) * 128], eneg.ap[:, pr * 128:(pr + 1) * 128],
                    epos.ap[:, pr * 128 + col:pr * 128 + col + 1], pq.ap[:, 256 + pr * 128:384 + pr * 128], ALU.mult, ALU.mult,
                    [eneg, epos, pq], [keT])
            pke = PSR.next()
            for pr in range(2):
                tp(bfv(pke)[:, pr * 128:(pr + 1) * 128], keT.ap[:, pr * 128:(pr + 1) * 128], [keT], [pke])
            cp("act", ke.ap, bfv(pke)[:, 0:256], [pke], [ke])
            po = None
            if STOP_AT <= 1.42:
                return None
            if do_out:
                psc = [PSR.next(), PSR.next()]
                for h in range(4):
                    pr, par, base = h // 2, h % 2, 64 * (h % 2)
                    mm(psc[par].ap[:, pr * 128:(pr + 1) * 128], kiT.ap[base:base + 48, pr * 128:(pr + 1) * 128],
                       qdT.ap[base:base + 48, pr * 128:(pr + 1) * 128], [kiT, qdT], [psc[par]])
                for par in range(2):
                    tt("dve", scT.ap[:, 2 * par:2 * par + 2, :], psc[par].ap[:, 0:256].rearrange("p (h i) -> p h i", h=2),
                       maskb.ap[:, di:di + 1, :].to_broadcast([128, 2, 128]), ALU.mult, [psc[par], maskb], [scT])
                if STOP_AT <= 1.425:
                    return None
                po = PSR.next()
                po2 = [PSR.next(), PSR.next()]
                for h in range(4):
                    pr, par, base = h // 2, h % 2, 64 * (h % 2)
                    mm(po.ap[:, h * 96:(h + 1) * 96], scT.ap[:, 2 * par + pr, :], vbf.ap[:, h * 96:(h + 1) * 96], [scT, vbf], [po], start=True, stop=True)
                for h in range(4):
                    pr, par, base = h // 2, h % 2, 64 * (h % 2)
                    mm(po2[par].ap[:, h * 96:(h + 1) * 96], qdT.ap[base:base + 48, pr * 128:(pr + 1) * 128], Sbf[d].ap[base:base + 48, pr, :],
                       [qdT, Sbf[d]], [po2[par]], start=True, stop=True)
                po = (po, po2[0], po2[1])
            if STOP_AT <= 1.43:
                return None
            pup = PSR.next()
            for pr in range(2):
                mm(pup.ap[:, pr * 192:(pr + 1) * 192], ke.ap[:, pr * 128:(pr + 1) * 128], vbf.ap[:, pr * 192:(pr + 1) * 192], [ke, vbf], [pup])
            for h in range(4):
                pr, base = h // 2, 64 * (h % 2)
                stt("dve", Sst[d].ap[base:base + 48, pr, :], Sst[d].ap[base:base + 48, pr, :],
                    epos.ap[base:base + 48, pr * 128 + col:pr * 128 + col + 1],
                    pup.ap[base:base + 48, pr * 192 + (h % 2) * 96:pr * 192 + (h % 2) * 96 + 96], ALU.mult, ALU.add,
                    [Sst[d], epos, pup], [Sst[d]])
            cp("pool", Sbf[d].ap, Sst[d].ap, [Sst[d]], [Sbf[d]])
            return po

        XPRE = {}

        def prefetch_x(name, n):
            xb = xs_r.next()
            dma("sp", xb.ap, x_src[name][n * 128:(n + 1) * 128, :], [], [xb], f"xs{xs_r.i}")
            XPRE[n] = xb

        def pass_B(sq, do_out):
            name, nb, w = sq["name"], sq["nb"], sq["w"]
            lists = []
            order = list(reversed(range(nb)))
            XPRE.clear()
            prefetch_x(name, order[0])
            for i_, n in enumerate(order):
                select(n)
                PSR.set(n)
                S.cur = []
                lists.append(S.cur)
                if i_ + 1 < len(order):
                    prefetch_x(name, order[i_ + 1])
                pass_B_block(sq, do_out, n)
                S.cur = None
            S.merge_threads(lists)

        def pass_B_block(sq, do_out, n):
            name, nb, w = sq["name"], sq["nb"], sq["w"]
            if True:
                if STOP_AT <= 1.1:
                    return
                xb, hT = front(None, G1T, SH1T, w, xb_pre=XPRE[n])
                pk = PSR.next()
                projT(hT, C_AK, 256, pk.ap[:, 0:256], pk)
                cp("act", kn.ap, pk.ap[:, 0:128], [pk], [kn])
                cp("act", VA[name].ap[:, n, :, 0:64], pk.ap[:, 128:256].rearrange("p (g d) -> p g d", g=2), [pk], [VA[name]])
                pv = PSR.next()
                projT(hT, C_GV, 384, pv.ap[:, 0:384], pv)
                cp("act", vbf.ap, pv.ap[:, 0:384], [pv], [vbf])
                headnorm(kn, ksq, 2, 64, kg)
                if sq["rope"]:
                    rp = rp_r.next()
                    dma("sp", rp.ap, rope_d[n * 128:(n + 1) * 128, :], [], [rp], f"rp{rp_r.i}")
                    rope(kn, kr, ksq, 2, rp, krb)
                else:
                    cp("dve", krb.ap, kn.ap, [kn], [krb])
                pkt = PSR.next()
                tp(bfv(pkt)[:, 0:128], krb.ap, [krb], [pkt])
                cp("act", KT[name].ap[:, n * 128:(n + 1) * 128], bfv(pkt)[:, 0:128], [pkt], [KT[name]])
                nz = 320 if do_out else 64
                pzt = PSR.next()
                projT(hT, C_Z, nz, pzt.ap[:, 0:nz], pzt)
                cp("act", zpu_sb.ap[:, 0:nz], pzt.ap[:, 0:nz], [pzt], [zpu_sb])
                pz = PSR.next()
                tp(pz.ap[0:64, 0:128], zpu_sb.ap[:, 0:64], [zpu_sb], [pz])
                if do_out:
                    tp(pz.ap[:, 128:256], zpu_sb.ap[:, 64:192], [zpu_sb], [pz])
                    tp(pz.ap[:, 256:384], zpu_sb.ap[:, 192:320], [zpu_sb], [pz])
                    cp("act", puT.ap, pz.ap[:, 128:384].rearrange("p (c t) -> p c t", c=2), [pz], [puT])
                    dma("sp", pu_d[name][:, :, 8 + n * 128:8 + (n + 1) * 128], puT.ap, [puT], [], "st_pu")
                pq = qk_proj(hT, do_out)
                po = gla_block("b", pq, pz, do_out)
                if STOP_AT <= 1.5:
                    return
                if do_out:
                    cp("act", obst.ap, po[0].ap[:, 0:384], [po[0]], [obst])
                    for par in range(2):
                        ov = obst.ap.rearrange("p (pr x) -> p pr x", pr=2)[:, :, par * 96:(par + 1) * 96]
                        pv2 = po[1 + par].ap[:, 0:384].rearrange("p (pr x) -> p pr x", pr=2)[:, :, par * 96:(par + 1) * 96]
                        tt("dve", ov, ov, pv2, ALU.add, [obst, po[1 + par]], [obst])
                    dma("sp", ob_d[name][n * 128:(n + 1) * 128, :], obst.ap, [obst], [], "st_ob")

        def pass_F(sq, do_out):
            name, nb, w = sq["name"], sq["nb"], sq["w"]
            lists = []
            order = list(range(nb))
            XPRE.clear()
            prefetch_x(name, order[0])
            for i_, n in enumerate(order):
                select(n)
                PSR.set(n)
                S.cur = []
                lists.append(S.cur)
                if i_ + 1 < len(order):
                    prefetch_x(name, order[i_ + 1])
                pass_F_block(sq, do_out, n)
                S.cur = None
            S.merge_threads(lists)

        def pass_F_block(sq, do_out, n):
            name, nb, w = sq["name"], sq["nb"], sq["w"]
            if True:
                xb, hT = front(None, G1T, SH1T, w, xb_pre=XPRE[n])
                pv = PSR.next()
                projT(hT, C_GV, 384, pv.ap[:, 0:384], pv)
                cp("act", vbf.ap, pv.ap[:, 0:384], [pv], [vbf])
                if do_out:
                    pg = PSR.next()
                    projT(hT, C_GG, 384, pg.ap[:, 0:384], pg)
                    act(sgb.ap, pg.ap[:, 0:384], AF.Silu, [pg], [sgb])
                    pa = PSR.next()
                    projT(hT, C_AQ, 384, pa.ap[:, 0:384], pa)
                    cp("act", qn.ap, pa.ap[:, 0:384], [pa], [qn])
                pz = PSR.next()
                projF(hT, C_Z, 48, pz.ap[0:48, 0:128], pz)
                pq = qk_proj(hT, do_out)
                if not do_out:
                    gla_block("f", pq, pz, False)
                    return
                dma("sp", obl.ap, ob_d[name][n * 128:(n + 1) * 128, :], [], [obl], "ld_ob")
                po = gla_block("f", pq, pz, True)
                tt("dve", ob_o.ap, po[0].ap[:, 0:384], obl.ap, ALU.add, [po[0], obl], [ob_o])
                for par in range(2):
                    ov = ob_o.ap.rearrange("p (pr x) -> p pr x", pr=2)[:, :, par * 96:(par + 1) * 96]
                    pv2 = po[1 + par].ap[:, 0:384].rearrange("p (pr x) -> p pr x", pr=2)[:, :, par * 96:(par + 1) * 96]
                    tt("dve", ov, ov, pv2, ALU.add, [ob_o, po[1 + par]], [ob_o])
                headnorm(ob_o, ob_sq, 4, 96, glag)
                tt("dve", ob_bf.ap, ob_o.ap, sgb.ap, ALU.mult, [ob_o, sgb], [ob_bf])
                pmt = PSR.next()
                for c in range(3):
                    tp(bfv(pmt)[:, c * 128:(c + 1) * 128], ob_bf.ap[:, c * 128:(c + 1) * 128], [ob_bf], [pmt])
                cp("act", mixT.ap[:, 0:3, :], bfv(pmt)[:, 0:384].rearrange("p (c t) -> p c t", c=3), [pmt], [mixT])
                headnorm(qn, qsq, 6, 64, qg)
                if sq["rope"]:
                    rp = rp_r.next()
                    dma("sp", rp.ap, rope_d[n * 128:(n + 1) * 128, :], [], [rp], f"rp{rp_r.i}")
                    rope(qn, qr, qsq, 6, rp, qrb)
                else:
                    cp("dve", qrb.ap, qn.ap, [qn], [qrb])
                pqt = PSR.next()
                for c in range(3):
                    tp(bfv(pqt)[:, c * 128:(c + 1) * 128], qrb.ap[:, c * 128:(c + 1) * 128], [qrb], [pqt])
                cp("act", qT.ap, bfv(pqt)[:, 0:384].rearrange("p (c t) -> p c t", c=3), [pqt], [qT])
                kbs = []
                if name == "lat":
                    if n > 0:
                        kbs.append(("lat", n - 1, 1))
                    kbs.append(("lat", n, None))
                    if n < nb - 1:
                        kbs.append(("lat", n + 1, 0))
                for cbk in range(LC // 128):
                    kbs.append(("ctx", cbk, None))
                for ki, (ks, kn_, mk) in enumerate(kbs):
                    for g in range(2):
                        ps_ = PSR.next()
                        mm(ps_.ap[:, 0:384].rearrange("p (h i) -> p h i", h=3), KT[ks].ap[g * 64:(g + 1) * 64, kn_ * 128:(kn_ + 1) * 128],
                           qT.ap[g * 64:(g + 1) * 64, :, :], [KT[ks], qT], [ps_])
                        act(PT.ap[:, ki, 3 * g:3 * g + 3, :], ps_.ap[:, 0:384].rearrange("p (h i) -> p h i", h=3), AF.Exp, [ps_], [PT], scale=0.125)
                        if mk is not None:
                            tt("dve", PT.ap[:, ki, 3 * g:3 * g + 3, :], PT.ap[:, ki, 3 * g:3 * g + 3, :],
                               maskb.ap[:, mk:mk + 1, :].to_broadcast([128, 3, 128]), ALU.mult, [PT, maskb], [PT])
                pov = PSR.next()
                for h in range(6):
                    g = h // 3
                    for ki, (ks, kn_, mk) in enumerate(kbs):
                        mm(pov.ap[:, h * 65:(h + 1) * 65], PT.ap[:, ki, h, :], VA[ks].ap[:, kn_, g, :], [PT, VA[ks]], [pov],
                           start=(ki == 0), stop=(ki == len(kbs) - 1))
                sm = small.next()
                pov3 = pov.ap[:, 0:390].rearrange("p (h e) -> p h e", e=65)
                tt("dve", sm.ap[:, 0:6], pov3[:, :, 64], esink.ap, ALU.add, [pov, esink], [sm])
                recip(sm.ap[:, 0:6], sm.ap[:, 0:6], [sm], [sm])
                tt("dve", oatt.ap.rearrange("p (h d) -> p h d", h=6), pov3[:, :, 0:64], sm.ap[:, 0:6, None].to_broadcast([128, 6, 64]),
                   ALU.mult, [pov, sm], [oatt])
                pat = PSR.next()
                for c in range(3):
                    tp(bfv(pat)[:, c * 128:(c + 1) * 128], oatt.ap[:, c * 128:(c + 1) * 128], [oatt], [pat])
                cp("act", mixT.ap[:, 3:6, :], bfv(pat)[:, 0:384].rearrange("p (c t) -> p c t", c=3), [pat], [mixT])
                dma("sp", puw.ap, pu_d[name][:, :, n * 128:n * 128 + 144], [], [puw], "ld_pu")
                tt("pool", s2.ap[:, :, 1:144], puw.ap[:, :, 0:143], puw.ap[:, :, 1:144], ALU.add, [puw], [s2])
                tt("pool", s4.ap[:, :, 2:143], s2.ap[:, :, 1:142], s2.ap[:, :, 3:144], ALU.add, [s2], [s4])
                tt("pool", s8.ap[:, 1, 4:141], s4.ap[:, 1, 2:139], s4.ap[:, 1, 6:143], ALU.add, [s4], [s8])
                tt("pool", s16.ap[:, 1, 8:136], s8.ap[:, 1, 4:132], s8.ap[:, 1, 12:140], ALU.add, [s8], [s16])
                combos = ((0, 64, 0, s2), (64, 128, 0, s4), (0, 64, 1, s8), (64, 128, 1, s16))
                for (r0, r1, c, sb_) in combos:
                    stt("dve", dT.ap[r0:r1, c, :], sb_.ap[r0:r1, c, 8:136], pinv.ap[r0:r1, c:c + 1], puw.ap[r0:r1, c, 8:136],
                        ALU.mult, ALU.subtract, [sb_, pinv, puw], [dT])
                for edge, cols, dcols in ((0, slice(8, 16), slice(0, 8)), (1, slice(128, 136), slice(120, 128))):
                    if (edge == 0 and n == 0) or (edge == 1 and n == nb - 1):
                        for (r0, r1, c, sb_) in combos:
                            tt("pool", tmp8.ap[r0:r1, :], sb_.ap[r0:r1, c, cols], pedge.ap[r0:r1, edge, c, :], ALU.mult, [sb_, pedge], [tmp8])
                            tt("pool", dT.ap[r0:r1, c, dcols], tmp8.ap[r0:r1, :], puw.ap[r0:r1, c, cols], ALU.subtract, [tmp8, puw], [dT])
                pp = PSR.next()
                for c in range(2):
                    mm(pp.ap[:, c * 128:(c + 1) * 128], pw.ap[:, c, :], dT.ap[:, c, :], [pw, dT], [pp])
                for c in range(2):
                    act(mixT.ap[:, 6 + c, :], pp.ap[:, c * 128:(c + 1) * 128], AF.Copy, [pp, pscT], [mixT], scale=pscT.ap[:, c:c + 1])
                xm = xm_r.next()
                for half in range(2):
                    py = PSR.next()
                    for kc in range(8):
                        mm(py.ap, mixT.ap[:, kc, :], WOUT.ap[:, kc, half * 512:(half + 1) * 512], [mixT, WOUT], [py], start=(kc == 0), stop=(kc == 7))
                    hs = slice(half * 512, (half + 1) * 512)
                    tt("dve", ytmp.ap[:, hs], py.ap, G1B.ap[:, w, hs], ALU.mult, [py, G1B], [ytmp])
                    tt("pool", xm.ap[:, hs], ytmp.ap[:, hs], xb.ap[:, hs], ALU.add, [ytmp, xb], [xm])
                dma("sp", xmid_d[name][n * 128:(n + 1) * 128, :], xm.ap, [xm], [], f"st_xm{xm_r.i}")

        for d in "fb":
            memset("dve", Sst[d].ap, 0.0, [Sst[d]])
            memset("pool", Sbf[d].ap, 0.0, [Sbf[d]])
        ctx_out = not last
        if STOP_AT <= 1:
            break
        pass_B(seqs["ctx"], ctx_out)
        if STOP_AT <= 2:
            break
        pass_F(seqs["ctx"], ctx_out)
        if STOP_AT <= 3:
            break
        pass_B(seqs["lat"], True)
        if STOP_AT <= 4:
            break
        pass_F(seqs["lat"], True)
        if STOP_AT <= 5:
            break

        S.barrier()
        AR.reset(gbmark)
        WUP = newbuf(AR, [8, 2 * D_FF], BF16, "WUP"); WDN = newbuf(AR, [22, D], BF16, "WDN")
        wuv = wup_d[l].rearrange("(k p) n -> p k n", p=128)
        wdv = wdn_d[l].rearrange("(k p) n -> p k n", p=128)
        for k in range(8):
            for cc in range(11):
                dma("pool", WUP.ap[:, k, cc * 512:(cc + 1) * 512], wuv[:, k, cc * 512:(cc + 1) * 512], [], [WUP], "w_up")
        for k in range(22):
            for cc in range(2):
                dma("pool", WDN.ap[:, k, cc * 512:(cc + 1) * 512], wdv[:, k, cc * 512:(cc + 1) * 512], [], [WDN], "w_dn")
        xs_r = ring(AR, 2, [D], F32, "xs"); xn = newbuf(AR, [D], BF16); junk = newbuf(AR, [D], BF16)
        h2T_r = [newbuf(AR, [8, 256], BF16, "h2Ta"), newbuf(AR, [8, 256], BF16, "h2Tb")]; actT = newbuf(AR, [22, 256], BF16, "actT")
        cva = ring(AR, 3, [256], F32, "cva"); cvb = ring(AR, 1, [256], F32, "cvb"); cga = ring(AR, 3, [256], F32, "cga")
        cgb = ring(AR, 1, [256], F32, "cgb"); csg = ring(AR, 2, [256], F32, "csg")
        xr_r = ring(AR, 1, [D], F32, "xr"); yt_r = ring(AR, 1, [D], F32, "yt")
        print("ARENA ffn end", AR.off, "of", AR.size)

        def ffn_pass(sq):
            name, n_tok, w = sq["name"], sq["n"], sq["w"]
            starts = sorted(set(min(FT * j, n_tok - FT) for j in range((n_tok + FT - 1) // FT)))
            tiles = []
            for ti, s in enumerate(starts):
                h2T = h2T_r[ti % 2]
                PSR.set(ti)
                T = []
                S.cur = T
                for bi in range(2):
                    r0 = s - 1 + 128 * bi
                    lo, hi = max(r0, 0), min(r0 + 128, n_tok)

                    def pre(xb, r0=r0, lo=lo, hi=hi):
                        if lo > r0:
                            memset("dve", xb.ap[0:32, :], 0.0, [xb])
                        if hi < r0 + 128:
                            memset("dve", xb.ap[96:128, :], 0.0, [xb])
                        dma("sp", xb.ap[lo - r0:hi - r0, :], xmid_d[name][lo:hi, :], [], [xb], f"xs{xs_r.i}")
                    front(None, G2T, SH2T, w, dst_hT=h2T, col0=bi * 128, pre=pre)
                if s == 0:
                    memset("dve", h2T.ap[:, :, 0:1], 0.0, [h2T])
                if s + FT == n_tok:
                    memset("dve", h2T.ap[:, :, 255:256], 0.0, [h2T])
                clists = []
                for c in range(22):
                    S.cur = []
                    clists.append(S.cur)
                    pus = (PSR.next(), PSR.next())
                    for pu_, col in ((pus[0], c * 128), (pus[1], D_FF + c * 128)):
                        for k in range(8):
                            mm(pu_.ap[:, 0:256], WUP.ap[:, k, col:col + 128], h2T.ap[:, k, :], [WUP, h2T], [pu_], start=(k == 0), stop=(k == 7))
                    res = []
                    for fc, pu_, ra, rb in ((c, pus[0], cva, cvb), (22 + c, pus[1], cga, cgb)):
                        t1, t2 = ra.next(), rb.next()
                        act(t1.ap[:, 0:FT], pu_.ap[:, 1:1 + FT], AF.Identity, [pu_, cw, cb], [t1], scale=cw.ap[:, fc, 1:2], bias=cb.ap[:, fc:fc + 1])
                        stt("dve", t2.ap[:, 0:FT], pu_.ap[:, 0:FT], cw.ap[:, fc, 0:1], t1.ap[:, 0:FT], ALU.mult, ALU.add, [pu_, cw, t1], [t2])
                        stt("dve", t1.ap[:, 0:FT], pu_.ap[:, 2:2 + FT], cw.ap[:, fc, 2:3], t2.ap[:, 0:FT], ALU.mult, ALU.add, [pu_, cw, t2], [t1])
                        res.append(t1)
                    sg_ = csg.next()
                    act(sg_.ap[:, 0:FT], res[1].ap[:, 0:FT], AF.Silu, [res[1]], [sg_])
                    tt("pool", actT.ap[:, c, 0:FT], sg_.ap[:, 0:FT], res[0].ap[:, 0:FT], ALU.mult, [sg_, res[0]], [actT])
                S.cur = None
                T.extend(S.merge_threads(clists, ret=True))
                S.cur = T
                for sub, m in ((0, 128), (1, FT - 128)):
                    t0 = s + sub * 128
                    xr = xr_r.next()
                    dma("sp", xr.ap[0:m, :], xmid_d[name][t0:t0 + m, :], [], [xr], f"xr{xr_r.i}")
                    yt = yt_r.next()
                    for half in range(2):
                        py = PSR.next()
                        for c in range(22):
                            mm(py.ap[0:m, :], actT.ap[:, c, sub * 128:sub * 128 + m], WDN.ap[:, c, half * 512:(half + 1) * 512], [actT, WDN], [py],
                               start=(c == 0), stop=(c == 21))
                        hs = slice(half * 512, (half + 1) * 512)
                        tt("dve", yt.ap[0:m, hs], py.ap[0:m, :], G2B.ap[0:m, w, hs], ALU.mult, [py, G2B], [yt])
                        tt("pool", yt.ap[0:m, hs], yt.ap[0:m, hs], xr.ap[0:m, hs], ALU.add, [yt, xr], [yt])
                    dma("sp", x_dst[name][t0:t0 + m, :], yt.ap[0:m, :], [yt], [], f"st_xo{yt_r.i}")
                S.cur = None
                tiles.append(T)
            S.merge_threads(tiles)

        if STOP_AT <= 6:
            break
        if not last:
            ffn_pass(seqs["ctx"])
        if STOP_AT <= 7:
            break
        ffn_pass(seqs["lat"])
        if STOP_AT <= 8:
            break

    S.barrier()
    S.emit(st)
    st.close()
    return nc, S


def host_consts(SL):
    ident = np.eye(128, dtype=np.float32)
    j = np.arange(128)[:, None]
    i = np.arange(128)[None, :]
    le = (j <= i).astype(np.float32)
    ge = (j >= i).astype(np.float32)
    tri = np.stack([le, ge], axis=1) * np.float32(-1.0 / 16.0)
    mask = np.stack([le, ge], axis=1)
    rows_n = SL // GRID_W
    rows = np.repeat(np.arange(rows_n), GRID_W).astype(np.float32)
    cols = np.tile(np.arange(GRID_W), rows_n).astype(np.float32)
    nf = 16
    inv = (np.float32(10000.0) ** (-np.arange(nf, dtype=np.float32) / np.float32(nf))).astype(np.float32)
    ang = np.concatenate([rows[:, None] * inv, cols[:, None] * inv], axis=-1).astype(np.float32)
    cos, sin = np.cos(ang).astype(np.float32), np.sin(ang).astype(np.float32)
    cr, cc, sr, sc = cos[:, :16], cos[:, 16:], sin[:, :16], sin[:, 16:]
    rope = np.concatenate([cr, cr, cc, cc, -sr, -sc, sr, sc], axis=-1).astype(np.float32)
    wins = (2, 4, 8, 16)
    pinv = np.zeros((128, 2), np.float32)
    pedge = np.zeros((128, 2, 2, 8), np.float32)
    T = 1 << 20
    for c in range(2):
        for gi in range(2):
            wd = wins[2 * c + gi]
            sl = slice(gi * 64, (gi + 1) * 64)
            pinv[sl, c] = 1.0 / wd
            for t in range(8):
                cnt_first = min(t + wd // 2, T) - max(t - wd // 2, 0)
                tl = T - 8 + t
                cnt_last = min(tl + wd // 2, T) - max(tl - wd // 2, 0)
                pedge[sl, 0, c, t] = 1.0 / cnt_first
                pedge[sl, 1, c, t] = 1.0 / cnt_last
    return dict(ident=ident, tri=np.ascontiguousarray(tri), mask=np.ascontiguousarray(mask), rope=rope, pinv=pinv, pedge=pedge)


def host_weights(inp):
    f = lambda a: np.ascontiguousarray(np.asarray(a, dtype=np.float32))
    L = DEPTH
    out = {}
    out["w_ada"] = f(inp["w_ada"])
    out["b_adaT"] = f(np.asarray(inp["b_ada"]).reshape(L, 48, 128).transpose(0, 2, 1))
    out["b_adar"] = f(np.asarray(inp["b_ada"]).reshape(L, 1, 6 * D))
    out["n1T"] = f(np.asarray(inp["norm1_g"]).reshape(L, 8, 128).transpose(0, 2, 1))
    out["n2T"] = f(np.asarray(inp["norm2_g"]).reshape(L, 8, 128).transpose(0, 2, 1))
    out["w_in"] = f(inp["w_in"])
    wd = np.asarray(inp["gla_w_dec"])
    bd = np.asarray(inp["gla_b_dec"])
    wdec = np.zeros((L, 48, 2, 256), np.float32)
    bdec = np.zeros((L, 1, 2, 256), np.float32)
    for d in range(2):
        for h in range(4):
            wdec[:, 32 * d:32 * d + 16, d, 64 * h:64 * h + 48] = wd[:, d, :, 48 * h:48 * h + 48]
            bdec[:, 0, d, 64 * h:64 * h + 48] = bd[:, d, 48 * h:48 * h + 48]
    out["wdec"], out["bdec"] = wdec, bdec
    out["glag"] = f(np.broadcast_to(np.tile(np.asarray(inp["gla_norm_g"]), (1, 4))[:, None, :], (L, 128, 384)))
    out["qg"] = f(np.broadcast_to(np.tile(np.asarray(inp["q_norm_g"]), (1, 6))[:, None, :], (L, 128, 384)))
    out["kg"] = f(np.broadcast_to(np.tile(np.asarray(inp["k_norm_g"]), (1, 2))[:, None, :], (L, 128, 128)))
    out["sink"] = f(np.broadcast_to(np.asarray(inp["sink_logit"])[:, None, :], (L, 128, 6)))
    pwi = np.asarray(inp["pool_w"])
    pw = np.zeros((L, 128, 2, 128), np.float32)
    for c in range(2):
        for gi in range(2):
            pw[:, gi * 64:(gi + 1) * 64, c, gi * 64:(gi + 1) * 64] = pwi[:, 2 * c + gi]
    out["pw"] = pw
    out["pscT"] = f(np.asarray(inp["pool_scale"]).reshape(L, 2, 128).transpose(0, 2, 1))
    out["w_out"] = f(inp["w_out"])
    out["w_up"] = f(inp["w_up"])
    out["cw"] = f(np.asarray(inp["conv_w"]).reshape(L, 3, 44, 128).transpose(0, 3, 2, 1))
    out["cb"] = f(np.asarray(inp["conv_b"]).reshape(L, 44, 128).transpose(0, 2, 1))
    out["w_down"] = f(inp["w_down"])
    return out


_CACHE = {}


def kernel(**inputs):
    x = np.asarray(inputs["x"], dtype=np.float32)
    ctx = np.asarray(inputs["ctx"], dtype=np.float32)
    c = np.asarray(inputs["c"], dtype=np.float32)
    c_ctx = np.asarray(inputs["c_ctx"], dtype=np.float32)
    Bn, SL, _ = x.shape
    LC = ctx.shape[1]
    key = (SL, LC)
    if key not in _CACHE:
        _CACHE[key] = build(SL, LC)
    nc, _ = _CACHE[key]
    shared = dict(host_consts(SL))
    shared.update(host_weights(inputs))
    in_maps = []
    for b in range(Bn):
        m = dict(shared)
        m["x"] = np.ascontiguousarray(x[b])
        m["ctx"] = np.ascontiguousarray(ctx[b])
        cT = np.stack([c[b].reshape(8, 128).T, c_ctx.reshape(8, 128).T], axis=-1)
        m["cT"] = np.ascontiguousarray(cT.astype(np.float32))
        in_maps.append(m)
    res = run_bass_kernel_spmd(nc, in_maps, core_ids=list(range(Bn)))
    return np.stack([np.asarray(r["out"], dtype=np.float32) for r in res.results], axis=0)
```

```python
import numpy as np
from contextlib import ExitStack
import concourse.bass as bass
import concourse.mybir as mybir
from concourse.bass_utils import run_bass_kernel_spmd

F32 = mybir.dt.float32
BF16 = mybir.dt.bfloat16
AF = mybir.ActivationFunctionType
ALU = mybir.AluOpType
AX = mybir.AxisListType

D = 1024
DEPTH = 2
GRID_W = 64
NH_G, DK, DV = 4, 48, 96
D_FF = 2816
EPS = 1e-6
IN_W = 2080
C_Q, C_K, C_Z, C_PU, C_GV, C_GG, C_AQ, C_AK = 0, 256, 512, 576, 832, 1216, 1600, 1984
NW = 2240
FT = 254

import os
SAME_ENGINE_SYNC = True
RAW_ONLY = os.environ.get('MK_RAWONLY', '1') == '1'
import os
STOP_AT = float(os.environ.get('MK_STOP', '99'))


class Tok:
    __slots__ = ("name", "w", "r", "excl")

    def __init__(self, name):
        self.name = name
        self.w = []
        self.r = {}
        self.excl = False


class Op:
    __slots__ = ("eng", "fn", "deps", "odeps", "sig", "dma", "key", "ev", "tag")

    def __init__(self, eng, fn, dma, key):
        self.eng, self.fn, self.dma, self.key = eng, fn, dma, key
        self.deps = []
        self.odeps = []
        self.sig = False
        self.ev = None


class Sched:
    ENGS = ("pe", "act", "dve", "pool", "sp")

    def __init__(self, nc):
        self.nc = nc
        self.ops = []
        self.cur = None
        self.log = []
        self.ntok = 0

    def tok(self, name=None):
        self.ntok += 1
        return Tok(name or f"t{self.ntok}")

    def op(self, eng, fn, reads=(), writes=(), dma=False, key=None):
        o = Op(eng, fn, dma, key)
        if os.environ.get('MK_DEBUG'):
            import inspect
            o.tag = inspect.stack()[2].lineno
        deps = {}
        raw = set()
        for t in reads:
            for w in t.w:
                deps[id(w)] = w
                raw.add(id(w))
            if t.excl:
                for re_, rd in t.r.items():
                    if re_ != eng and not isinstance(rd, list):
                        deps[id(rd)] = rd
        for t in writes:
            samekey = dma and t.w and all(w.dma and w.key == key for w in t.w) and not t.r
            if not samekey:
                for w in t.w:
                    deps[id(w)] = w
                for rd in t.r.values():
                    if isinstance(rd, list):
                        for x in rd:
                            deps[id(x)] = x
                    else:
                        deps[id(rd)] = rd
        for d in deps.values():
            if (not d.dma) and (not dma) and d.eng == eng:
                if eng == "pe" or not SAME_ENGINE_SYNC or (RAW_ONLY and id(d) not in raw):
                    o.odeps.append(d)
                    continue
            o.deps.append(d)
            d.sig = True
        for t in reads:
            if dma:
                t.r.setdefault("dma", []).append(o)
            else:
                t.r[eng] = o
        for t in writes:
            samekey = dma and t.w and all(w.dma and w.key == key for w in t.w) and not t.r
            if samekey:
                t.w.append(o)
            else:
                t.w = [o]
                t.r = {}
        if dma:
            o.sig = True
        (self.cur if self.cur is not None else self.ops).append(o)
        return o

    def merge_threads(self, lists, ratio=1, ret=False):
        ratio = float(os.environ.get("MK_RATIO", "1"))
        delay = float(os.environ.get("MK_DELAY", "0"))
        unem = set()
        for L in lists:
            for o in L:
                unem.add(id(o))
        out = []
        k = len(lists)
        pos = [0] * k
        i = 0
        credit = 0.0
        while i < k:
            L = lists[i]
            if pos[i] >= len(L):
                i += 1
                continue
            o = L[pos[i]]
            pos[i] += 1
            out.append(o)
            unem.discard(id(o))
            j = i + 1
            if j < k and pos[i] >= delay * len(L):
                Lj = lists[j]
                credit += ratio
                while credit >= 1.0:
                    credit -= 1.0
                    if pos[j] < len(Lj):
                        o2 = Lj[pos[j]]
                        if all(id(d) not in unem for d in o2.deps) and all(id(d) not in unem for d in o2.odeps):
                            pos[j] += 1
                            out.append(o2)
                            unem.discard(id(o2))
        if ret:
            return out
        self.ops.extend(out)

    def barrier(self):
        last, dmas = {}, {}
        for o in self.ops:
            if o.fn is None:
                continue
            if o.dma:
                dmas[o.key] = o
            else:
                last[o.eng] = o
        for e in self.ENGS:
            b = Op(e, None, False, None)
            for d in list(last.values()) + list(dmas.values()):
                if (not d.dma) and d.eng == e:
                    continue
                b.deps.append(d)
                d.sig = True
            self.ops.append(b)

    def emit(self, stack):
        nc = self.nc
        cnt = {e: 0 for e in self.ENGS}
        keycnt = {}
        for o in self.ops:
            if o.fn is None:
                continue
            if o.dma:
                keycnt[o.key] = keycnt.get(o.key, 0) + 16
                o.ev = (("dma", o.key), keycnt[o.key])
            elif o.sig:
                cnt[o.eng] += 1
                o.ev = (("eng", o.eng), cnt[o.eng])
        sems = {}
        for e in self.ENGS:
            if cnt[e]:
                sems[("eng", e)] = stack.enter_context(nc.semaphore(f"c_{e}"))
        for k in keycnt:
            sems[("dma", k)] = stack.enter_context(nc.semaphore(f"d_{k}"))
        per = {e: [] for e in self.ENGS}
        for o in self.ops:
            per[o.eng].append(o)
        self.stats = {e: len(per[e]) for e in self.ENGS}
        self.stats["sems"] = len(sems)
        self.stats["maxcnt"] = dict(cnt)
        block = stack.enter_context(nc.Block())

        def run(engobj, lst):
            waited = {}
            for o in lst:
                need = {}
                for d in o.deps:
                    if d.ev is None:
                        continue
                    s, v = d.ev
                    if waited.get(s, 0) >= v:
                        continue
                    if need.get(s, 0) < v:
                        need[s] = v
                for s, v in need.items():
                    engobj.wait_ge(sems[s], v)
                    waited[s] = v
                    if os.environ.get('MK_DEBUG'):
                        self.log.append(f"{o.eng} WAIT {s} >= {v}")
                if os.environ.get('MK_DEBUG') and o.fn is not None:
                    self.log.append(f"{o.eng} OP line{getattr(o, 'tag', 0)} ev={o.ev}")
                if o.fn is None:
                    continue
                ins = o.fn(engobj)
                if o.ev is not None:
                    ins.then_inc(sems[o.ev[0]], 16 if o.dma else 1)

        @block.tensor
        def _(e):
            run(e, per["pe"])

        @block.scalar
        def _(e):
            run(e, per["act"])

        @block.vector
        def _(e):
            run(e, per["dve"])

        @block.gpsimd
        def _(e):
            run(e, per["pool"])

        @block.sync
        def _(e):
            run(e, per["sp"])


class Arena:
    def __init__(self, ap):
        self.ap = ap
        self.size = ap.shape[1]
        self.off = 0

    def reset(self, to=0):
        self.off = to

    def alloc(self, shape, dtype=F32):
        n = int(np.prod(shape))
        bpe = 4 if dtype == F32 else 2
        words = (n * bpe + 3) // 4
        words = (words + 7) // 8 * 8
        assert self.off + words <= self.size, f"arena overflow {self.off}+{words}>{self.size}"
        v = self.ap[:, self.off:self.off + words]
        self.off += words
        if dtype != F32:
            v = v.bitcast(dtype)
        v = v[:, 0:n]
        if len(shape) == 1:
            return v
        names = "abcd"[:len(shape)]
        pat = "p (" + " ".join(names) + ") -> p " + " ".join(names)
        return v.rearrange(pat, **{names[i]: shape[i] for i in range(len(shape))})


class B:
    __slots__ = ("ap", "t")

    def __init__(self, ap, t):
        self.ap, self.t = ap, t


class Ring:
    def __init__(self, bufs):
        self.bufs = bufs
        self.i = -1

    def next(self):
        self.i = (self.i + 1) % len(self.bufs)
        return self.bufs[self.i]


def build(SL, LC):
    nc = bass.Bass("TRN2", target_bir_lowering=False)

    def dten(name, shape, kind="ExternalInput"):
        return nc.dram_tensor(name, list(shape), F32, kind=kind).ap()

    x_d = dten("x", [SL, D]); ctx_d = dten("ctx", [LC, D]); cT_d = dten("cT", [128, 8, 2])
    ident_d = dten("ident", [128, 128]); tri_d = dten("tri", [128, 2, 128]); mask_d = dten("mask", [128, 2, 128])
    rope_d = dten("rope", [SL, 128]); pinv_d = dten("pinv", [128, 2]); pedge_d = dten("pedge", [128, 2, 2, 8])
    wada_d = dten("w_ada", [DEPTH, D, 6 * D]); badaT_d = dten("b_adaT", [DEPTH, 128, 48]); badar_d = dten("b_adar", [DEPTH, 1, 6 * D])
    n1T_d = dten("n1T", [DEPTH, 128, 8]); n2T_d = dten("n2T", [DEPTH, 128, 8])
    win_d = dten("w_in", [DEPTH, D, IN_W]); wdec_d = dten("wdec", [DEPTH, 48, 2, 256]); bdec_d = dten("bdec", [DEPTH, 1, 2, 256])
    glag_d = dten("glag", [DEPTH, 128, 384]); qg_d = dten("qg", [DEPTH, 128, 384]); kg_d = dten("kg", [DEPTH, 128, 128])
    sink_d = dten("sink", [DEPTH, 128, 6]); pw_d = dten("pw", [DEPTH, 128, 2, 128]); pscT_d = dten("pscT", [DEPTH, 128, 2])
    wout_d = dten("w_out", [DEPTH, D, D]); wup_d = dten("w_up", [DEPTH, D, 2 * D_FF]); cw_d = dten("cw", [DEPTH, 128, 44, 3])
    cb_d = dten("cb", [DEPTH, 128, 44]); wdn_d = dten("w_down", [DEPTH, D_FF, D])
    out_d = dten("out", [SL, D], kind="ExternalOutput")
    ob_d = {"lat": dten("ob_lat", [SL, 384], "Internal"), "ctx": dten("ob_ctx", [LC, 384], "Internal")}
    xmid_d = {"lat": dten("xmid_lat", [SL, D], "Internal"), "ctx": dten("xmid_ctx", [LC, D], "Internal")}
    x1_d = {"lat": dten("x1_lat", [SL, D], "Internal"), "ctx": dten("x1_ctx", [LC, D], "Internal")}
    pu_d = {"lat": dten("pu_lat", [128, 2, SL + 16], "Internal"), "ctx": dten("pu_ctx", [128, 2, LC + 16], "Internal")}

    st = ExitStack()
    S = Sched(nc)
    PERS_W, ARENA_W = 3904, 49280
    pers_t = st.enter_context(nc.sbuf_tensor("pers", [128, PERS_W], F32))
    arena_t = st.enter_context(nc.sbuf_tensor("arena", [128, ARENA_W], F32))
    PERS = Arena(pers_t[:, :]); AR = Arena(arena_t[:, :])
    psb = [st.enter_context(nc.psum_tensor(f"ps{i}", [128, 512], F32)) for i in range(8)]
    class PRing:
        def __init__(self, bufs):
            self.bufs = bufs
            self.par = 0
            self.i = [-1, -1]

        def set(self, par):
            self.par = par % 2

        def next(self):
            p = self.par
            self.i[p] = (self.i[p] + 1) % 4
            return self.bufs[4 * p + self.i[p]]

    PSR = PRing([B(psb[i][:, :], S.tok(f"ps{i}")) for i in range(8)])
    for b_ in PSR.bufs:
        b_.t.excl = True

    def newbuf(arena, shape, dtype=F32, name=None):
        return B(arena.alloc(shape, dtype), S.tok(name))

    def ring(arena, n, shape, dtype=F32, name="r"):
        return Ring([newbuf(arena, shape, dtype, f"{name}{i}") for i in range(n)])

    def toks(bs):
        return [b.t if isinstance(b, B) else b for b in bs]

    def mm(out, lhsT, rhs, R, W, start=True, stop=True):
        S.op("pe", lambda e: e.matmul(out, lhsT=lhsT, rhs=rhs, start=start, stop=stop), toks(R), toks(W))

    def tp(out, in_, R, W):
        idn = ident_bf if in_.dtype == BF16 else ident
        S.op("pe", lambda e: e.transpose(out, in_, idn.ap), toks(R) + [idn.t], toks(W))

    def bfv(pb):
        return pb.ap.bitcast(BF16)

    def act(out, in_, func, R, W, **kw):
        S.op("act", lambda e: e.activation(out=out, in_=in_, func=func, **kw), toks(R), toks(W))

    def tt(eng, out, in0, in1, op, R, W):
        S.op(eng, lambda e: e.tensor_tensor(out=out, in0=in0, in1=in1, op=op), toks(R), toks(W))

    def ts(eng, out, in0, s1, s2, op0, op1, R, W):
        if s2 is None:
            S.op(eng, lambda e: e.tensor_scalar(out=out, in0=in0, scalar1=s1, scalar2=None, op0=op0), toks(R), toks(W))
        else:
            S.op(eng, lambda e: e.tensor_scalar(out=out, in0=in0, scalar1=s1, scalar2=s2, op0=op0, op1=op1), toks(R), toks(W))

    def stt(eng, out, in0, scalar, in1, op0, op1, R, W):
        S.op(eng, lambda e: e.scalar_tensor_tensor(out=out, in0=in0, scalar=scalar, in1=in1, op0=op0, op1=op1), toks(R), toks(W))

    def cp(eng, out, in_, R, W):
        if eng == "act":
            act(out, in_, AF.Copy, R, W)
        else:
            S.op(eng, lambda e: e.tensor_copy(out=out, in_=in_), toks(R), toks(W))

    def memset(eng, ap, val, W):
        S.op(eng, lambda e: e.memset(ap, val), [], toks(W))

    def red(eng, out, in_, R, W):
        S.op(eng, lambda e: e.tensor_reduce(out=out, in_=in_, axis=AX.X, op=ALU.add), toks(R), toks(W))

    def recip(out, in_, R, W):
        S.op("dve", lambda e: e.reciprocal(out=out, in_=in_), toks(R), toks(W))

    def dma(eng, out, in_, R, W, key):
        S.op(eng, lambda e: e.dma_start(out=out, in_=in_), toks(R), toks(W), dma=True, key=key)

    def rstd_from_ss(ss, n, R_buf):
        act(ss, ss, AF.Ln, [R_buf], [R_buf], scale=1.0 / n, bias=EPS)
        act(ss, ss, AF.Exp, [R_buf], [R_buf], scale=-0.5)

    ident = newbuf(PERS, [128], F32, "ident"); ident.ap = ident.ap
    tri = newbuf(PERS, [2, 128]); maskb = newbuf(PERS, [2, 128]); pinv = newbuf(PERS, [2]); pedge = newbuf(PERS, [2, 2, 8])
    cact = newbuf(PERS, [8, 2]); onesr = newbuf(PERS, [128]); zeros = newbuf(PERS, [144])
    MODT = newbuf(PERS, [48, 2]); badaT = newbuf(PERS, [48]); n1T = newbuf(PERS, [8]); n2T = newbuf(PERS, [8])
    G1T = newbuf(PERS, [8, 2]); G2T = newbuf(PERS, [8, 2])
    glag = newbuf(PERS, [384]); qg = newbuf(PERS, [384])
    kg = newbuf(PERS, [128]); esink = newbuf(PERS, [6]); pscT = newbuf(PERS, [2]); cw = newbuf(PERS, [44, 3]); cb = newbuf(PERS, [44])
    pw = newbuf(PERS, [2, 128], BF16)
    Sst = {d: newbuf(PERS, [2, 96]) for d in "fb"}
    Sbf = {d: newbuf(PERS, [2, 96], BF16) for d in "fb"}
    small = Ring([newbuf(PERS, [16], F32, f"small{i}") for i in range(8)])

    dma("sp", ident.ap, ident_d, [], [ident], "c_ident")
    ident_bf = newbuf(PERS, [128], BF16, "ident_bf")
    cp("dve", ident_bf.ap, ident.ap, [ident], [ident_bf])
    dma("sp", tri.ap, tri_d, [], [tri], "c_tri")
    dma("sp", maskb.ap, mask_d, [], [maskb], "c_mask")
    dma("sp", pinv.ap, pinv_d, [], [pinv], "c_pinv")
    dma("sp", pedge.ap, pedge_d, [], [pedge], "c_pedge")
    dma("sp", cact.ap, cT_d, [], [cact], "c_cact")
    memset("dve", onesr.ap, 1.0, [onesr])
    memset("dve", zeros.ap, 0.0, [zeros])
    act(cact.ap, cact.ap, AF.Silu, [cact], [cact])
    for sname, n in (("lat", SL), ("ctx", LC)):
        dma("sp", pu_d[sname][:, :, 0:8], zeros.ap[:, 0:16].rearrange("p (a b) -> p a b", a=2), [zeros], [], "c_pz")
        dma("sp", pu_d[sname][:, :, 8 + n:16 + n], zeros.ap[:, 0:16].rearrange("p (a b) -> p a b", a=2), [zeros], [], "c_pz")

    seqs = {"lat": dict(name="lat", n=SL, nb=SL // 128, w=0, rope=True),
            "ctx": dict(name="ctx", n=LC, nb=LC // 128, w=1, rope=False)}

    for l in range(DEPTH):
        last = (l == DEPTH - 1)
        x_src = {"lat": x_d if l == 0 else x1_d["lat"], "ctx": ctx_d if l == 0 else x1_d["ctx"]}
        x_dst = {"lat": out_d if last else x1_d["lat"], "ctx": x1_d["ctx"]}
        S.barrier()
        AR.reset()
        G2B = newbuf(AR, [2, D], F32, "G2B")
        gbmark = AR.off
        G1B = newbuf(AR, [2, D], F32, "G1B")
        KT = {"lat": newbuf(AR, [SL], BF16, "KT"), "ctx": newbuf(AR, [LC], BF16, "KTc")}
        VA = {"lat": newbuf(AR, [SL // 128, 2, 65], BF16, "VA"), "ctx": newbuf(AR, [LC // 128, 2, 65], BF16, "VAc")}
        WIN = newbuf(AR, [8, NW], BF16, "WIN"); WOUT = newbuf(AR, [8, D], BF16, "WOUT")
        mark = AR.off
        for bufx, src in ((badaT, badaT_d[l]), (n1T, n1T_d[l]), (n2T, n2T_d[l]), (glag, glag_d[l]), (qg, qg_d[l]),
                          (kg, kg_d[l]), (esink, sink_d[l]), (pscT, pscT_d[l]), (cw, cw_d[l]), (cb, cb_d[l])):
            dma("sp", bufx.ap, src, [], [bufx], "c_small")
        dma("pool", pw.ap, pw_d[l], [], [pw], "c_pw")
        act(esink.ap, esink.ap, AF.Exp, [esink], [esink])
        if STOP_AT <= 0.1:
            break
        memset("pool", WIN.ap, 0.0, [WIN])
        wv = win_d[l].rearrange("(k p) n -> p k n", p=128)

        def stage(dst0, src0, n):
            dma("pool", WIN.ap[:, :, dst0:dst0 + n], wv[:, :, src0:src0 + n], [], [WIN], "w_in")
        for h in range(4):
            stage(C_Q + 64 * h, 48 * h, 48)
            stage(C_K + 64 * h, 192 + 48 * h, 48)
        stage(C_Z, 1152, 16); stage(C_Z + 32, 1168, 16)
        stage(C_PU, 1824, 256)
        stage(C_GV, 384, 384); stage(C_GG, 768, 384)
        for i, h in enumerate((0, 3, 1, 4, 2, 5)):
            stage(C_AQ + 64 * i, 1184 + 64 * h, 64)
        stage(C_AK, 1568, 256)
        dma("pool", WOUT.ap, wout_d[l].rearrange("(k p) n -> p k n", p=128), [], [WOUT], "w_out")
        if STOP_AT <= 0.2:
            break
        crep = newbuf(AR, [8, 2, 128], F32, "crep")
        brow = newbuf(AR, [6 * D], F32, "brow")
        dma("sp", brow.ap[0:1, :], badar_d[l], [], [brow], "c_brow")
        for k in range(8):
            for w in range(2):
                cp("dve", crep.ap[:, k, w, :], cact.ap[:, k, w:w + 1].to_broadcast([128, 128]), [cact], [crep])
        wst = ring(AR, 2, [8, 512], F32, "wst")
        for cbk in range(12):
            if cbk == 6 and os.environ.get('MK_NOBAR', '') == '':
                S.barrier()
            wb = wst.next()
            dma("sp", wb.ap, wada_d[l][:, cbk * 512:(cbk + 1) * 512].rearrange("(k p) n -> p k n", p=128), [], [wb], f"wst{wst.i}")
            which, half = cbk // 2, cbk % 2
            if os.environ.get('MK_SKIP', '') == 'C' and cbk >= 6:
                continue
            if os.environ.get('MK_SKIP', '') == 'D' and cbk < 6:
                continue
            if os.environ.get('MK_SKIP', '') == 'A' and which in (2, 5):
                continue
            if os.environ.get('MK_SKIP', '') == 'B' and which not in (2, 5):
                continue
            if which in (2, 5):
                GB = G1B if which == 2 else G2B
                for w in range(2):
                    pb = PSR.next()
                    for k in range(8):
                        mm(pb.ap, crep.ap[:, k, w, :], wb.ap[:, k, :], [crep, wb], [pb], start=(k == 0), stop=False)
                    mm(pb.ap, onesr.ap[0:1, :], brow.ap[0:1, cbk * 512:(cbk + 1) * 512], [onesr, brow], [pb], start=False, stop=True)
                    cp("act", GB.ap[:, w, half * 512:(half + 1) * 512], pb.ap, [pb], [GB])
            else:
                pb = PSR.next()
                for j in range(4):
                    for k in range(8):
                        mm(pb.ap[:, j * 2:j * 2 + 2], wb.ap[:, k, j * 128:(j + 1) * 128], cact.ap[:, k, :], [wb, cact], [pb],
                           start=(k == 0), stop=(k == 7))
                j0 = cbk * 4
                tt("dve", MODT.ap[:, j0:j0 + 4, :], pb.ap[:, 0:8].rearrange("p (j w) -> p j w", j=4),
                   badaT.ap[:, j0:j0 + 4, None].to_broadcast([128, 4, 2]), ALU.add, [pb, badaT], [MODT])
        ts("dve", G1T.ap, MODT.ap[:, 8:16, :], 1.0, None, ALU.add, None, [MODT], [G1T])
        tt("dve", G1T.ap, G1T.ap, n1T.ap[:, :, None].to_broadcast([128, 8, 2]), ALU.mult, [G1T, n1T], [G1T])
        ts("dve", G2T.ap, MODT.ap[:, 32:40, :], 1.0, None, ALU.add, None, [MODT], [G2T])
        tt("dve", G2T.ap, G2T.ap, n2T.ap[:, :, None].to_broadcast([128, 8, 2]), ALU.mult, [G2T, n2T], [G2T])
        SH1T = MODT.ap[:, 0:8, :]
        SH2T = MODT.ap[:, 24:32, :]

        S.barrier()
        AR.reset(mark)
        xs_r = ring(AR, 3, [D], F32, "xs"); xn = newbuf(AR, [D], BF16); junk = newbuf(AR, [D], BF16); hT_r = ring(AR, 2, [8, 128], BF16, "hT")
        PT = newbuf(AR, [5, 6, 128], BF16); rp_r = ring(AR, 2, [128], F32, "rp")
        qn = newbuf(AR, [384]); qsq = newbuf(AR, [384]); qr = newbuf(AR, [384]); qT = newbuf(AR, [3, 128], BF16); oatt = newbuf(AR, [384], BF16)
        kn = B(qn.ap[:, 0:128], qn.t); ksq = B(qsq.ap[:, 0:128], qsq.t); kr = B(qr.ap[:, 0:128], qr.t); ob_sq = qsq
        qrb = newbuf(AR, [384], BF16); krb = B(qrb.ap[:, 0:128], qrb.t)
        puT = newbuf(AR, [2, 128]); puw = newbuf(AR, [2, 144]); s2 = newbuf(AR, [2, 144]); s4 = newbuf(AR, [2, 144]); s8 = newbuf(AR, [2, 144])
        s16 = newbuf(AR, [2, 144]); tmp8 = newbuf(AR, [8])
        ytmp = newbuf(AR, [D]); xm_r = ring(AR, 2, [D], F32, "xm")
        qk_sb = newbuf(AR, [512], BF16); zpu_sb = newbuf(AR, [320])
        WSETS = []
        for _ws in range(2):
            w_ = {}
            w_["zT"] = newbuf(AR, [128]); w_["e1"] = newbuf(AR, [256]); w_["spb"] = newbuf(AR, [256]); w_["epos"] = newbuf(AR, [256]); w_["eneg"] = newbuf(AR, [256])
            w_["qdT"] = newbuf(AR, [256], BF16); w_["kiT"] = newbuf(AR, [256], BF16); w_["keT"] = newbuf(AR, [256], BF16); w_["ke"] = newbuf(AR, [256], BF16)
            w_["scT"] = newbuf(AR, [4, 128], BF16); w_["vbf"] = newbuf(AR, [384], BF16); w_["obl"] = newbuf(AR, [384]); w_["ob_o"] = newbuf(AR, [384])
            w_["sgb"] = newbuf(AR, [384]); w_["dT"] = newbuf(AR, [2, 128], BF16); w_["mixT"] = newbuf(AR, [8, 128], BF16)
            w_["obst"] = w_["ob_o"]
            w_["ob_bf"] = newbuf(AR, [384], BF16)
            WSETS.append(w_)
        zT = e1 = spb = epos = eneg = qdT = kiT = keT = ke = scT = vbf = obl = ob_o = sgb = dT = mixT = obst = ob_bf = None

        def select(slot):
            nonlocal zT, e1, spb, epos, eneg, qdT, kiT, keT, ke, scT, vbf, obl, ob_o, sgb, dT, mixT, obst, ob_bf
            w_ = WSETS[slot % 2]
            zT, e1, spb, epos, eneg = w_["zT"], w_["e1"], w_["spb"], w_["epos"], w_["eneg"]
            qdT, kiT, keT, ke, scT, vbf = w_["qdT"], w_["kiT"], w_["keT"], w_["ke"], w_["scT"], w_["vbf"]
            obl, ob_o, sgb, dT, mixT, obst = w_["obl"], w_["ob_o"], w_["sgb"], w_["dT"], w_["mixT"], w_["obst"]
            ob_bf = w_["ob_bf"]
        select(0)
        print("ARENA mixer end", AR.off, "of", AR.size)
        wdec = newbuf(AR, [2, 256]); bdec = newbuf(AR, [2, 256])
        dma("sp", wdec.ap[0:48, :, :], wdec_d[l], [], [wdec], "c_wdec")
        dma("sp", bdec.ap[0:1, :, :], bdec_d[l], [], [bdec], "c_wdec")
        for sname in ("lat", "ctx"):
            memset("pool", VA[sname].ap[:, :, :, 64:65], 1.0, [VA[sname]])

        def front(src_ap, GT, SHT, w, dst_hT=None, col0=0, pre=None, xb_pre=None):
            if xb_pre is not None:
                xb = xb_pre
            else:
                xb = xs_r.next()
                if pre is not None:
                    pre(xb)
                else:
                    dma("sp", xb.ap, src_ap, [], [xb], f"xs{xs_r.i}")
            sm = small.next()
            memset("dve", sm.ap[:, 0:1], 0.0, [sm])
            act(junk.ap, xb.ap, AF.Square, [xb, sm], [junk, sm], accum_out=sm.ap[:, 0:1])
            rstd_from_ss(sm.ap[:, 0:1], D, sm)
            ts("dve", xn.ap, xb.ap, sm.ap[:, 0:1], None, ALU.mult, None, [xb, sm], [xn])
            if dst_hT is None:
                hT = hT_r.next()
            else:
                hT = dst_hT
            for half in range(2):
                pb = PSR.next()
                pbv = bfv(pb)
                for j in range(4):
                    k = half * 4 + j
                    tp(pbv[:, j * 128:(j + 1) * 128], xn.ap[:, k * 128:(k + 1) * 128], [xn], [pb])
                for j in range(4):
                    k = half * 4 + j
                    act(hT.ap[:, k, col0:col0 + 128], pbv[:, j * 128:(j + 1) * 128], AF.Identity, [pb, GT, MODT], [hT],
                        scale=GT.ap[:, k, w:w + 1], bias=SHT[:, k, w:w + 1])
            return xb, hT

        def projF(hT, col0, M, out, pb):
            for k in range(8):
                mm(out, WIN.ap[:, k, col0:col0 + M], hT.ap[:, k, :], [WIN, hT], [pb], start=(k == 0), stop=(k == 7))

        def projT(hT, col0, N, out, pb):
            for k in range(8):
                mm(out, hT.ap[:, k, :], WIN.ap[:, k, col0:col0 + N], [WIN, hT], [pb], start=(k == 0), stop=(k == 7))

        def qk_proj(hT, do_out):
            pqk = PSR.next()
            if do_out:
                projT(hT, C_Q, 512, pqk.ap[:, 0:512], pqk)
                cp("act", qk_sb.ap, pqk.ap, [pqk], [qk_sb])
            else:
                projT(hT, C_K, 256, pqk.ap[:, 256:512], pqk)
                cp("act", qk_sb.ap[:, 256:512], pqk.ap[:, 256:512], [pqk], [qk_sb])
            pq = PSR.next()
            for c in range(4):
                if c < 2 and not do_out:
                    continue
                tp(bfv(pq)[:, c * 128:(c + 1) * 128], qk_sb.ap[:, c * 128:(c + 1) * 128], [qk_sb], [pq])
            return B(bfv(pq)[:, 0:512], pq.t)

        def rope(src, dst, tmp, H, rp, outb):
            tt("dve", tmp.ap.rearrange("p (h d) -> p h d", h=H), src.ap.rearrange("p (h d) -> p h d", h=H),
               rp.ap[:, None, 0:64].to_broadcast([128, H, 64]), ALU.mult, [src, rp], [tmp])
            s5 = src.ap.rearrange("p (h a f c) -> p h a f c", h=H, a=2, f=2)
            d5 = dst.ap.rearrange("p (h a f c) -> p h a f c", h=H, a=2, f=2)
            sneg = rp.ap[:, 64:96].rearrange("p (a c) -> p a c", a=2)[:, None, :, :].to_broadcast([128, H, 2, 16])
            spos = rp.ap[:, 96:128].rearrange("p (a c) -> p a c", a=2)[:, None, :, :].to_broadcast([128, H, 2, 16])
            tt("dve", d5[:, :, :, 0, :], s5[:, :, :, 1, :], sneg, ALU.mult, [src, rp], [dst])
            tt("dve", d5[:, :, :, 1, :], s5[:, :, :, 0, :], spos, ALU.mult, [src, rp], [dst])
            tt("dve", outb.ap, dst.ap, tmp.ap, ALU.add, [dst, tmp], [outb])

        def headnorm(buf, sq, H, dh, gbuf):
            sm = small.next()
            tt("dve", sq.ap, buf.ap, buf.ap, ALU.mult, [buf], [sq])
            red("dve", sm.ap[:, 0:H], sq.ap.rearrange("p (h d) -> p h d", h=H), [sq], [sm])
            rstd_from_ss(sm.ap[:, 0:H], dh, sm)
            tt("dve", buf.ap.rearrange("p (h d) -> p h d", h=H), buf.ap.rearrange("p (h d) -> p h d", h=H),
               sm.ap[:, 0:H, None].to_broadcast([128, H, dh]), ALU.mult, [buf, sm], [buf])
            tt("dve", buf.ap, buf.ap, gbuf.ap, ALU.mult, [buf, gbuf], [buf])

        def gla_block(d, pq, pz, do_out):
            di = 0 if d == "f" else 1
            zr = 0 if d == "f" else 32
            col = 127 if d == "f" else 0
            cp("act", zT.ap[0:48, :], pz.ap[0:48, 0:128], [pz], [zT])
            pl = PSR.next()
            mm(pl.ap[:, 0:256], zT.ap[0:48, :], wdec.ap[0:48, di, :], [zT, wdec], [pl], start=True, stop=False)
            mm(pl.ap[:, 0:256], onesr.ap[0:1, :], bdec.ap[0:1, di, :], [onesr, bdec], [pl], start=False, stop=True)
            act(e1.ap, pl.ap[:, 0:256], AF.Exp, [pl], [e1], scale=-1.0)
            ts("dve", e1.ap, e1.ap, 1.0, None, ALU.add, None, [e1], [e1])
            act(spb.ap, e1.ap, AF.Ln, [e1], [spb])
            if STOP_AT <= 1.41:
                return None
            pbT = PSR.next()
            for pr in range(2):
                mm(pbT.ap[:, pr * 128:(pr + 1) * 128], spb.ap[:, pr * 128:(pr + 1) * 128], tri.ap[:, di, :], [spb, tri], [pbT])
            act(epos.ap, pbT.ap[:, 0:256], AF.Exp, [pbT], [epos])
            act(eneg.ap, pbT.ap[:, 0:256], AF.Exp, [pbT], [eneg], scale=-1.0)
            if do_out:
                stt("dve", qdT.ap, pq.ap[:, 0:256], DK ** -0.5, epos.ap, ALU.mult, ALU.mult, [pq, epos], [qdT])
                tt("dve", kiT.ap, pq.ap[:, 256:512], eneg.ap, ALU.mult, [pq, eneg], [kiT])
            for pr in range(2):
                stt("dve", keT.ap[:, pr * 128:(pr + 1) * 128], eneg.ap[:, pr * 128:(pr + 1) * 128],
                    epos.ap[:, pr * 128 + col:pr * 128 + col + 1], pq.ap[:, 256 + pr * 128:384 + pr * 128], ALU.mult, ALU.mult,
                    [eneg, epos, pq], [keT])
            pke = PSR.next()
            for pr in range(2):
                tp(bfv(pke)[:, pr * 128:(pr + 1) * 128], keT.ap[:, pr * 128:(pr + 1) * 128], [keT], [pke])
            cp("act", ke.ap, bfv(pke)[:, 0:256], [pke], [ke])
            po = None
            if STOP_AT <= 1.42:
                return None
            if do_out:
                psc = [PSR.next(), PSR.next()]
                for h in range(4):
                    pr, par, base = h // 2, h % 2, 64 * (h % 2)
                    mm(psc[par].ap[:, pr * 128:(pr + 1) * 128], kiT.ap[base:base + 48, pr * 128:(pr + 1) * 128],
                       qdT.ap[base:base + 48, pr * 128:(pr + 1) * 128], [kiT, qdT], [psc[par]])
                for par in range(2):
                    tt("dve", scT.ap[:, 2 * par:2 * par + 2, :], psc[par].ap[:, 0:256].rearrange("p (h i) -> p h i", h=2),
                       maskb.ap[:, di:di + 1, :].to_broadcast([128, 2, 128]), ALU.mult, [psc[par], maskb], [scT])
                if STOP_AT <= 1.425:
                    return None
                po = PSR.next()
                po2 = [PSR.next(), PSR.next()]
                for h in range(4):
                    pr, par, base = h // 2, h % 2, 64 * (h % 2)
                    mm(po.ap[:, h * 96:(h + 1) * 96], scT.ap[:, 2 * par + pr, :], vbf.ap[:, h * 96:(h + 1) * 96], [scT, vbf], [po], start=True, stop=True)
                for h in range(4):
                    pr, par, base = h // 2, h % 2, 64 * (h % 2)
                    mm(po2[par].ap[:, h * 96:(h + 1) * 96], qdT.ap[base:base + 48, pr * 128:(pr + 1) * 128], Sbf[d].ap[base:base + 48, pr, :],
                       [qdT, Sbf[d]], [po2[par]], start=True, stop=True)
                po = (po, po2[0], po2[1])
            if STOP_AT <= 1.43:
                return None
            pup = PSR.next()
            for pr in range(2):
                mm(pup.ap[:, pr * 192:(pr + 1) * 192], ke.ap[:, pr * 128:(pr + 1) * 128], vbf.ap[:, pr * 192:(pr + 1) * 192], [ke, vbf], [pup])
            for h in range(4):
                pr, base = h // 2, 64 * (h % 2)
                stt("dve", Sst[d].ap[base:base + 48, pr, :], Sst[d].ap[base:base + 48, pr, :],
                    epos.ap[base:base + 48, pr * 128 + col:pr * 128 + col + 1],
                    pup.ap[base:base + 48, pr * 192 + (h % 2) * 96:pr * 192 + (h % 2) * 96 + 96], ALU.mult, ALU.add,
                    [Sst[d], epos, pup], [Sst[d]])
            cp("pool", Sbf[d].ap, Sst[d].ap, [Sst[d]], [Sbf[d]])
            return po

        XPRE = {}

        def prefetch_x(name, n):
            xb = xs_r.next()
            dma("sp", xb.ap, x_src[name][n * 128:(n + 1) * 128, :], [], [xb], f"xs{xs_r.i}")
            XPRE[n] = xb

        def pass_B(sq, do_out):
            name, nb, w = sq["name"], sq["nb"], sq["w"]
            lists = []
            order = list(reversed(range(nb)))
            XPRE.clear()
            prefetch_x(name, order[0])
            for i_, n in enumerate(order):
                select(n)
                PSR.set(n)
                S.cur = []
                lists.append(S.cur)
                if i_ + 1 < len(order):
                    prefetch_x(name, order[i_ + 1])
                pass_B_block(sq, do_out, n)
                S.cur = None
            S.merge_threads(lists)

        def pass_B_block(sq, do_out, n):
            name, nb, w = sq["name"], sq["nb"], sq["w"]
            if True:
                if STOP_AT <= 1.1:
                    return
                xb, hT = front(None, G1T, SH1T, w, xb_pre=XPRE[n])
                pk = PSR.next()
                projT(hT, C_AK, 256, pk.ap[:, 0:256], pk)
                cp("act", kn.ap, pk.ap[:, 0:128], [pk], [kn])
                cp("act", VA[name].ap[:, n, :, 0:64], pk.ap[:, 128:256].rearrange("p (g d) -> p g d", g=2), [pk], [VA[name]])
                pv = PSR.next()
                projT(hT, C_GV, 384, pv.ap[:, 0:384], pv)
                cp("act", vbf.ap, pv.ap[:, 0:384], [pv], [vbf])
                headnorm(kn, ksq, 2, 64, kg)
                if sq["rope"]:
                    rp = rp_r.next()
                    dma("sp", rp.ap, rope_d[n * 128:(n + 1) * 128, :], [], [rp], f"rp{rp_r.i}")
                    rope(kn, kr, ksq, 2, rp, krb)
                else:
                    cp("dve", krb.ap, kn.ap, [kn], [krb])
                pkt = PSR.next()
                tp(bfv(pkt)[:, 0:128], krb.ap, [krb], [pkt])
                cp("act", KT[name].ap[:, n * 128:(n + 1) * 128], bfv(pkt)[:, 0:128], [pkt], [KT[name]])
                nz = 320 if do_out else 64
                pzt = PSR.next()
                projT(hT, C_Z, nz, pzt.ap[:, 0:nz], pzt)
                cp("act", zpu_sb.ap[:, 0:nz], pzt.ap[:, 0:nz], [pzt], [zpu_sb])
                pz = PSR.next()
                tp(pz.ap[0:64, 0:128], zpu_sb.ap[:, 0:64], [zpu_sb], [pz])
                if do_out:
                    tp(pz.ap[:, 128:256], zpu_sb.ap[:, 64:192], [zpu_sb], [pz])
                    tp(pz.ap[:, 256:384], zpu_sb.ap[:, 192:320], [zpu_sb], [pz])
                    cp("act", puT.ap, pz.ap[:, 128:384].rearrange("p (c t) -> p c t", c=2), [pz], [puT])
                    dma("sp", pu_d[name][:, :, 8 + n * 128:8 + (n + 1) * 128], puT.ap, [puT], [], "st_pu")
                pq = qk_proj(hT, do_out)
                po = gla_block("b", pq, pz, do_out)
                if STOP_AT <= 1.5:
                    return
                if do_out:
                    cp("act", obst.ap, po[0].ap[:, 0:384], [po[0]], [obst])
                    for par in range(2):
                        ov = obst.ap.rearrange("p (pr x) -> p pr x", pr=2)[:, :, par * 96:(par + 1) * 96]
                        pv2 = po[1 + par].ap[:, 0:384].rearrange("p (pr x) -> p pr x", pr=2)[:, :, par * 96:(par + 1) * 96]
                        tt("dve", ov, ov, pv2, ALU.add, [obst, po[1 + par]], [obst])
                    dma("sp", ob_d[name][n * 128:(n + 1) * 128, :], obst.ap, [obst], [], "st_ob")

        def pass_F(sq, do_out):
            name, nb, w = sq["name"], sq["nb"], sq["w"]
            lists = []
            order = list(range(nb))
            XPRE.clear()
            prefetch_x(name, order[0])
            for i_, n in enumerate(order):
                select(n)
                PSR.set(n)
                S.cur = []
                lists.append(S.cur)
                if i_ + 1 < len(order):
                    prefetch_x(name, order[i_ + 1])
                pass_F_block(sq, do_out, n)
                S.cur = None
            S.merge_threads(lists)

        def pass_F_block(sq, do_out, n):
            name, nb, w = sq["name"], sq["nb"], sq["w"]
            if True:
                xb, hT = front(None, G1T, SH1T, w, xb_pre=XPRE[n])
                rp_f = None
                if do_out and sq["rope"]:
                    rp_f = rp_r.next()
                    dma("sp", rp_f.ap, rope_d[n * 128:(n + 1) * 128, :], [], [rp_f], f"rp{rp_r.i}")
                pv = PSR.next()
                projT(hT, C_GV, 384, pv.ap[:, 0:384], pv)
                cp("act", vbf.ap, pv.ap[:, 0:384], [pv], [vbf])
                if do_out:
                    pg = PSR.next()
                    projT(hT, C_GG, 384, pg.ap[:, 0:384], pg)
                    act(sgb.ap, pg.ap[:, 0:384], AF.Silu, [pg], [sgb])
                    pa = PSR.next()
                    projT(hT, C_AQ, 384, pa.ap[:, 0:384], pa)
                    cp("act", qn.ap, pa.ap[:, 0:384], [pa], [qn])
                pz = PSR.next()
                projF(hT, C_Z, 48, pz.ap[0:48, 0:128], pz)
                pq = qk_proj(hT, do_out)
                if not do_out:
                    gla_block("f", pq, pz, False)
                    return
                dma("sp", obl.ap, ob_d[name][n * 128:(n + 1) * 128, :], [], [obl], "ld_ob")
                po = gla_block("f", pq, pz, True)
                tt("dve", ob_o.ap, po[0].ap[:, 0:384], obl.ap, ALU.add, [po[0], obl], [ob_o])
                for par in range(2):
                    ov = ob_o.ap.rearrange("p (pr x) -> p pr x", pr=2)[:, :, par * 96:(par + 1) * 96]
                    pv2 = po[1 + par].ap[:, 0:384].rearrange("p (pr x) -> p pr x", pr=2)[:, :, par * 96:(par + 1) * 96]
                    tt("dve", ov, ov, pv2, ALU.add, [ob_o, po[1 + par]], [ob_o])
                headnorm(ob_o, ob_sq, 4, 96, glag)
                tt("dve", ob_bf.ap, ob_o.ap, sgb.ap, ALU.mult, [ob_o, sgb], [ob_bf])
                pmt = PSR.next()
                for c in range(3):
                    tp(bfv(pmt)[:, c * 128:(c + 1) * 128], ob_bf.ap[:, c * 128:(c + 1) * 128], [ob_bf], [pmt])
                cp("act", mixT.ap[:, 0:3, :], bfv(pmt)[:, 0:384].rearrange("p (c t) -> p c t", c=3), [pmt], [mixT])
                headnorm(qn, qsq, 6, 64, qg)
                if sq["rope"]:
                    rope(qn, qr, qsq, 6, rp_f, qrb)
                else:
                    cp("dve", qrb.ap, qn.ap, [qn], [qrb])
                pqt = PSR.next()
                for c in range(3):
                    tp(bfv(pqt)[:, c * 128:(c + 1) * 128], qrb.ap[:, c * 128:(c + 1) * 128], [qrb], [pqt])
                cp("act", qT.ap, bfv(pqt)[:, 0:384].rearrange("p (c t) -> p c t", c=3), [pqt], [qT])
                kbs = []
                if name == "lat":
                    if n > 0:
                        kbs.append(("lat", n - 1, 1))
                    kbs.append(("lat", n, None))
                    if n < nb - 1:
                        kbs.append(("lat", n + 1, 0))
                for cbk in range(LC // 128):
                    kbs.append(("ctx", cbk, None))
                for ki, (ks, kn_, mk) in enumerate(kbs):
                    for g in range(2):
                        ps_ = PSR.next()
                        mm(ps_.ap[:, 0:384].rearrange("p (h i) -> p h i", h=3), KT[ks].ap[g * 64:(g + 1) * 64, kn_ * 128:(kn_ + 1) * 128],
                           qT.ap[g * 64:(g + 1) * 64, :, :], [KT[ks], qT], [ps_])
                        act(PT.ap[:, ki, 3 * g:3 * g + 3, :], ps_.ap[:, 0:384].rearrange("p (h i) -> p h i", h=3), AF.Exp, [ps_], [PT], scale=0.125)
                        if mk is not None:
                            tt("dve", PT.ap[:, ki, 3 * g:3 * g + 3, :], PT.ap[:, ki, 3 * g:3 * g + 3, :],
                               maskb.ap[:, mk:mk + 1, :].to_broadcast([128, 3, 128]), ALU.mult, [PT, maskb], [PT])
                pov = PSR.next()
                for h in range(6):
                    g = h // 3
                    for ki, (ks, kn_, mk) in enumerate(kbs):
                        mm(pov.ap[:, h * 65:(h + 1) * 65], PT.ap[:, ki, h, :], VA[ks].ap[:, kn_, g, :], [PT, VA[ks]], [pov],
                           start=(ki == 0), stop=(ki == len(kbs) - 1))
                sm = small.next()
                pov3 = pov.ap[:, 0:390].rearrange("p (h e) -> p h e", e=65)
                tt("dve", sm.ap[:, 0:6], pov3[:, :, 64], esink.ap, ALU.add, [pov, esink], [sm])
                recip(sm.ap[:, 0:6], sm.ap[:, 0:6], [sm], [sm])
                tt("dve", oatt.ap.rearrange("p (h d) -> p h d", h=6), pov3[:, :, 0:64], sm.ap[:, 0:6, None].to_broadcast([128, 6, 64]),
                   ALU.mult, [pov, sm], [oatt])
                pat = PSR.next()
                for c in range(3):
                    tp(bfv(pat)[:, c * 128:(c + 1) * 128], oatt.ap[:, c * 128:(c + 1) * 128], [oatt], [pat])
                cp("act", mixT.ap[:, 3:6, :], bfv(pat)[:, 0:384].rearrange("p (c t) -> p c t", c=3), [pat], [mixT])
                dma("sp", puw.ap, pu_d[name][:, :, n * 128:n * 128 + 144], [], [puw], "ld_pu")
                tt("pool", s2.ap[:, :, 1:144], puw.ap[:, :, 0:143], puw.ap[:, :, 1:144], ALU.add, [puw], [s2])
                tt("pool", s4.ap[:, :, 2:143], s2.ap[:, :, 1:142], s2.ap[:, :, 3:144], ALU.add, [s2], [s4])
                tt("pool", s8.ap[:, 1, 4:141], s4.ap[:, 1, 2:139], s4.ap[:, 1, 6:143], ALU.add, [s4], [s8])
                tt("pool", s16.ap[:, 1, 8:136], s8.ap[:, 1, 4:132], s8.ap[:, 1, 12:140], ALU.add, [s8], [s16])
                combos = ((0, 64, 0, s2), (64, 128, 0, s4), (0, 64, 1, s8), (64, 128, 1, s16))
                for (r0, r1, c, sb_) in combos:
                    stt("dve", dT.ap[r0:r1, c, :], sb_.ap[r0:r1, c, 8:136], pinv.ap[r0:r1, c:c + 1], puw.ap[r0:r1, c, 8:136],
                        ALU.mult, ALU.subtract, [sb_, pinv, puw], [dT])
                for edge, cols, dcols in ((0, slice(8, 16), slice(0, 8)), (1, slice(128, 136), slice(120, 128))):
                    if (edge == 0 and n == 0) or (edge == 1 and n == nb - 1):
                        for (r0, r1, c, sb_) in combos:
                            tt("pool", tmp8.ap[r0:r1, :], sb_.ap[r0:r1, c, cols], pedge.ap[r0:r1, edge, c, :], ALU.mult, [sb_, pedge], [tmp8])
                            tt("pool", dT.ap[r0:r1, c, dcols], tmp8.ap[r0:r1, :], puw.ap[r0:r1, c, cols], ALU.subtract, [tmp8, puw], [dT])
                pp = PSR.next()
                for c in range(2):
                    mm(pp.ap[:, c * 128:(c + 1) * 128], pw.ap[:, c, :], dT.ap[:, c, :], [pw, dT], [pp])
                for c in range(2):
                    act(mixT.ap[:, 6 + c, :], pp.ap[:, c * 128:(c + 1) * 128], AF.Copy, [pp, pscT], [mixT], scale=pscT.ap[:, c:c + 1])
                xm = xm_r.next()
                for half in range(2):
                    py = PSR.next()
                    for kc in range(8):
                        mm(py.ap, mixT.ap[:, kc, :], WOUT.ap[:, kc, half * 512:(half + 1) * 512], [mixT, WOUT], [py], start=(kc == 0), stop=(kc == 7))
                    hs = slice(half * 512, (half + 1) * 512)
                    tt("dve", ytmp.ap[:, hs], py.ap, G1B.ap[:, w, hs], ALU.mult, [py, G1B], [ytmp])
                    tt("pool", xm.ap[:, hs], ytmp.ap[:, hs], xb.ap[:, hs], ALU.add, [ytmp, xb], [xm])
                dma("sp", xmid_d[name][n * 128:(n + 1) * 128, :], xm.ap, [xm], [], f"st_xm{xm_r.i}")

        for d in "fb":
            memset("dve", Sst[d].ap, 0.0, [Sst[d]])
            memset("pool", Sbf[d].ap, 0.0, [Sbf[d]])
        ctx_out = not last
        if STOP_AT <= 1:
            break
        pass_B(seqs["ctx"], ctx_out)
        if STOP_AT <= 2:
            break
        pass_F(seqs["ctx"], ctx_out)
        if STOP_AT <= 3:
            break
        pass_B(seqs["lat"], True)
        if STOP_AT <= 4:
            break
        pass_F(seqs["lat"], True)
        if STOP_AT <= 5:
            break

        S.barrier()
        AR.reset(gbmark)
        WUP = newbuf(AR, [8, 2 * D_FF], BF16, "WUP"); WDN = newbuf(AR, [22, D], BF16, "WDN")
        wuv = wup_d[l].rearrange("(k p) n -> p k n", p=128)
        wdv = wdn_d[l].rearrange("(k p) n -> p k n", p=128)
        for k in range(8):
            for cc in range(11):
                dma("pool", WUP.ap[:, k, cc * 512:(cc + 1) * 512], wuv[:, k, cc * 512:(cc + 1) * 512], [], [WUP], "w_up")
        for k in range(22):
            for cc in range(2):
                dma("pool", WDN.ap[:, k, cc * 512:(cc + 1) * 512], wdv[:, k, cc * 512:(cc + 1) * 512], [], [WDN], "w_dn")
        xs_r = ring(AR, 2, [D], F32, "xs"); xn = newbuf(AR, [D], BF16); junk = newbuf(AR, [D], BF16)
        h2T_r = [newbuf(AR, [8, 256], BF16, "h2Ta"), newbuf(AR, [8, 256], BF16, "h2Tb")]; actT = newbuf(AR, [22, 256], BF16, "actT")
        cva = ring(AR, 3, [256], F32, "cva"); cvb = ring(AR, 1, [256], F32, "cvb"); cga = ring(AR, 3, [256], F32, "cga")
        cgb = ring(AR, 1, [256], F32, "cgb"); csg = ring(AR, 2, [256], F32, "csg")
        xr_r = ring(AR, 1, [D], F32, "xr"); yt_r = ring(AR, 1, [D], F32, "yt")
        print("ARENA ffn end", AR.off, "of", AR.size)

        def ffn_pass(sq):
            name, n_tok, w = sq["name"], sq["n"], sq["w"]
            starts = sorted(set(min(FT * j, n_tok - FT) for j in range((n_tok + FT - 1) // FT)))
            tiles = []
            for ti, s in enumerate(starts):
                h2T = h2T_r[ti % 2]
                PSR.set(ti)
                T = []
                S.cur = T
                for bi in range(2):
                    r0 = s - 1 + 128 * bi
                    lo, hi = max(r0, 0), min(r0 + 128, n_tok)

                    def pre(xb, r0=r0, lo=lo, hi=hi):
                        if lo > r0:
                            memset("dve", xb.ap[0:32, :], 0.0, [xb])
                        if hi < r0 + 128:
                            memset("dve", xb.ap[96:128, :], 0.0, [xb])
                        dma("sp", xb.ap[lo - r0:hi - r0, :], xmid_d[name][lo:hi, :], [], [xb], f"xs{xs_r.i}")
                    front(None, G2T, SH2T, w, dst_hT=h2T, col0=bi * 128, pre=pre)
                if s == 0:
                    memset("dve", h2T.ap[:, :, 0:1], 0.0, [h2T])
                if s + FT == n_tok:
                    memset("dve", h2T.ap[:, :, 255:256], 0.0, [h2T])
                clists = []
                for c in range(22):
                    S.cur = []
                    clists.append(S.cur)
                    pus = (PSR.next(), PSR.next())
                    for pu_, col in ((pus[0], c * 128), (pus[1], D_FF + c * 128)):
                        for k in range(8):
                            mm(pu_.ap[:, 0:256], WUP.ap[:, k, col:col + 128], h2T.ap[:, k, :], [WUP, h2T], [pu_], start=(k == 0), stop=(k == 7))
                    res = []
                    for fc, pu_, ra, rb in ((c, pus[0], cva, cvb), (22 + c, pus[1], cga, cgb)):
                        t1, t2 = ra.next(), rb.next()
                        act(t1.ap[:, 0:FT], pu_.ap[:, 1:1 + FT], AF.Identity, [pu_, cw, cb], [t1], scale=cw.ap[:, fc, 1:2], bias=cb.ap[:, fc:fc + 1])
                        stt("dve", t2.ap[:, 0:FT], pu_.ap[:, 0:FT], cw.ap[:, fc, 0:1], t1.ap[:, 0:FT], ALU.mult, ALU.add, [pu_, cw, t1], [t2])
                        stt("dve", t1.ap[:, 0:FT], pu_.ap[:, 2:2 + FT], cw.ap[:, fc, 2:3], t2.ap[:, 0:FT], ALU.mult, ALU.add, [pu_, cw, t2], [t1])
                        res.append(t1)
                    sg_ = csg.next()
                    act(sg_.ap[:, 0:FT], res[1].ap[:, 0:FT], AF.Silu, [res[1]], [sg_])
                    tt("pool", actT.ap[:, c, 0:FT], sg_.ap[:, 0:FT], res[0].ap[:, 0:FT], ALU.mult, [sg_, res[0]], [actT])
                S.cur = None
                T.extend(S.merge_threads(clists, ret=True))
                S.cur = T
                for sub, m in ((0, 128), (1, FT - 128)):
                    t0 = s + sub * 128
                    xr = xr_r.next()
                    dma("sp", xr.ap[0:m, :], xmid_d[name][t0:t0 + m, :], [], [xr], f"xr{xr_r.i}")
                    yt = yt_r.next()
                    for half in range(2):
                        py = PSR.next()
                        for c in range(22):
                            mm(py.ap[0:m, :], actT.ap[:, c, sub * 128:sub * 128 + m], WDN.ap[:, c, half * 512:(half + 1) * 512], [actT, WDN], [py],
                               start=(c == 0), stop=(c == 21))
                        hs = slice(half * 512, (half + 1) * 512)
                        tt("dve", yt.ap[0:m, hs], py.ap[0:m, :], G2B.ap[0:m, w, hs], ALU.mult, [py, G2B], [yt])
                        tt("pool", yt.ap[0:m, hs], yt.ap[0:m, hs], xr.ap[0:m, hs], ALU.add, [yt, xr], [yt])
                    dma("sp", x_dst[name][t0:t0 + m, :], yt.ap[0:m, :], [yt], [], f"st_xo{yt_r.i}")
                S.cur = None
                tiles.append(T)
            S.merge_threads(tiles)

        if STOP_AT <= 6:
            break
        if not last:
            ffn_pass(seqs["ctx"])
        if STOP_AT <= 7:
            break
        ffn_pass(seqs["lat"])
        if STOP_AT <= 8:
            break

    S.barrier()
    S.emit(st)
    st.close()
    return nc, S


def host_consts(SL):
    ident = np.eye(128, dtype=np.float32)
    j = np.arange(128)[:, None]
    i = np.arange(128)[None, :]
    le = (j <= i).astype(np.float32)
    ge = (j >= i).astype(np.float32)
    tri = np.stack([le, ge], axis=1) * np.float32(-1.0 / 16.0)
    mask = np.stack([le, ge], axis=1)
    rows_n = SL // GRID_W
    rows = np.repeat(np.arange(rows_n), GRID_W).astype(np.float32)
    cols = np.tile(np.arange(GRID_W), rows_n).astype(np.float32)
    nf = 16
    inv = (np.float32(10000.0) ** (-np.arange(nf, dtype=np.float32) / np.float32(nf))).astype(np.float32)
    ang = np.concatenate([rows[:, None] * inv, cols[:, None] * inv], axis=-1).astype(np.float32)
    cos, sin = np.cos(ang).astype(np.float32), np.sin(ang).astype(np.float32)
    cr, cc, sr, sc = cos[:, :16], cos[:, 16:], sin[:, :16], sin[:, 16:]
    rope = np.concatenate([cr, cr, cc, cc, -sr, -sc, sr, sc], axis=-1).astype(np.float32)
    wins = (2, 4, 8, 16)
    pinv = np.zeros((128, 2), np.float32)
    pedge = np.zeros((128, 2, 2, 8), np.float32)
    T = 1 << 20
    for c in range(2):
        for gi in range(2):
            wd = wins[2 * c + gi]
            sl = slice(gi * 64, (gi + 1) * 64)
            pinv[sl, c] = 1.0 / wd
            for t in range(8):
                cnt_first = min(t + wd // 2, T) - max(t - wd // 2, 0)
                tl = T - 8 + t
                cnt_last = min(tl + wd // 2, T) - max(tl - wd // 2, 0)
                pedge[sl, 0, c, t] = 1.0 / cnt_first
                pedge[sl, 1, c, t] = 1.0 / cnt_last
    return dict(ident=ident, tri=np.ascontiguousarray(tri), mask=np.ascontiguousarray(mask), rope=rope, pinv=pinv, pedge=pedge)


def host_weights(inp):
    f = lambda a: np.ascontiguousarray(np.asarray(a, dtype=np.float32))
    L = DEPTH
    out = {}
    out["w_ada"] = f(inp["w_ada"])
    out["b_adaT"] = f(np.asarray(inp["b_ada"]).reshape(L, 48, 128).transpose(0, 2, 1))
    out["b_adar"] = f(np.asarray(inp["b_ada"]).reshape(L, 1, 6 * D))
    out["n1T"] = f(np.asarray(inp["norm1_g"]).reshape(L, 8, 128).transpose(0, 2, 1))
    out["n2T"] = f(np.asarray(inp["norm2_g"]).reshape(L, 8, 128).transpose(0, 2, 1))
    out["w_in"] = f(inp["w_in"])
    wd = np.asarray(inp["gla_w_dec"])
    bd = np.asarray(inp["gla_b_dec"])
    wdec = np.zeros((L, 48, 2, 256), np.float32)
    bdec = np.zeros((L, 1, 2, 256), np.float32)
    for d in range(2):
        for h in range(4):
            wdec[:, 32 * d:32 * d + 16, d, 64 * h:64 * h + 48] = wd[:, d, :, 48 * h:48 * h + 48]
            bdec[:, 0, d, 64 * h:64 * h + 48] = bd[:, d, 48 * h:48 * h + 48]
    out["wdec"], out["bdec"] = wdec, bdec
    out["glag"] = f(np.broadcast_to(np.tile(np.asarray(inp["gla_norm_g"]), (1, 4))[:, None, :], (L, 128, 384)))
    out["qg"] = f(np.broadcast_to(np.tile(np.asarray(inp["q_norm_g"]), (1, 6))[:, None, :], (L, 128, 384)))
    out["kg"] = f(np.broadcast_to(np.tile(np.asarray(inp["k_norm_g"]), (1, 2))[:, None, :], (L, 128, 128)))
    out["sink"] = f(np.broadcast_to(np.asarray(inp["sink_logit"])[:, None, :], (L, 128, 6)))
    pwi = np.asarray(inp["pool_w"])
    pw = np.zeros((L, 128, 2, 128), np.float32)
    for c in range(2):
        for gi in range(2):
            pw[:, gi * 64:(gi + 1) * 64, c, gi * 64:(gi + 1) * 64] = pwi[:, 2 * c + gi]
    out["pw"] = pw
    out["pscT"] = f(np.asarray(inp["pool_scale"]).reshape(L, 2, 128).transpose(0, 2, 1))
    out["w_out"] = f(inp["w_out"])
    out["w_up"] = f(inp["w_up"])
    out["cw"] = f(np.asarray(inp["conv_w"]).reshape(L, 3, 44, 128).transpose(0, 3, 2, 1))
    out["cb"] = f(np.asarray(inp["conv_b"]).reshape(L, 44, 128).transpose(0, 2, 1))
    out["w_down"] = f(inp["w_down"])
    return out


_CACHE = {}


def kernel(**inputs):
    x = np.asarray(inputs["x"], dtype=np.float32)
    ctx = np.asarray(inputs["ctx"], dtype=np.float32)
    c = np.asarray(inputs["c"], dtype=np.float32)
    c_ctx = np.asarray(inputs["c_ctx"], dtype=np.float32)
    Bn, SL, _ = x.shape
    LC = ctx.shape[1]
    key = (SL, LC)
    if key not in _CACHE:
        _CACHE[key] = build(SL, LC)
    nc, _ = _CACHE[key]
    shared = dict(host_consts(SL))
    shared.update(host_weights(inputs))
    in_maps = []
    for b in range(Bn):
        m = dict(shared)
        m["x"] = np.ascontiguousarray(x[b])
        m["ctx"] = np.ascontiguousarray(ctx[b])
        cT = np.stack([c[b].reshape(8, 128).T, c_ctx.reshape(8, 128).T], axis=-1)
        m["cT"] = np.ascontiguousarray(cT.astype(np.float32))
        in_maps.append(m)
    res = run_bass_kernel_spmd(nc, in_maps, core_ids=list(range(Bn)))
    return np.stack([np.asarray(r["out"], dtype=np.float32) for r in res.results], axis=0)
```
